# Optimizing a Trainium2 kernel written in Bass

```python
import jax, jax.numpy as jnp
from jax import lax
import numpy as np

D_MODEL = 2048
BATCH = 2
SEQ = 4096
DEPTH = 2

HEAD_DIM = 128
DILATED_PATTERNS = ((128, 1), (512, 4), (2048, 16))
N_GROUPS_A = len(DILATED_PATTERNS)
N_HEADS_A = 6
HPG_A = N_HEADS_A // N_GROUPS_A
N_HEADS_B = 6
N_HEADS_C = 4
N_HEADS_MIX = N_HEADS_A + N_HEADS_B + N_HEADS_C
MIX_WIDTH = N_HEADS_MIX * HEAD_DIM
OUT_A = HPG_A * HEAD_DIM
OUT_B = N_HEADS_B * HEAD_DIM
OUT_C = N_HEADS_C * HEAD_DIM
N_BRANCHES = 3
MOBA_BLOCK = 256
MOBA_TOPK = 3
MOBA_Q_CHUNK = 32
Q_BLOCK = 128
N_MEM = 256
N_HEADS_MEM = 4
MEM_WIDTH = N_HEADS_MEM * HEAD_DIM
ROPE_THETA = 500000.0
ROT_DIM = HEAD_DIM // 4
D_FF = ((8 * D_MODEL // 3 + 127) // 128) * 128
CONV_WIDTH = 3
EPS = 1e-6

kernel_name = "hybrid_gated_dilated_moba_stickbreak_block"


def rmsnorm(x, gain):
    xf = x.astype(jnp.float32)
    y = xf * lax.rsqrt(jnp.mean(xf * xf, axis=-1, keepdims=True) + EPS)
    return (y * gain.astype(jnp.float32)).astype(x.dtype)


def rope_tables(positions):
    inv_freq = ROPE_THETA ** (-jnp.arange(0, ROT_DIM, 2, dtype=jnp.float32) / ROT_DIM)
    ang = positions.astype(jnp.float32)[..., None] * inv_freq
    return jnp.cos(ang)[:, :, None, :], jnp.sin(ang)[:, :, None, :]


def partial_rope(x, cos, sin):
    half = ROT_DIM // 2
    xf = x.astype(jnp.float32)
    x1, x2, rest = xf[..., :half], xf[..., half:ROT_DIM], xf[..., ROT_DIM:]
    return jnp.concatenate([x1 * cos - x2 * sin, x2 * cos + x1 * sin, rest], axis=-1).astype(x.dtype)


def dilated_attention(q, k, v):
    b, s, _, d = q.shape
    nblk = s // Q_BLOCK
    scale = d ** -0.5
    qb = q.reshape(b, nblk, Q_BLOCK, N_HEADS_A, d).transpose(1, 0, 2, 3, 4)
    kgs = [k[:, :, g * HPG_A:(g + 1) * HPG_A] for g in range(N_GROUPS_A)]
    vgs = [v[:, :, g * HPG_A:(g + 1) * HPG_A] for g in range(N_GROUPS_A)]

    def one_block(args):
        qblk, bi = args
        t = bi * Q_BLOCK + jnp.arange(Q_BLOCK)
        outs, lses = [], []
        for g, (window, dilation) in enumerate(DILATED_PATTERNS):
            offs = jnp.arange(window // dilation + 1) * dilation
            idx = t[:, None] - offs[None, :]
            valid = idx >= 0
            idx = jnp.maximum(idx, 0)
            kg = jnp.take(kgs[g], idx, axis=1)
            vg = jnp.take(vgs[g], idx, axis=1)
            qg = qblk[:, :, g * HPG_A:(g + 1) * HPG_A]
            sc = jnp.einsum('bqgd,bqngd->bgqn', qg, kg, preferred_element_type=jnp.float32) * scale
            sc = jnp.where(valid[None, None], sc, -jnp.inf)
            lse = jax.nn.logsumexp(sc, axis=-1)
            p = jnp.exp(sc - lse[..., None])
            outs.append(jnp.einsum('bgqn,bqngd->bqgd', p.astype(vg.dtype), vg,
                                   preferred_element_type=jnp.float32))
            lses.append(lse)
        mix = jax.nn.softmax(jnp.stack(lses, 0), axis=0)
        o = jnp.einsum('nbgq,nbqgd->bqgd', mix, jnp.stack(outs, 0))
        return o.astype(q.dtype)

    out = lax.map(one_block, (qb, jnp.arange(nblk)))
    return out.transpose(1, 0, 2, 3, 4).reshape(b, s, OUT_A)


def moba_attention(q, k, v):
    b, s, h, d = q.shape
    scale = d ** -0.5
    nblk = -(-s // MOBA_BLOCK)
    pad = nblk * MOBA_BLOCK - s
    kp = jnp.pad(k, ((0, 0), (0, pad), (0, 0), (0, 0)))
    vp = jnp.pad(v, ((0, 0), (0, pad), (0, 0), (0, 0)))
    kb = kp.reshape(b, nblk, MOBA_BLOCK, h, d).transpose(0, 3, 1, 2, 4)
    vb = vp.reshape(b, nblk, MOBA_BLOCK, h, d).transpose(0, 3, 1, 2, 4)
    n_gate = max(nblk, MOBA_TOPK)
    kmean = jnp.mean(kb.astype(jnp.float32), axis=3)
    kmean = jnp.pad(kmean, ((0, 0), (0, 0), (0, n_gate - nblk), (0, 0)))
    nch = s // MOBA_Q_CHUNK
    qc = q.reshape(b, nch, MOBA_Q_CHUNK, h, d).transpose(1, 0, 3, 2, 4)
    bidx = jnp.arange(b)[:, None, None, None]
    hidx = jnp.arange(h)[None, :, None, None]
    n_sel = MOBA_TOPK * MOBA_BLOCK

    def one_chunk(args):
        qblk, ci = args
        t = ci * MOBA_Q_CHUNK + jnp.arange(MOBA_Q_CHUNK)
        own = (ci * MOBA_Q_CHUNK) // MOBA_BLOCK
        gate = jnp.einsum('bhqd,bhnd->bhqn', qblk.astype(jnp.float32), kmean)
        gate = jnp.where(jnp.arange(n_gate) < own, gate, -jnp.inf)
        _, sel = lax.top_k(gate, MOBA_TOPK)
        sel_valid = sel < own
        sel = jnp.minimum(sel, nblk - 1)
        ks = kb[bidx, hidx, sel]
        vs = vb[bidx, hidx, sel]
        s_sel = jnp.einsum('bhqd,bhqkjd->bhqkj', qblk, ks, preferred_element_type=jnp.float32) * scale
        s_sel = jnp.where(sel_valid[..., None], s_sel, -jnp.inf).reshape(b, h, MOBA_Q_CHUNK, n_sel)
        k_own = lax.dynamic_index_in_dim(kb, own, axis=2, keepdims=False)
        v_own = lax.dynamic_index_in_dim(vb, own, axis=2, keepdims=False)
        s_own = jnp.einsum('bhqd,bhjd->bhqj', qblk, k_own, preferred_element_type=jnp.float32) * scale
        key_pos = own * MOBA_BLOCK + jnp.arange(MOBA_BLOCK)
        s_own = jnp.where(key_pos[None, :] <= t[:, None], s_own, -jnp.inf)
        p = jax.nn.softmax(jnp.concatenate([s_sel, s_own], axis=-1), axis=-1)
        p_sel = p[..., :n_sel].reshape(b, h, MOBA_Q_CHUNK, MOBA_TOPK, MOBA_BLOCK).astype(v.dtype)
        p_own = p[..., n_sel:].astype(v.dtype)
        o = (jnp.einsum('bhqkj,bhqkjd->bhqd', p_sel, vs, preferred_element_type=jnp.float32)
             + jnp.einsum('bhqj,bhjd->bhqd', p_own, v_own, preferred_element_type=jnp.float32))
        return o.astype(q.dtype)

    out = lax.map(one_chunk, (qc, jnp.arange(nch)))
    return out.transpose(1, 0, 3, 2, 4).reshape(b, s, OUT_B)


def stick_breaking_attention(q, k, v):
    b, s, h, d = q.shape
    scale = d ** -0.5
    nblk = s // Q_BLOCK
    qb = q.reshape(b, nblk, Q_BLOCK, h, d).transpose(1, 0, 3, 2, 4)
    kt = k.transpose(0, 2, 1, 3)
    vt = v.transpose(0, 2, 1, 3)
    key_pos = jnp.arange(s)

    def one_block(args):
        qblk, bi = args
        t = bi * Q_BLOCK + jnp.arange(Q_BLOCK)
        z = jnp.einsum('bhqd,bhsd->bhqs', qblk, kt, preferred_element_type=jnp.float32) * scale
        causal = key_pos[None, :] < t[:, None]
        log_1m = jnp.where(causal, jax.nn.log_sigmoid(-z), 0.0)
        after = lax.cumsum(log_1m, axis=3, reverse=True) - log_1m
        a = jnp.where(causal, jnp.exp(jax.nn.log_sigmoid(z) + after), 0.0)
        o = jnp.einsum('bhqs,bhsd->bhqd', a.astype(v.dtype), vt, preferred_element_type=jnp.float32)
        return o.astype(q.dtype)

    out = lax.map(one_block, (qb, jnp.arange(nblk)))
    return out.transpose(1, 0, 3, 2, 4).reshape(b, s, OUT_C)


def hybrid_mixer(h, cos, sin, w_qkv, qk_gain, w_br_a, w_br_b, w_br_c, w_gate, b_gate, w_o):
    b, s, _ = h.shape
    qkv = h @ w_qkv
    q, k, v = jnp.split(qkv, 3, axis=-1)
    q = q.reshape(b, s, N_HEADS_MIX, HEAD_DIM)
    k = k.reshape(b, s, N_HEADS_MIX, HEAD_DIM)
    v = v.reshape(b, s, N_HEADS_MIX, HEAD_DIM)
    a0, a1, c0 = N_HEADS_A, N_HEADS_A + N_HEADS_B, N_HEADS_A + N_HEADS_B
    qa = partial_rope(rmsnorm(q[:, :, :a0], qk_gain[0]), cos, sin)
    ka = partial_rope(rmsnorm(k[:, :, :a0], qk_gain[1]), cos, sin)
    o_a = dilated_attention(qa, ka, v[:, :, :a0])
    qb = partial_rope(rmsnorm(q[:, :, a0:a1], qk_gain[2]), cos, sin)
    kb = partial_rope(rmsnorm(k[:, :, a0:a1], qk_gain[3]), cos, sin)
    o_b = moba_attention(qb, kb, v[:, :, a0:a1])
    o_c = stick_breaking_attention(q[:, :, c0:], k[:, :, c0:], v[:, :, c0:])
    gates = jax.nn.sigmoid((h @ w_gate + b_gate).astype(jnp.float32)).astype(h.dtype)
    gates = gates.reshape(b, s, N_BRANCHES, D_MODEL)
    merged = (gates[:, :, 0] * (o_a @ w_br_a) + gates[:, :, 1] * (o_b @ w_br_b)
              + gates[:, :, 2] * (o_c @ w_br_c))
    return merged @ w_o


def memory_cross_attention(h, m, wm_q, wm_kv, wm_o, gains):
    b, s, _ = h.shape
    n = m.shape[1]
    q = rmsnorm((h @ wm_q).reshape(b, s, N_HEADS_MEM, HEAD_DIM), gains[0])
    k, v = jnp.split(m @ wm_kv, 2, axis=-1)
    k = rmsnorm(k.reshape(b, n, N_HEADS_MEM, HEAD_DIM), gains[1])
    v = v.reshape(b, n, N_HEADS_MEM, HEAD_DIM)
    sc = jnp.einsum('bqhd,bnhd->bhqn', q, k, preferred_element_type=jnp.float32) * HEAD_DIM ** -0.5
    p = jax.nn.softmax(sc, axis=-1).astype(v.dtype)
    o = jnp.einsum('bhqn,bnhd->bqhd', p, v).reshape(b, s, MEM_WIDTH)
    return o @ wm_o


def conv_ffn(h, w_up, conv_w, conv_b, w_down):
    u = h @ w_up
    c = u.shape[-1]
    rhs = conv_w[:, None, :].astype(u.dtype)
    y = lax.conv_general_dilated(u, rhs, window_strides=(1,), padding=[(CONV_WIDTH - 1, 0)],
                                 dimension_numbers=('NWC', 'WIO', 'NWC'), feature_group_count=c)
    y = y + conv_b
    g, val = jnp.split(y, 2, axis=-1)
    return (jax.nn.silu(g) * val) @ w_down


def setup_inputs(seed: int = 0) -> dict:
    key = jax.random.key(seed)
    ks = jax.random.split(key, 24)
    L = DEPTH

    def nrm(k, shape, scale):
        return jax.random.normal(k, shape, jnp.float32) * scale

    start = jax.random.randint(ks[2], (BATCH, 1), 0, 1024, dtype=jnp.int32)
    return {
        "x": nrm(ks[0], (BATCH, SEQ, D_MODEL), 1.0),
        "mem": nrm(ks[1], (BATCH, N_MEM, D_MODEL), 1.0),
        "positions": start + jnp.arange(SEQ, dtype=jnp.int32)[None, :],
        "ln_mix": 1.0 + nrm(ks[3], (L, D_MODEL), 0.02),
        "w_qkv": nrm(ks[4], (L, D_MODEL, 3 * MIX_WIDTH), D_MODEL ** -0.5),
        "qk_gain": 1.0 + nrm(ks[5], (L, 4, HEAD_DIM), 0.02),
        "w_br_a": nrm(ks[6], (L, OUT_A, D_MODEL), OUT_A ** -0.5),
        "w_br_b": nrm(ks[7], (L, OUT_B, D_MODEL), OUT_B ** -0.5),
        "w_br_c": nrm(ks[8], (L, OUT_C, D_MODEL), OUT_C ** -0.5),
        "w_gate": nrm(ks[9], (L, D_MODEL, N_BRANCHES * D_MODEL), D_MODEL ** -0.5),
        "b_gate": nrm(ks[10], (L, N_BRANCHES * D_MODEL), 0.01),
        "w_o": nrm(ks[11], (L, D_MODEL, D_MODEL), D_MODEL ** -0.5),
        "ln_mem_q": 1.0 + nrm(ks[12], (L, D_MODEL), 0.02),
        "ln_mem_kv": 1.0 + nrm(ks[13], (L, D_MODEL), 0.02),
        "wm_q": nrm(ks[14], (L, D_MODEL, MEM_WIDTH), D_MODEL ** -0.5),
        "wm_kv": nrm(ks[15], (L, D_MODEL, 2 * MEM_WIDTH), D_MODEL ** -0.5),
        "wm_o": nrm(ks[16], (L, MEM_WIDTH, D_MODEL), MEM_WIDTH ** -0.5),
        "mem_qk_gain": 1.0 + nrm(ks[17], (L, 2, HEAD_DIM), 0.02),
        "ln_ffn": 1.0 + nrm(ks[18], (L, D_MODEL), 0.02),
        "w_up": nrm(ks[19], (L, D_MODEL, 2 * D_FF), D_MODEL ** -0.5),
        "conv_w": nrm(ks[20], (L, CONV_WIDTH, 2 * D_FF), CONV_WIDTH ** -0.5),
        "conv_b": nrm(ks[21], (L, 2 * D_FF), 0.01),
        "w_down": nrm(ks[22], (L, D_FF, D_MODEL), D_FF ** -0.5),
    }


def reference(x, mem, positions, ln_mix, w_qkv, qk_gain, w_br_a, w_br_b, w_br_c, w_gate, b_gate, w_o,
              ln_mem_q, ln_mem_kv, wm_q, wm_kv, wm_o, mem_qk_gain, ln_ffn, w_up, conv_w, conv_b, w_down):
    cos, sin = rope_tables(positions)
    for l in range(DEPTH):
        x = x + hybrid_mixer(rmsnorm(x, ln_mix[l]), cos, sin, w_qkv[l], qk_gain[l], w_br_a[l], w_br_b[l],
                             w_br_c[l], w_gate[l], b_gate[l], w_o[l])
        x = x + memory_cross_attention(rmsnorm(x, ln_mem_q[l]), rmsnorm(mem, ln_mem_kv[l]), wm_q[l], wm_kv[l],
                                       wm_o[l], mem_qk_gain[l])
        x = x + conv_ffn(rmsnorm(x, ln_ffn[l]), w_up[l], conv_w[l], conv_b[l], w_down[l])
    return x
```

```python
import contextlib
import numpy as np
import ml_dtypes
import concourse.bass as bass
import concourse.mybir as mybir
from concourse.bass_utils import run_bass_kernel_spmd

F32 = mybir.dt.float32
BF16 = mybir.dt.bfloat16
I32 = mybir.dt.int32
AF = mybir.ActivationFunctionType
ALU = mybir.AluOpType
AX = mybir.AxisListType
NPBF = ml_dtypes.bfloat16

D = 2048
NCORE = 8
TOK = 1024
NB = 8
EPS = 1e-6
DFF = 5504
SCALE = 128 ** -0.5
PI = 3.14159265358979
NEG = -30000.0
WCAP = 2048


class Prog:
    CE = ('scalar', 'tensor', 'vector', 'gpsimd', 'sync')

    def __init__(self, nc):
        self.nc = nc
        self.q = {e: [] for e in self.CE}
        self.lastw = {}
        self.rd = {}
        self.dcnt = {}
        self.known = {e: {} for e in self.CE}
        self.pending = {e: [] for e in self.CE}
        self.emitted = {e: 0 for e in self.CE}
        self.cum = {e: [] for e in self.CE}
        self.esem = None
        self.dsem = {}

    def barrier(self):
        evs = []
        for e in self.CE:
            if self.q[e] and e != 'sync':
                evs.append((('e', e, len(self.q[e]) - 1), True))
        for k, n in self.dcnt.items():
            evs.append((('d', k, n), True))
        for e in self.CE:
            self.pending[e] = [ev for ev in evs if not (ev[0][0] == 'e' and ev[0][1] == e)]

    def op(self, eng, fn, r=(), w=(), dma=None, extra=()):
        q = self.q[eng]
        idx = len(q)
        cand = list(extra) + self.pending[eng]
        self.pending[eng] = []
        for t in r:
            ev = self.lastw.get(t)
            if ev is not None:
                cand.append((ev, True))
        for t in w:
            ev = self.lastw.get(t)
            if ev is not None:
                cand.append((ev, False))
            for ev in self.rd.get(t, ()):
                cand.append((ev, False))
        best = {}
        for ev, raw in cand:
            if ev[0] == 'e':
                if ev[1] == eng and (eng in ('tensor', 'sync') or not raw):
                    continue
                k = ('e', ev[1])
                val = ev[2]
            else:
                k = ('d', ev[1])
                val = self.dcnt[ev[1]]
            if val > best.get(k, -1):
                best[k] = val
        fw = []
        for k, val in best.items():
            if self.known[eng].get(k, -1) >= val:
                continue
            self.known[eng][k] = val
            fw.append((k[0], k[1], val))
        rec = dict(fn=fn, waits=fw, inc=False, dma=dma)
        q.append(rec)
        if dma is not None:
            n = self.dcnt.get(dma, 0) + 1
            self.dcnt[dma] = n
            me = ('d', dma, n)
        else:
            me = ('e', eng, idx)
        for n_, ev in enumerate(fw):
            if ev[0] == 'e':
                pe, pidx = ev[1], ev[2]
                if pidx < self.emitted[pe]:
                    while not self.q[pe][pidx]['inc']:
                        pidx += 1
                    fw[n_] = ('e', pe, pidx)
                else:
                    self.q[pe][pidx]['inc'] = True
        for t in w:
            self.lastw[t] = me
            self.rd[t] = []
        for t in r:
            self.rd.setdefault(t, []).append(me)
        return me

    def emit(self, st=None):
        nc = self.nc
        if self.esem is None:
            self.esem = {e: nc.alloc_semaphore('s_' + e) for e in self.CE}
        for k in self.dcnt:
            if k not in self.dsem:
                self.dsem[k] = nc.alloc_semaphore('d_' + k)
        esem, dsem, cum = self.esem, self.dsem, self.cum
        for e in self.CE:
            q = self.q[e]
            if len(q) > self.emitted[e]:
                q[-1]['inc'] = True
            c = cum[e][-1] if cum[e] else 0
            for rec in q[len(cum[e]):]:
                if rec['inc']:
                    c += 1
                cum[e].append(c)
        with nc.Block() as block:
            def run(e):
                start = self.emitted[e]

                def f(eng):
                    for rec in self.q[e][start:]:
                        rec['wv'] = []
                        for ev in rec['waits']:
                            if ev[0] == 'e':
                                eng.wait_ge(esem[ev[1]], cum[ev[1]][ev[2]])
                                rec['wv'].append((('e', ev[1]), cum[ev[1]][ev[2]]))
                            else:
                                eng.wait_ge(dsem[ev[1]], 16 * ev[2])
                                rec['wv'].append((('d', ev[1]), ev[2]))
                        ins = rec['fn'](eng)
                        if rec['dma'] is not None:
                            ins.then_inc(dsem[rec['dma']], 16)
                        elif rec['inc']:
                            ins.then_inc(esem[e], 1)
                return f

            block.sync(run('sync'))
            block.scalar(run('scalar'))
            block.tensor(run('tensor'))
            block.vector(run('vector'))
            block.gpsimd(run('gpsimd'))
        for e in self.CE:
            self.emitted[e] = len(self.q[e])


class Ctx:
    def __init__(self):
        self.nc = bass.Bass("TRN2", target_bir_lowering=False)
        self.P = Prog(self.nc)
        self.st = contextlib.ExitStack()
        self.rot = {}
        self.outs = []
        self.dq = 0
        self.pst = None
        self.phase_id = 0
        self.dcache = {}

    def phase_begin(self):
        self.P.barrier()
        self.pst = contextlib.ExitStack()
        self.phase_id += 1

    def phase_end(self):
        self.P.emit()
        self.pst.close()
        self.pst = None

    def dram_in(self, name, shape, dt):
        if name not in self.dcache:
            self.dcache[name] = self.nc.dram_tensor(name, list(shape), dt, kind="ExternalInput").ap()
        return self.dcache[name]

    def dram_tmp(self, name, shape, dt):
        if name not in self.dcache:
            self.dcache[name] = self.nc.dram_tensor(name, list(shape), dt).ap()
        return self.dcache[name]

    def dram_out(self, name, shape, dt):
        return self.nc.dram_tensor(name, list(shape), dt, kind="ExternalOutput").ap()

    def sb(self, name, shape, dt):
        st = self.st if self.pst is None else self.pst
        return st.enter_context(self.nc.sbuf_tensor(f"{name}_p{self.phase_id}", list(shape), dt))

    def ps(self, name, shape, dt=F32):
        st = self.st if self.pst is None else self.pst
        return st.enter_context(self.nc.psum_tensor(f"{name}_p{self.phase_id}", list(shape), dt))

    def nxt(self, name, n):
        i = self.rot.get(name, 0)
        self.rot[name] = i + 1
        return i % n

    def dma(self, out, in_, r, w, key, eng=None):
        if eng is None:
            eng = ('sync', 'gpsimd')[self.dq % 2] if False else 'sync'
        def fn(e):
            try:
                return e.dma_start(out=out, in_=in_)
            except Exception:
                print("DMA FAILED", out, in_)
                raise
        return self.P.op(eng, fn, r, w, dma=key)

    def act(self, out, in_, func, r, w, **kw):
        return self.P.op('scalar', lambda e: e.activation(out=out, in_=in_, func=func, **kw), r, w)

    def mm(self, out, lhsT, rhs, start, stop, r, w):
        return self.P.op('tensor', lambda e: e.matmul(out, lhsT=lhsT, rhs=rhs, start=start, stop=stop), r, w)

    def tr(self, out, in_, ident, r, w):
        return self.P.op('tensor', lambda e: e.transpose(out, in_, ident), r, w)

    def tt(self, out, in0, in1, op, r, w, eng='vector'):
        return self.P.op(eng, lambda e: e.tensor_tensor(out=out, in0=in0, in1=in1, op=op), r, w)

    def ts(self, out, in0, s1, s2, op0, op1, r, w, eng='vector'):
        if op1 is None:
            return self.P.op(eng, lambda e: e.tensor_scalar(out=out, in0=in0, scalar1=s1, scalar2=None, op0=op0), r, w)
        return self.P.op(eng, lambda e: e.tensor_scalar(out=out, in0=in0, scalar1=s1, scalar2=s2, op0=op0, op1=op1), r, w)

    def stt(self, out, in0, scalar, in1, op0, op1, r, w):
        return self.P.op('vector', lambda e: e.scalar_tensor_tensor(out=out, in0=in0, scalar=scalar, in1=in1,
                                                                    op0=op0, op1=op1), r, w)

    def cp(self, out, in_, r, w, eng='vector'):
        if eng == 'scalar':
            return self.P.op(eng, lambda e: e.copy(out=out, in_=in_), r, w)
        return self.P.op(eng, lambda e: e.tensor_copy(out=out, in_=in_), r, w)

    def recip(self, out, in_, r, w):
        return self.P.op('vector', lambda e: e.reciprocal(out=out, in_=in_), r, w)

    def memset(self, ap, val, w, eng='gpsimd'):
        return self.P.op(eng, lambda e: e.memset(ap, val), (), w)

    def finish(self):
        self.P.op('sync', lambda e: e.nop(), r=tuple(self.outs), w=())
        self.P.emit(self.st)
        self.st.close()
        return self.nc


def load_consts(C, need_ident=True):
    nc = C.nc
    ident_d = C.dram_in("ident", [128, 128], BF16)
    C.ident = C.sb("ident_sb", [128, 128], BF16)
    C.dma(C.ident[:], ident_d, (), ('ident',), 'c0')


def load_x(C, name="x_in"):
    xd = C.dram_in(name, [TOK, D], F32)
    C.X = C.sb("X", [128, NB, D], F32)
    for m in range(NB):
        C.dma(C.X[:, m, :], xd[m * 128:(m + 1) * 128, :], (), (('X', m),), 'xl')


def store_x(C, name="x_out"):
    xo = C.dram_out(name, [TOK, D], F32)
    for m in range(NB):
        C.dma(xo[m * 128:(m + 1) * 128, :], C.X[:, m, :], (('X', m),), (('xo', m),), 'xs')
        C.outs.append(('xo', m))


def alloc_norm(C, nhb=2):
    C.nhb = nhb
    C.hT = C.sb("hT", [128, 16, TOK], BF16)
    C.hb = [C.sb(f"hb{i}", [128, D], BF16) for i in range(nhb)]
    C.nst = [C.sb(f"nst{i}", [128, 4], F32) for i in range(2)]
    C.tp = [C.ps(f"tp{i}", [128, 4, 128], BF16) for i in range(2)]


def rstd_from_ss(C, st, n, inv_n, tok):
    C.ts(st[:, 0:n], st[:, 0:n], inv_n, EPS, ALU.mult, ALU.add, (tok,), (tok,))
    C.act(st[:, 0:n], st[:, 0:n], AF.Sqrt, (tok,), (tok,))
    C.recip(st[:, 0:n], st[:, 0:n], (tok,), (tok,))


def make_hT(C, gain_tile, gain_tok, src=None, nblk=NB, dst=None, dst_tok='hT'):
    dst = C.hT if dst is None else dst
    for m in range(nblk):
        xin, xtok = (C.X[:, m, :], ('X', m)) if src is None else src(m)
        i = C.nxt('hb', C.nhb)
        st = C.nst[i]
        C.act(C.hb[i][:], xin, AF.Square, (xtok,), (('hb', i), ('nst', i)), accum_out=st[:, 0:1])
        rstd_from_ss(C, st, 1, 1.0 / D, ('nst', i))
        C.stt(C.hb[i][:], xin, st[:, 0:1], gain_tile, ALU.mult, ALU.mult, (xtok, ('nst', i), gain_tok), (('hb', i),))
        for g in range(4):
            j = C.nxt('tp', 2)
            for a in range(4):
                kc = g * 4 + a
                C.tr(C.tp[j][:, a, :], C.hb[i][:, kc * 128:(kc + 1) * 128], C.ident[:], (('hb', i), 'ident'),
                     (('tp', j),))
            eng = 'scalar' if g % 2 == 0 else 'vector'
            C.cp(dst[:, g * 4:(g + 1) * 4, m * 128:(m + 1) * 128], C.tp[j][:], (('tp', j),), ((dst_tok, m),), eng=eng)


def alloc_wstream(C, nst=3, nbf=6):
    C.nwst, C.nwbf = nst, nbf
    C.wst = [C.sb(f"wst{i}", [128, WCAP], F32) for i in range(nst)]
    C.wbf = [C.sb(f"wbf{i}", [128, WCAP], BF16) for i in range(nbf)]


def load_w(C, views):
    s = C.nxt('wst', C.nwst)
    b = C.nxt('wbf', C.nwbf)
    off = 0
    outs = []
    for v in views:
        k, n = v.shape[1], v.shape[2]
        sz = k * n
        dst = C.wst[s][:, off:off + sz].rearrange("p (k n) -> p k n", n=n)
        C.dma(dst, v, (), (('wst', s),), f'ws{s}')
        outs.append(C.wbf[b][:, off:off + sz].rearrange("p (k n) -> p k n", n=n))
        off += sz
    assert off <= WCAP
    h = (off // 2 + 127) // 128 * 128
    C.cp(C.wbf[b][:, 0:h], C.wst[s][:, 0:h], (('wst', s),), (('wbf', b, 0),), eng='gpsimd')
    C.cp(C.wbf[b][:, h:off], C.wst[s][:, h:off], (('wst', s),), (('wbf', b, 1),), eng='vector')
    return outs, (('wbf', b, 0), ('wbf', b, 1))


def rope_tables(C):
    pos_d = C.dram_in("pos", [128, NB], I32)
    invf_d = C.dram_in("invf", [128, 16], F32)
    posi = C.sb("posi", [128, NB], I32)
    posf = C.sb("posf", [128, NB], F32)
    invf = C.sb("invf_sb", [128, 16], F32)
    AC = C.sb("AC", [128, 2, NB * 16], F32)
    KF = C.sb("KF", [128, 2, NB * 16], F32)
    KI = C.sb("KI", [128, 2, NB * 16], I32)
    MK = C.sb("MK", [128, 2, NB * 16], F32)
    C.dma(posi[:], pos_d, (), ('posi',), 'c0')
    C.dma(invf[:], invf_d, (), ('invf',), 'c0')
    C.cp(posf[:], posi[:], ('posi',), ('posf',))
    for m in range(NB):
        C.ts(AC[:, 0, m * 16:(m + 1) * 16], invf[:], posf[:, m:m + 1], None, ALU.mult, None, ('posf', 'invf'), ('AC',))
    C.ts(AC[:, 1, :], AC[:, 0, :], PI / 2, None, ALU.add, None, ('AC',), ('AC',))
    C.ts(KF[:], AC[:], 1.0 / (2 * PI), None, ALU.mult, None, ('AC',), ('KF',))
    C.cp(KI[:], KF[:], ('KF',), ('KI',))
    C.cp(KF[:], KI[:], ('KI',), ('KF',))
    C.stt(AC[:], KF[:], -2 * PI, AC[:], ALU.mult, ALU.add, ('KF', 'AC'), ('AC',))
    C.ts(MK[:], AC[:], PI, None, ALU.is_gt, None, ('AC',), ('MK',))
    C.stt(AC[:], MK[:], -2 * PI, AC[:], ALU.mult, ALU.add, ('MK', 'AC'), ('AC',))
    C.ts(MK[:], AC[:], -PI, None, ALU.is_lt, None, ('AC',), ('MK',))
    C.stt(AC[:], MK[:], 2 * PI, AC[:], ALU.mult, ALU.add, ('MK', 'AC'), ('AC',))
    C.ts(AC[:], AC[:], PI, -PI, ALU.min, ALU.max, ('AC',), ('AC',))
    C.act(KF[:], AC[:], AF.Sin, ('AC',), ('KF',))
    for h in range(4):
        C.cp(C.CS[:, :, :, h, :], KF[:].rearrange("p a (m f) -> p a m f", f=16), ('KF',), ('CS',))


def gathered_bufs(C):
    C.KL = C.dram_tmp("KL", [4, 16, 128, 256], BF16)
    C.VL = C.dram_tmp("VL", [TOK, D], BF16)
    C.QL = C.dram_tmp("QL", [16, 128, TOK], BF16)
    C.KG = C.dram_tmp("KG", [NPADB + 16, 16, 128, 256], BF16)
    C.VG = C.dram_tmp("VG", [NPADB + 16, 256, D], BF16)
    C.OL = C.dram_tmp("OL", [12, 128, TOK], BF16)
    C.KW = C.dram_tmp("KW", [16, 16, 128, 256], BF16)
    C.VW = C.dram_tmp("VW", [16, 256, D], BF16)
    C.XW = C.dram_tmp("XW", [13, 2, D], F32)
    C.XHL = C.dram_tmp("XHL", [4, 2, D], F32)
    C.XG = C.dram_tmp("XG", [1 + 16, 2, D], F32)


def zero_pads(C):
    z = C.sb("zpad", [128, 4096], BF16)
    C.memset(z[:], 0.0, ('zpad',))
    for b in range(NPADB):
        C.dma(C.KG[b].rearrange("h d t -> d h t"), z[:].rearrange("p (h t) -> p h t", t=256), ('zpad',), ('KGpad',), 'c0')
        for hf in range(2):
            C.dma(C.VG[b, hf * 128:(hf + 1) * 128, :], z[:, 0:D], ('zpad',), ('VGpad',), 'c0')
    C.dma(C.XG[0], z[0:2, 0:2 * D].bitcast(F32), ('zpad',), ('XGpad',), 'c0')


RG = [[0, 1, 2, 3], [4, 5, 6, 7]]


def allgather(C, src2d, dst2d, r, w):
    C.P.op('gpsimd', lambda e: e.collective_compute("AllGather", ALU.bypass, replica_groups=RG, ins=[src2d],
                                                    outs=[dst2d]), r, w)
    C.P.q['gpsimd'][-1]['inc'] = True


def phase_qkv(C, l):
    nc = C.nc
    wq_d = C.dram_in("w_qkv", [DEPTH, D, 3 * D], F32)[l].rearrange("(kc p) n -> p kc n", p=128)
    ln_d = C.dram_in("ln_mix", [DEPTH, 128, D], F32)[l]
    g_d = C.dram_in("qk_gain4", [DEPTH, 128, 4, 128], F32)[l]
    KLv = C.KL.rearrange("m h d t -> h d m t")
    qT_o = C.QL
    v_o = C.VL

    lnw = C.sb("lnw", [128, D], F32)
    C.dma(lnw[:], ln_d, (), ('lnw',), 'c0')
    G6 = C.sb("G6", [128, 4, 128], F32)
    C.dma(G6[:], g_d, (), ('G6',), 'c0')
    make_hT(C, lnw[:], 'lnw')
    import os
    ccm = os.environ.get("CCMODE", "kv")
    if 'onlyht' in ccm:
        return

    acc = [C.ps(f"acc{i}", [128, 512], F32) for i in range(4)]
    sq = C.sb("sq", [128, 512], F32)
    qn = [C.sb(f"qn{i}", [128, 4, 128], F32) for i in range(2)]
    qb = [C.sb(f"qb{i}", [128, 4, 128], BF16) for i in range(3)]
    rt = [C.sb(f"rt{i}", [128, 4, 4, 16], F32) for i in range(2)]
    ss = [C.sb(f"ss{i}", [128, 4], F32) for i in range(2)]
    qTs = [C.sb(f"qTs{i}", [128, 4, TOK], BF16) for i in range(2)]
    vs = [t[:].rearrange("p h (m c) -> p (h m) c", c=512) for t in qTs]

    def load_chunk(c):
        wbs, wtok = [], ()
        for q4 in range(4):
            (wb_,), wt_ = load_w(C, [wq_d[:, q4 * 4:(q4 + 1) * 4, c * 512:(c + 1) * 512]])
            wbs.append(wb_)
            wtok = wtok + wt_
        return wbs, wtok

    nxt_w = load_chunk(0)
    for c in range(12):
        wbs, wtok = nxt_w
        if c + 1 < 12:
            nxt_w = load_chunk(c + 1)
        kind = c // 4
        cc = c % 4
        so = C.nxt('qTs', 2)
        deferred = []
        for m in range(NB):
            a = C.nxt('acc', 4)
            for kc in range(16):
                C.mm(acc[a][:], C.hT[:, kc, m * 128:(m + 1) * 128], wbs[kc // 4][:, kc % 4, :], kc == 0, kc == 15,
                     (('hT', m),) + wtok, (('acc', a),))
            if kind == 2:
                C.cp(vs[so][:, m, :], acc[a][:], (('acc', a),), (('qTs', so, m),), eng='scalar')
                continue
            i = C.nxt('qn', 2)
            ib = C.nxt('qbr', 3)
            av = acc[a][:].rearrange("p (h d) -> p h d", d=128)
            if cc < 3:
                C.act(sq[:], acc[a][:], AF.Square, (('acc', a),), ('sq',))
                C.P.op('vector', lambda e, o=ss[i][:], s=sq[:].rearrange("p (h d) -> p h d", d=128):
                       e.tensor_reduce(out=o, in_=s, axis=AX.X, op=ALU.add), ('sq',), (('ss', i),))
                rstd_from_ss(C, ss[i], 4, 1.0 / 128, ('ss', i))
                for h in range(4):
                    C.stt(qn[i][:, h, :], av[:, h, :], ss[i][:, h:h + 1],
                          G6[:, (kind if cc * 4 + h < 6 else 2 + kind), :], ALU.mult, ALU.mult,
                          (('acc', a), ('ss', i), 'G6'), (('qn', i),))
                x1 = qn[i][:, :, 0:16]
                x2 = qn[i][:, :, 16:32]
                sn = C.CS[:, 0, m]
                cs = C.CS[:, 1, m]
                R = rt[i]
                C.tt(R[:, 0], x1, cs, ALU.mult, (('qn', i), 'CS'), (('rt', i),))
                C.tt(R[:, 1], x2, sn, ALU.mult, (('qn', i), 'CS'), (('rt', i),))
                C.tt(R[:, 2], x2, cs, ALU.mult, (('qn', i), 'CS'), (('rt', i),))
                C.tt(R[:, 3], x1, sn, ALU.mult, (('qn', i), 'CS'), (('rt', i),))
                C.tt(qb[ib][:, :, 0:16], R[:, 0], R[:, 1], ALU.subtract, (('rt', i),), (('qb', ib),))
                C.tt(qb[ib][:, :, 16:32], R[:, 2], R[:, 3], ALU.add, (('rt', i),), (('qb', ib),))
                C.cp(qb[ib][:, :, 32:128], qn[i][:, :, 32:128], (('qn', i),), (('qb', ib),), eng='scalar')
            else:
                C.cp(qb[ib][:], av, (('acc', a),), (('qb', ib),), eng='scalar')
            def do_tr(ib=ib, m=m, so=so):
                j = C.nxt('tp', 2)
                for h in range(4):
                    C.tr(C.tp[j][:, h, :], qb[ib][:, h, :], C.ident[:], (('qb', ib), 'ident'), (('tp', j),))
                C.cp(qTs[so][:, :, m * 128:(m + 1) * 128], C.tp[j][:], (('tp', j),), (('qTs', so, m),), eng='vector')
            if deferred:
                deferred.pop()()
            deferred.append(do_tr)
        if deferred:
            deferred.pop()()
        if 'nostore' in ccm or ('nok' in ccm and kind == 1) or ('nov' in ccm and kind == 2) or ('noq' in ccm and kind == 0):
            continue
        if kind == 2:
            for m in range(NB):
                C.dma(v_o[m * 128:(m + 1) * 128, cc * 512:(cc + 1) * 512], vs[so][:, m, :], (('qTs', so, m),),
                      (('VL', m, cc),), f'os{so}')
        elif kind == 0:
            C.dma(qT_o[cc * 4:(cc + 1) * 4].rearrange("h d t -> d h t"), qTs[so][:],
                  tuple(('qTs', so, m) for m in range(NB)), (('QL', cc),), f'os{so}')
        else:
            for mq in range(4):
                C.dma(C.KL[mq, cc * 4:(cc + 1) * 4].rearrange("h d t -> d h t"), qTs[so][:, :, mq * 256:(mq + 1) * 256],
                      (('qTs', so, 2 * mq), ('qTs', so, 2 * mq + 1)), (('KL', cc, mq),), f'os{so}')
    KG2 = C.KG.rearrange("b h d t -> (b h d) t")
    VG2 = C.VG.rearrange("b t c -> (b t) c")
    for m in range(4):
        if 'cck' in ccm or ccm == 'kv':
          allgather(C, C.KL[m].rearrange("h d t -> (h d) t"), KG2[(NPADB + 4 * m) * 2048:(NPADB + 4 * m + 4) * 2048, :],
                  tuple(('KL', cc, m) for cc in range(4)), (('KG', m),))
        if 'ccv' in ccm or ccm == 'kv':
          allgather(C, C.VL[m * 256:(m + 1) * 256, :], VG2[(NPADB + 4 * m) * 256:(NPADB + 4 * m + 4) * 256, :],
                  tuple(('VL', mm_, cc) for mm_ in (2 * m, 2 * m + 1) for cc in range(4)), (('VG', m),))


def own_tokens(j):
    return np.concatenate([np.arange((4 * m + j) * 256, (4 * m + j + 1) * 256) for m in range(4)])


def bc128(v):
    v = np.asarray(v, np.float32).reshape(1, -1)
    return np.ascontiguousarray(np.broadcast_to(v, (128, v.shape[1])))


def consts():
    ident = np.eye(128, dtype=np.float32).astype(NPBF)
    invf = (500000.0 ** (-np.arange(0, 32, 2, dtype=np.float32) / 32)).astype(np.float32)
    return dict(ident=ident, invf=bc128(invf))


def gain4(qk_gain_l):
    return np.ascontiguousarray(np.broadcast_to(np.asarray(qk_gain_l, np.float32)[None], (128, 4, 128)))


def gain6(qk_gain_l):
    out = np.zeros((6, 512), np.float32)
    for kind in range(2):
        for cc in range(3):
            for h in range(4):
                head = cc * 4 + h
                row = kind if head < 6 else 2 + kind
                out[kind * 3 + cc, h * 128:(h + 1) * 128] = qk_gain_l[row]
    return np.ascontiguousarray(np.broadcast_to(out[None], (128, 6, 512)))


def run(nc, in_maps):
    res = run_bass_kernel_spmd(nc, in_maps, core_ids=list(range(NCORE)))
    return res.results


NKB = 32
NKB2 = 16
NPADB = 3
DEPTH = 2
DIL = ((128, 1), (512, 4), (2048, 16))


def amask_index(g, dl):
    nd = DIL[g][0] // 128
    if g == 0:
        return dl
    base = 2 + 3 * (g - 1)
    return base + (0 if dl == 0 else (2 if dl == nd else 1))


def phase_attn(C, l):
    nc = C.nc
    am_d = C.dram_in("amask", [128, 8, 128], BF16)
    tri_d = C.dram_in("tri", [128, 2, 128], BF16)
    neg_d = C.dram_in("neglt", [128, 128], F32)
    padv_d = C.dram_in("padv", [128, 6, 128], BF16)
    gm_d = C.dram_in("bsel", [128, 3, 4, NKB2], F32)
    e19_d = C.dram_in("e19", [NKB2, NKB2 * 128], BF16)
    u_d = C.dram_in("uones", [128, 2, 128], F32)
    idf_d = C.dram_in("identf", [128, 128], F32)
    oT_o = C.OL
    jv = C.jv
    kg_tok = ('KW',)
    vg_tok = ('VW',)
    f2 = lambda ap, pat: ap.rearrange(pat).rearrange("(r c) -> r c", c=16384)
    C.dma(f2(C.KW, "b h d t -> (b h d t)"), f2(C.KG[bass.ds(jv, 16)], "b h d t -> (b h d t)"),
          tuple(('KG', m) for m in range(4)) + ('KGpad',), ('KW',), 'kw')
    C.dma(f2(C.VW, "b t c -> (b t c)"), f2(C.VG[bass.ds(jv, 16)], "b t c -> (b t c)"),
          tuple(('VG', m) for m in range(4)) + ('VGpad',), ('VW',), 'kw')
    ql_tok = tuple(('QL', cc) for cc in range(4))

    def cload(name, d, shape, dt):
        t = C.sb(name, shape, dt)
        C.dma(t[:], d, (), (name,), 'c0')
        return t
    AM = cload("AM", am_d, [128, 8, 128], BF16)
    TRI = cload("TRI", tri_d, [128, 2, 128], BF16)
    NEGLT = cload("NEGLT", neg_d, [128, 128], F32)
    PADV = cload("PADV", padv_d, [128, 6, 128], BF16)
    BS = cload("BS", gm_d, [128, 3, 4, NKB2], F32)
    E19 = cload("E19", e19_d, [NKB2, NKB2 * 128], BF16)
    UO = cload("UO", u_d, [128, 2, 128], F32)
    IDF = cload("IDF", idf_d, [128, 128], F32)
    ONESB = C.sb("ONESB", [128, 128], BF16)
    C.memset(ONESB[:], 1.0, ('ONESB',))
    ctoks = ('AM', 'TRI', 'NEGLT', 'PADV', 'BS', 'E19', 'UO', 'IDF', 'ONESB')

    kTb = [C.sb(f"kTb{i}", [128, NKB * 128], BF16) for i in range(2)]
    vb = [C.sb(f"vb{i}", [128, NKB, 128], BF16) for i in range(2)]
    qb = [C.sb(f"qTb{i}", [128, TOK], BF16) for i in range(2)]
    ost = [C.sb(f"ost{i}", [128, TOK], BF16) for i in range(2)]
    PB = [C.ps(f"pb{i}", [128, 512], F32) for i in range(8)]
    Z = [PB[i][:, 0:128] for i in range(3)]
    ACCO = [PB[3][:, 0:128], PB[4][:, 0:128]]
    AFT = [PB[5][:, 0:128], PB[6][:, 0:128]]
    ACCS = AFT
    GATE = [PB[7][:, 0:NKB2]] * 2
    SBT = [PB[7][0:NKB2, 128:256]] * 2
    NT = 4
    tmp = {nm: [C.sb(f"{nm}{i}", [128, 128], F32) for i in range(NT)] for nm in ('et', 'spt', 'lmt', 't1', 't2')}
    at = [C.sb(f"at{i}", [128, 128], BF16) for i in range(NT)]
    Rt = C.sb("Rt", [128, 128], F32)
    AO = C.sb("AO", [128, NB, 128], F32)
    AS = C.sb("AS", [128, NB, 128], F32)
    rs = [C.sb(f"rs{i}", [128, 128], F32) for i in range(2)]
    kmf = C.sb("kmf", [128, NKB2], F32)
    kmb = C.sb("kmb", [128, NKB2], BF16)
    kml = C.sb("kml", [128, NKB2], BF16)
    gsb = [C.sb(f"gsb{i}", [128, NKB2], F32) for i in range(2)]
    top8 = [C.sb(f"top8{i}", [128, 8], F32) for i in range(2)]
    sbTb = [C.sb(f"sbTb{i}", [NKB2, 128], BF16) for i in range(NB)]

    def run_pipeline(jobs, stages):
        n, S = len(jobs), len(stages)
        for t in range(n + S - 1):
            for st_ in range(S - 1, -1, -1):
                k = t - st_
                if 0 <= k < n:
                    stages[st_](jobs[k], k)

    head_order = [12, 13, 14, 15] + [2 * g + sl for sl in range(2) for g in range(3)] + [6, 7, 8, 9, 10, 11]
    loaded = {}

    def load_head(h):
        if h not in loaded:
            loaded[h] = _load_head(h)
        k = head_order.index(h)
        if k + 1 < len(head_order) and head_order[k + 1] not in loaded:
            loaded[head_order[k + 1]] = _load_head(head_order[k + 1])
        return loaded[h]

    def _load_head(h):
        i = C.nxt('kv', 2)
        kdst = kTb[i][:].rearrange("d (n t) -> d n t", t=256)
        for q4 in range(4):
            C.dma(kdst[:, q4 * 4:(q4 + 1) * 4, :],
                  C.KW[q4 * 4:(q4 + 1) * 4, h].rearrange("n d t -> d n t"), kg_tok, (('kT', i),), f'kv{i}')
            C.dma(vb[i][:, q4 * 8:(q4 + 1) * 8, :],
                  C.VW[q4 * 4:(q4 + 1) * 4, :, h * 128:(h + 1) * 128].rearrange("n (hf s) d -> s (n hf) d", hf=2),
                  vg_tok, (('v', i),), f'kv{i}')
        C.dma(qb[i][:], C.QL[h], ql_tok, (('q', i),), f'kv{i}')
        return i

    def store_head(slot, oi):
        C.dma(oT_o[slot], ost[oi][:], (('ost', oi),), (('OL', slot),), f'oo{oi}')

    def kq(lm):
        return 2 * (4 * (lm // 2) + 3) + (lm % 2)

    for hc in range(4):
        hi = load_head(12 + hc)
        oi = C.nxt('ost', 2)
        kT, v, q = kTb[hi], vb[hi], qb[hi]
        hk = (('kT', hi), ('q', hi))
        jobs = [dict(lm=lm, kb=kb, first=(kb == kq(lm)), last=(kb == 0)) for lm in range(NB)
                for kb in range(kq(lm), -1, -1)]

        def c_s0(j, k):
            zi, ti = k % 3, k % NT
            C.mm(Z[zi], kT[:, j['kb'] * 128:(j['kb'] + 1) * 128], q[:, j['lm'] * 128:(j['lm'] + 1) * 128], True, True,
                 hk, (('z', zi),))
            C.act(tmp['et'][ti][:], Z[zi], AF.Exp, (('z', zi),), (('et', ti),), scale=SCALE)

        def c_s0b(j, k):
            ti = k % NT
            C.act(tmp['spt'][ti][:], tmp['et'][ti][:], AF.Ln, (('et', ti),), (('spt', ti),), bias=1.0, scale=1.0)
            if j['first']:
                C.tt(tmp['lmt'][ti][:], tmp['spt'][ti][:], TRI[:, 1, :], ALU.mult, (('spt', ti), 'TRI'),
                     (('lmt', ti),), eng='gpsimd')

        def c_s1(j, k):
            zi, ti, ai = k % 3, k % NT, k % 2
            first, last = j['first'], j['last']
            L, Ltok = (tmp['lmt'][ti], ('lmt', ti)) if first else (tmp['spt'][ti], ('spt', ti))
            C.mm(AFT[ai], UO[:, 0, :], L[:], True, first, (Ltok, 'UO'), (('ps56', ai),))
            if not first:
                C.mm(AFT[ai], UO[:, 1, :], Rt[:], False, True, ('Rt', 'UO'), (('ps56', ai),))
            if not last:
                if first:
                    C.cp(Rt[:], L[:], (Ltok,), ('Rt',), eng='vector')
                else:
                    C.tt(Rt[:], Rt[:], L[:], ALU.add, ('Rt', Ltok), ('Rt',), eng='vector')
            C.stt(tmp['t1'][ti][:], Z[zi], SCALE, tmp['spt'][ti][:], ALU.mult, ALU.subtract, (('z', zi), ('spt', ti)),
                  (('t1', ti),))
            C.tt(tmp['t2'][ti][:], tmp['t1'][ti][:], AFT[ai], ALU.subtract, (('t1', ti), ('ps56', ai)), (('t2', ti),))
            if first:
                C.tt(tmp['t2'][ti][:], tmp['t2'][ti][:], NEGLT[:], ALU.add, (('t2', ti), 'NEGLT'), (('t2', ti),))
            C.act(at[ti][:], tmp['t2'][ti][:], AF.Exp, (('t2', ti),), (('at', ti),))

        def c_s2(j, k, oi=oi, v=v, hi=hi):
            ti, ao = k % NT, j['lm'] % 2
            C.mm(ACCO[ao], v[:, j['kb'], :], at[ti][:], j['first'], j['last'], (('v', hi), ('at', ti)), (('acco', ao),))
            if j['last']:
                C.cp(ost[oi][:, j['lm'] * 128:(j['lm'] + 1) * 128], ACCO[ao], (('acco', ao),), (('ost', oi),),
                     eng='scalar')

        run_pipeline(jobs, [c_s0, c_s0b, c_s1, c_s2])
        store_head(8 + hc, oi)

    for slot in range(2):
        oi = C.nxt('ost', 2)
        for g in range(3):
            hi = load_head(2 * g + slot)
            kT, v, q = kTb[hi], vb[hi], qb[hi]
            hk = (('kT', hi), ('q', hi))
            nd = DIL[g][0] // 128
            jobs = []
            for lm in range(NB):
                kbs = [kq(lm) - dl for dl in range(nd + 1) if kq(lm) - dl >= 0]
                for kb in kbs:
                    jobs.append(dict(lm=lm, kb=kb, first=(kb == kbs[0]), last=(kb == kbs[-1]), dl=kq(lm) - kb))

            def a_s0(j, k, g=g):
                zi, ti = k % 3, k % NT
                C.mm(Z[zi], kT[:, j['kb'] * 128:(j['kb'] + 1) * 128], q[:, j['lm'] * 128:(j['lm'] + 1) * 128], True,
                     True, hk, (('z', zi),))
                C.act(tmp['et'][ti][:], Z[zi], AF.Exp, (('z', zi),), (('et', ti),), scale=SCALE)
                C.tt(at[ti][:], tmp['et'][ti][:], AM[:, amask_index(g, j['dl']), :], ALU.mult, (('et', ti), 'AM'),
                     (('at', ti),), eng='gpsimd' if (k % 2) else 'vector')

            def a_s1(j, k, g=g):
                ti, ao = k % NT, j['lm'] % 2
                kb, lm = j['kb'], j['lm']
                C.mm(ACCO[ao], v[:, kb, :], at[ti][:], j['first'], j['last'], (('v', hi), ('at', ti)), (('acco', ao),))
                C.mm(ACCS[ao], PADV[:, kb, :] if kb < 6 else ONESB[:], at[ti][:], j['first'], j['last'],
                     (('at', ti), 'PADV', 'ONESB'), (('ps56', ao),))
                if j['last']:
                    if g == 0:
                        C.cp(AO[:, lm, :], ACCO[ao], (('acco', ao),), (('AO', lm),), eng='scalar')
                        C.cp(AS[:, lm, :], ACCS[ao], (('ps56', ao),), (('AS', lm),), eng='scalar')
                    else:
                        C.tt(AO[:, lm, :], AO[:, lm, :], ACCO[ao], ALU.add, (('AO', lm), ('acco', ao)), (('AO', lm),))
                        C.tt(AS[:, lm, :], AS[:, lm, :], ACCS[ao], ALU.add, (('AS', lm), ('ps56', ao)), (('AS', lm),))

            run_pipeline(jobs, [a_s0, a_s1])
        for lm in range(NB):
            C.recip(AS[:, lm, :], AS[:, lm, :], (('AS', lm),), (('AS', lm),))
            C.tt(ost[oi][:, lm * 128:(lm + 1) * 128], AO[:, lm, :], AS[:, lm, :], ALU.mult, (('AO', lm), ('AS', lm)),
                 (('ost', oi),))
        store_head(slot, oi)

    for hb_ in range(6):
        hi = load_head(6 + hb_)
        oi = C.nxt('ost', 2)
        kT, v, q = kTb[hi], vb[hi], qb[hi]
        hk = (('kT', hi), ('q', hi))
        C.P.op('vector', lambda e, o=kmf[:], s=kT[:].rearrange("p (n k) -> p n k", k=256):
               e.tensor_reduce(out=o, in_=s, axis=AX.X, op=ALU.add), (('kT', hi),), ('kmf',))
        C.ts(kmf[:], kmf[:], 1.0 / 256, None, ALU.mult, None, ('kmf',), ('kmf',))
        C.cp(kmb[:], kmf[:], ('kmf',), ('kmb',))
        C.tt(kml[:], kmf[:], kmb[:], ALU.subtract, ('kmf', 'kmb'), ('kml',))
        def b_prep(lm):
            mp = lm // 2
            qs = q[:, lm * 128:(lm + 1) * 128]
            gi = lm % 2
            G = gsb[gi]
            C.mm(GATE[gi], qs, kmb[:], True, False, (('q', hi), 'kmb'), ('pb7',))
            C.mm(GATE[gi], qs, kml[:], False, True, (('q', hi), 'kml'), ('pb7',))
            C.tt(G[:], GATE[gi], BS[:, 0, mp, :], ALU.add, ('pb7', 'BS'), (('gsb', gi),))
            C.P.op('vector', lambda e, o=top8[gi][:], s=G[:]: e.max(out=o, in_=s), (('gsb', gi),), (('top8', gi),))
            C.ts(G[:], G[:], top8[gi][:, 2:3], None, ALU.is_ge, None, (('gsb', gi), ('top8', gi)), (('gsb', gi),))
            C.tt(G[:], G[:], BS[:, 1, mp, :], ALU.mult, (('gsb', gi), 'BS'), (('gsb', gi),))
            C.tt(G[:], G[:], BS[:, 2, mp, :], ALU.add, (('gsb', gi), 'BS'), (('gsb', gi),))
            C.tr(SBT[gi], G[:], IDF[:], (('gsb', gi), 'IDF'), ('pb7',))
            C.cp(sbTb[lm][:], SBT[gi], ('pb7',), (('sbTb', lm),), eng='scalar')

        b_prep(0)
        b_prep(1)
        jobs = [dict(lm=lm, kb=kb, first=(kb == 0), last=(kb == kq(lm))) for lm in range(NB)
                for kb in range(0, kq(lm) + 1)]

        def b_s0(j, k):
            zi, ti = k % 3, k % NT
            kb, lm = j['kb'], j['lm']
            n2 = kb // 2
            if j['first'] and lm + 2 < NB:
                b_prep(lm + 2)
            C.mm(Z[zi], kT[:, kb * 128:(kb + 1) * 128], q[:, lm * 128:(lm + 1) * 128], True, False, hk, (('z', zi),))
            C.mm(Z[zi], E19[:, n2 * 128:(n2 + 1) * 128], sbTb[lm][:], False, True, (('sbTb', lm), 'E19'), (('z', zi),))
            if j['last']:
                C.act(tmp['et'][ti][:], Z[zi], AF.Exp, (('z', zi),), (('et', ti),), scale=SCALE)
                C.tt(at[ti][:], tmp['et'][ti][:], TRI[:, 0, :], ALU.mult, (('et', ti), 'TRI'), (('at', ti),))
            else:
                C.act(at[ti][:], Z[zi], AF.Exp, (('z', zi),), (('at', ti),), scale=SCALE)

        def b_s1(j, k, oi=oi):
            ti, ao = k % NT, j['lm'] % 2
            kb, lm = j['kb'], j['lm']
            C.mm(ACCO[ao], v[:, kb, :], at[ti][:], j['first'], j['last'], (('v', hi), ('at', ti)), (('acco', ao),))
            C.mm(ACCS[ao], ONESB[:], at[ti][:], j['first'], j['last'], (('at', ti), 'ONESB'), (('ps56', ao),))
            if j['last']:
                C.recip(rs[ao][:], ACCS[ao], (('ps56', ao),), (('rs', ao),))
                C.tt(ost[oi][:, lm * 128:(lm + 1) * 128], ACCO[ao], rs[ao][:], ALU.mult, (('acco', ao), ('rs', ao)),
                     (('ost', oi),))

        run_pipeline(jobs, [b_s0, b_s1])
        store_head(2 + hb_, oi)


def attn_consts(j):
    s = np.arange(128)[:, None]
    t = np.arange(128)[None, :]
    am = np.zeros((8, 128, 128), np.float32)
    for g, (w, dil) in enumerate(DIL):
        nd = w // 128
        for dl in range(nd + 1):
            dist = t - s + 128 * dl
            am[amask_index(g, dl)] = ((dist >= 0) & (dist <= w) & (dist % dil == 0))
    tri = np.stack([(s <= t), (s < t)]).astype(np.float32)
    neglt = np.where(s < t, 0.0, NEG).astype(np.float32)
    npad = 2 * (3 - j)
    padv = np.zeros((6, 128, 128), np.float32)
    padv[npad:] = 1.0
    bsel = np.zeros((3, 4, NKB2), np.float32)
    for mp in range(4):
        own = 4 * mp + 3
        for n in range(NKB2):
            valid_past = (n >= 3 - j) and (n < own)
            bsel[0, mp, n] = 0.0 if valid_past else -1e30
            bsel[1, mp, n] = -NEG if valid_past else 0.0
            bsel[2, mp, n] = 0.0 if n == own else NEG
    e19 = np.zeros((NKB2, NKB2, 128), np.float32)
    for n in range(NKB2):
        e19[n, n, :] = 1.0
    uo = np.stack([(s > t).astype(np.float32), np.ones((128, 128), np.float32)])
    return dict(amask=np.ascontiguousarray(am.transpose(1, 0, 2)).astype(NPBF),
                tri=np.ascontiguousarray(tri.transpose(1, 0, 2)).astype(NPBF),
                neglt=neglt,
                padv=np.ascontiguousarray(padv.transpose(1, 0, 2)).astype(NPBF),
                bsel=np.ascontiguousarray(np.broadcast_to(bsel[None], (128, 3, 4, NKB2))),
                e19=e19.reshape(NKB2, NKB2 * 128).astype(NPBF),
                uones=np.ascontiguousarray(uo.transpose(1, 0, 2)),
                identf=np.eye(128, dtype=np.float32))


def head_norm(C, accap, acctok, gain_ap, gain_tok, out_bf, out_tok, sq, ss, sstok):
    av = accap.rearrange("p (h d) -> p h d", d=128)
    C.act(sq[:], accap, AF.Square, (acctok,), ('sq',))
    C.P.op('vector', lambda e, o=ss[:], s_=sq[:].rearrange("p (h d) -> p h d", d=128):
           e.tensor_reduce(out=o, in_=s_, axis=AX.X, op=ALU.add), ('sq',), (sstok,))
    rstd_from_ss(C, ss, 4, 1.0 / 128, sstok)
    for h in range(4):
        C.stt(out_bf[:, h, :], av[:, h, :], ss[:, h:h + 1], gain_ap, ALU.mult, ALU.mult,
              (acctok, sstok, gain_tok), (out_tok,))


def phase_post(C, l):
    r3 = lambda ap: ap.rearrange("(kc p) n -> p kc n", p=128)
    wg_d = r3(C.dram_in("w_gate", [DEPTH, D, 3 * D], F32)[l])
    wbr_d = [r3(C.dram_in("w_br_a", [DEPTH, 256, D], F32)[l]), r3(C.dram_in("w_br_b", [DEPTH, 768, D], F32)[l]),
             r3(C.dram_in("w_br_c", [DEPTH, 512, D], F32)[l])]
    wo_d = r3(C.dram_in("w_o", [DEPTH, D, D], F32)[l])
    bg_d = C.dram_in("b_gate_p", [DEPTH, 128, 48], F32)[l]
    ln_d = C.dram_in("ln_mix", [DEPTH, 128, D], F32)[l]
    lnq_d = C.dram_in("ln_mem_q", [DEPTH, 128, D], F32)[l]
    lnkv_d = C.dram_in("ln_mem_kv", [DEPTH, 128, D], F32)[l]
    wmq_d = r3(C.dram_in("wm_q", [DEPTH, D, 512], F32)[l])
    wmkv_d = r3(C.dram_in("wm_kv", [DEPTH, D, 1024], F32)[l])
    wmo_d = r3(C.dram_in("wm_o", [DEPTH, 512, D], F32)[l])
    mg_d = C.dram_in("mem_gain", [DEPTH, 128, 2, 128], F32)[l]
    mem_d = C.dram_in("mem", [256, D], F32)
    oT_d = C.OL

    lnw = C.sb("lnw", [128, D], F32)
    C.dma(lnw[:], ln_d, (), ('lnw',), 'c0')
    BG = C.sb("BG", [128, 48], F32)
    C.dma(BG[:], bg_d, (), ('BG',), 'c0')
    MG = C.sb("MG", [128, 2, 128], F32)
    C.dma(MG[:], mg_d, (), ('MG',), 'c0')
    oT = C.sb("oTs", [128, 12, TOK], BF16)
    C.dma(oT[:], oT_d.rearrange("s d t -> d s t"), tuple(('OL', sl) for sl in range(12)), ('oT',), 'c1')
    ONESB = C.sb("ONESB", [128, 128], BF16)
    C.memset(ONESB[:], 1.0, ('ONESB',))

    make_hT(C, lnw[:], 'lnw')
    hT_all = tuple(('hT', m) for m in range(NB))

    PSB = [C.ps(f"psb{i}", [128, 512], F32) for i in range(6)]
    mT = [C.sb(f"mT{i}", [128, 4, TOK], BF16) for i in range(1)]
    gs = [C.sb(f"gs{i}", [128, 512], F32) for i in range(2)]
    tq = [C.sb(f"tq{i}", [128, 512], F32) for i in range(2)]
    macc = [C.sb(f"macc{i}", [128, 512], F32) for i in range(2)]
    brslot = (0, 2, 8)
    brk = (2, 6, 4)

    for grp in range(4):
        mi = 0
        for ncl in range(4):
            ncg = grp * 4 + ncl
            for br in range(3):
                (wg,), wgt = load_w(C, [wg_d[:, :, br * D + ncg * 128: br * D + (ncg + 1) * 128]])
                (wb,), wbt = load_w(C, [wbr_d[br][:, :, ncg * 128:(ncg + 1) * 128]])
                for tg in range(2):
                    pg = C.nxt('psg', 2)
                    pp = 2 + C.nxt('psp', 2)
                    tsl = slice(tg * 512, (tg + 1) * 512)
                    for kc in range(16):
                        C.mm(PSB[pg][:], wg[:, kc, :], C.hT[:, kc, tsl], kc == 0, kc == 15, hT_all + wgt,
                             (('psb', pg),))
                    for kc in range(brk[br]):
                        C.mm(PSB[pp][:], wb[:, kc, :], oT[:, brslot[br] + kc, tsl], kc == 0, kc == brk[br] - 1,
                             ('oT',) + wbt, (('psb', pp),))
                    gi = C.nxt('gs', 2)
                    C.act(gs[gi][:], PSB[pg][:], AF.Sigmoid, (('psb', pg), 'BG'), (('gs', gi),),
                          bias=BG[:, br * 16 + ncg: br * 16 + ncg + 1], scale=1.0)
                    if br == 0:
                        C.tt(macc[tg][:], gs[gi][:], PSB[pp][:], ALU.mult, (('gs', gi), ('psb', pp)), (('macc', tg),))
                    else:
                        C.tt(tq[gi][:], gs[gi][:], PSB[pp][:], ALU.mult, (('gs', gi), ('psb', pp)), (('tq', gi),))
                        if br == 1:
                            C.tt(macc[tg][:], macc[tg][:], tq[gi][:], ALU.add, (('macc', tg), ('tq', gi)),
                                 (('macc', tg),), eng='gpsimd')
                        else:
                            C.tt(mT[mi][:, ncl, tsl], macc[tg][:], tq[gi][:], ALU.add, (('macc', tg), ('tq', gi)),
                                 (('mT', mi),), eng='gpsimd')
        for n in range(4):
            (wo,), wot = load_w(C, [wo_d[:, grp * 4:(grp + 1) * 4, n * 512:(n + 1) * 512]])
            for m in range(NB):
                pa = 4 + C.nxt('psa', 2)
                for kc in range(4):
                    C.mm(PSB[pa][:], mT[mi][:, kc, m * 128:(m + 1) * 128], wo[:, kc, :], kc == 0, kc == 3,
                         (('mT', mi),) + wot, (('psb', pa),))
                xs = C.X[:, m, n * 512:(n + 1) * 512]
                C.tt(xs, xs, PSB[pa][:], ALU.add, (('X', m), ('psb', pa)), (('X', m),))

    lnq = lnw
    C.dma(lnq[:], lnq_d, (), ('lnw',), 'c0')
    make_hT(C, lnq[:], 'lnw')
    lnkv = lnw
    C.dma(lnkv[:], lnkv_d, (), ('lnw',), 'c0')
    C.P.op('vector', lambda e: e.memset(ONESB[:, 0:1], 1.0), ('ONESB',), ('oT', 'oTfree'))
    oTf = oT[:].rearrange("p s t -> p (s t)")
    memT = oT[:].rearrange("p s t -> p (s t)")[:, 0:4096].rearrange("p (k t) -> p k t", t=256)
    memx = mT[0][:].rearrange("p a t -> p (a t)").bitcast(F32)
    for mb in range(2):
        C.dma(memx, mem_d[mb * 128:(mb + 1) * 128, :], (), (('mT', 0),), 'c1')
        i = C.nxt('hb', C.nhb)
        st = C.nst[i]
        C.act(C.hb[i][:], memx, AF.Square, (('mT', 0),), (('hb', i), ('nst', i)), accum_out=st[:, 0:1])
        rstd_from_ss(C, st, 1, 1.0 / D, ('nst', i))
        C.stt(C.hb[i][:], memx, st[:, 0:1], lnkv[:], ALU.mult, ALU.mult, (('mT', 0), ('nst', i), 'lnw'), (('hb', i),))
        for g in range(4):
            j = C.nxt('tp', 2)
            for a in range(4):
                kc = g * 4 + a
                C.tr(C.tp[j][:, a, :], C.hb[i][:, kc * 128:(kc + 1) * 128], C.ident[:], (('hb', i), 'ident'),
                     (('tp', j),))
            C.cp(memT[:, g * 4:(g + 1) * 4, mb * 128:(mb + 1) * 128], C.tp[j][:], (('tp', j), 'oTfree'), ('memT',),
                 eng='scalar' if g % 2 == 0 else 'vector')
    memT = oTf[:, 0:4096].rearrange("p (k t) -> p k t", t=256)
    sq = C.sb("sq", [128, 512], F32)
    ss = [C.sb(f"ss{i}", [128, 4], F32) for i in range(2)]
    nb16 = [C.sb(f"nb16_{i}", [128, 4, 128], BF16) for i in range(2)]
    kmT = C.sb("kmT", [128, 4, 256], BF16)
    vm = C.sb("vm", [128, 2, 512], BF16)
    qmT = oTf[:, 4096:8192].rearrange("p (h t) -> p h t", t=TOK)
    omT = oTf[:, 8192:12288].rearrange("p (h t) -> p h t", t=TOK)
    for c in range(2):
        ws, wt = [], ()
        for q4 in range(4):
            (w_,), t_ = load_w(C, [wmkv_d[:, q4 * 4:(q4 + 1) * 4, c * 512:(c + 1) * 512]])
            ws.append(w_)
            wt = wt + t_
        for mb in range(2):
            pa = 4 + C.nxt('psa', 2)
            for kc in range(16):
                C.mm(PSB[pa][:], memT[:, kc, mb * 128:(mb + 1) * 128], ws[kc // 4][:, kc % 4, :], kc == 0, kc == 15,
                     ('memT',) + wt, (('psb', pa),))
            if c == 1:
                C.cp(vm[:, mb, :], PSB[pa][:], (('psb', pa),), ('vm',), eng='scalar')
            else:
                i = C.nxt('nb16', 2)
                head_norm(C, PSB[pa][:], ('psb', pa), MG[:, 1, :], 'MG', nb16[i], ('nb16', i), sq, ss[i], ('ss', i))
                j = C.nxt('tp', 2)
                for h in range(4):
                    C.tr(C.tp[j][:, h, :], nb16[i][:, h, :], C.ident[:], (('nb16', i), 'ident'), (('tp', j),))
                C.cp(kmT[:, :, mb * 128:(mb + 1) * 128], C.tp[j][:], (('tp', j),), ('kmT',))
    ws, wt = [], ()
    for q4 in range(4):
        (w_,), t_ = load_w(C, [wmq_d[:, q4 * 4:(q4 + 1) * 4, :]])
        ws.append(w_)
        wt = wt + t_
    for m in range(NB):
        pa = 4 + C.nxt('psa', 2)
        for kc in range(16):
            C.mm(PSB[pa][:], C.hT[:, kc, m * 128:(m + 1) * 128], ws[kc // 4][:, kc % 4, :], kc == 0, kc == 15,
                 (('hT', m),) + wt, (('psb', pa),))
        i = C.nxt('nb16', 2)
        head_norm(C, PSB[pa][:], ('psb', pa), MG[:, 0, :], 'MG', nb16[i], ('nb16', i), sq, ss[i], ('ss', i))
        j = C.nxt('tp', 2)
        for h in range(4):
            C.tr(C.tp[j][:, h, :], nb16[i][:, h, :], C.ident[:], (('nb16', i), 'ident'), (('tp', j),))
        C.cp(qmT[:, :, m * 128:(m + 1) * 128], C.tp[j][:], (('tp', j), 'oTfree'), ('qmT',))
    pb = [C.sb(f"pbm{i}", [128, 512], BF16) for i in range(2)]
    for h in range(4):
        for tg in range(2):
            tsl = slice(tg * 512, (tg + 1) * 512)
            for nb_ in range(2):
                pg = C.nxt('psg', 2)
                C.mm(PSB[pg][:], kmT[:, h, nb_ * 128:(nb_ + 1) * 128], qmT[:, h, tsl], True, True, ('kmT', 'qmT'),
                     (('psb', pg),))
                pi = C.nxt('pbm', 2)
                C.act(pb[pi][:], PSB[pg][:], AF.Exp, (('psb', pg),), (('pbm', pi),), scale=SCALE)
                C.mm(PSB[2][:], vm[:, nb_, h * 128:(h + 1) * 128], pb[pi][:], nb_ == 0, nb_ == 1, ('vm', ('pbm', pi)),
                     (('psb', 2),))
                C.mm(PSB[3][:], ONESB[:], pb[pi][:], nb_ == 0, nb_ == 1, ('ONESB', ('pbm', pi)), (('psb', 3),))
            gi = C.nxt('gs', 2)
            C.recip(gs[gi][:], PSB[3][:], (('psb', 3),), (('gs', gi),))
            C.tt(omT[:, h, tsl], PSB[2][:], gs[gi][:], ALU.mult, (('psb', 2), ('gs', gi), 'oTfree'), ('omT',))
    for n in range(4):
        (wo,), wot = load_w(C, [wmo_d[:, :, n * 512:(n + 1) * 512]])
        for m in range(NB):
            pa = 4 + C.nxt('psa', 2)
            for h in range(4):
                C.mm(PSB[pa][:], omT[:, h, m * 128:(m + 1) * 128], wo[:, h, :], h == 0, h == 3, ('omT',) + wot,
                     (('psb', pa),))
            xs = C.X[:, m, n * 512:(n + 1) * 512]
            C.tt(xs, xs, PSB[pa][:], ALU.add, (('X', m), ('psb', pa)), (('X', m),))


    XG2 = C.XG.rearrange("b r c -> (b r) c")
    for mp in range(4):
        C.dma(C.XHL[mp], C.X[126:128, 2 * mp + 1, :], (('X', 2 * mp + 1),), (('XHL', mp),), 'xh')
        allgather(C, C.XHL[mp], XG2[(1 + 4 * mp) * 2:(1 + 4 * mp + 4) * 2, :], (('XHL', mp),), (('XG', mp),))


NCH = DFF // 128


def phase_ffn(C, l):
    wu_d = C.dram_in("w_up", [DEPTH, D, 2 * DFF], F32)[l].rearrange("(kc p) n -> p kc n", p=128)
    wd_d = C.dram_in("w_down", [DEPTH, DFF, D], F32)[l].rearrange("(kc p) n -> p kc n", p=128)
    ln_d = C.dram_in("ln_ffn", [DEPTH, 128, D], F32)[l]
    cw_d = C.dram_in("conv_wp", [DEPTH, 128, 2 * NCH, 4], F32)[l]

    lnw = C.sb("lnw", [128, D], F32)
    C.dma(lnw[:], ln_d, (), ('lnw',), 'c0')
    CW = C.sb("CW", [128, 2 * NCH, 4], F32)
    C.dma(CW[:], cw_d, (), ('CW',), 'c0')
    xh = C.wst[0]
    C.memset(xh[:], 0.0, (('wst', 0),))
    C.dma(C.XW.rearrange("b r c -> b (r c)"), C.XG[bass.ds(C.jv, 13)].rearrange("b r c -> b (r c)"),
          tuple(('XG', m) for m in range(4)) + ('XGpad',), ('XW',), 'kw')
    for mp in range(4):
        C.dma(xh[2 * mp:2 * mp + 2, :], C.XW[4 * mp], ('XW',), (('wst', 0),), 'ws0')
    hTh = C.sb("hTh", [128, 16, 128], BF16)
    make_hT(C, lnw[:], 'lnw')
    make_hT(C, lnw[:], 'lnw', src=lambda m: (xh[:], ('wst', 0)), nblk=1, dst=hTh, dst_tok='hTh')
    hT_all = tuple(('hT', m) for m in range(NB))

    PSB = [C.ps(f"psb{i}", [128, 512], F32) for i in range(6)]
    UH = PSB[5]
    ub = [[C.sb(f"ub{a}{i}", [128, 4, 258], F32) for i in range(2)] for a in range(2)]
    Y = [[C.sb(f"Y{a}{i}", [128, 4, 256], F32) for i in range(2)] for a in range(2)]
    aT = [C.sb(f"aT{i}", [128, 8, TOK], BF16) for i in range(1)]

    ngrp = (NCH + 7) // 8
    for cg in range(ngrp):
        ai = 0
        chunks = list(range(cg * 8, min(NCH, cg * 8 + 8)))
        for ci, ch in enumerate(chunks):
            bi = C.nxt('ub', 2)
            for a in range(2):
                col = a * DFF + ch * 128
                (w,), wt = load_w(C, [wu_d[:, :, col:col + 128]])
                U = ub[a][bi]
                utok = ('ub', a, bi)
                for tg in range(2):
                    pg = a * 2 + tg
                    for kc in range(16):
                        C.mm(PSB[pg][:], w[:, kc, :], C.hT[:, kc, tg * 512:(tg + 1) * 512], kc == 0, kc == 15,
                             hT_all + wt, (('psb', pg),))
                    C.cp(U[:, 2 * tg:2 * tg + 2, 2:258], PSB[pg][:].rearrange("p (m t) -> p m t", t=256),
                         (('psb', pg),), (utok,), eng='scalar')
                for kc in range(16):
                    C.mm(UH[:, a * 8:a * 8 + 8], w[:, kc, :], hTh[:, kc, 0:8], kc == 0, kc == 15, ('hTh',) + wt,
                         (('uh', a),))
                C.cp(U[:, :, 0:2], UH[:, a * 8:a * 8 + 8].rearrange("p (m r) -> p m r", r=2), (('uh', a),), (utok,))
                cwc = CW[:, a * NCH + ch, :]
                Yt = Y[a][bi]
                ytok = ('Y', a, bi)
                C.act(Yt[:], U[:, :, 2:258], AF.Identity, (utok, 'CW'), (ytok,), scale=cwc[:, 2:3], bias=cwc[:, 3:4])
                C.stt(Yt[:], U[:, :, 1:257], cwc[:, 1:2], Yt[:], ALU.mult, ALU.add, (utok, 'CW', ytok), (ytok,))
                C.stt(Yt[:], U[:, :, 0:256], cwc[:, 0:1], Yt[:], ALU.mult, ALU.add, (utok, 'CW', ytok), (ytok,))
            C.act(Y[0][bi][:], Y[0][bi][:], AF.Silu, (('Y', 0, bi),), (('Y', 0, bi),))
            C.tt(aT[ai][:, ci, :].rearrange("p (m t) -> p m t", t=256), Y[0][bi][:], Y[1][bi][:], ALU.mult,
                 (('Y', 0, bi), ('Y', 1, bi)), (('aT', ai),), eng='gpsimd')
        ng = len(chunks)
        for n in range(4):
            ws, wt = [], ()
            for q4 in range((ng + 3) // 4):
                k0 = cg * 8 + q4 * 4
                k1 = min(cg * 8 + ng, k0 + 4)
                (w_,), t_ = load_w(C, [wd_d[:, k0:k1, n * 512:(n + 1) * 512]])
                ws.append(w_)
                wt = wt + t_
            for m in range(NB):
                pa = 4
                for i in range(ng):
                    C.mm(PSB[pa][:], aT[ai][:, i, m * 128:(m + 1) * 128], ws[i // 4][:, i % 4, :], i == 0, i == ng - 1,
                         (('aT', ai),) + wt, (('psb', pa),))
                xs = C.X[:, m, n * 512:(n + 1) * 512]
                C.tt(xs, xs, PSB[pa][:], ALU.add, (('X', m), ('psb', pa)), (('X', m),))


import os as _os
STOP = [_os.environ.get('STOPAT')]


def build_fused():
    C = Ctx()
    load_consts(C)
    load_x(C)
    C.CS = C.sb("CS", [128, 2, NB, 4, 16], F32)
    gathered_bufs(C)
    C.jv = C.nc.partition_id() % 4
    C.phase_begin()
    rope_tables(C)
    zero_pads(C)
    C.phase_end()
    for l in range(DEPTH):
        if STOP[0] == 'init':
            break
        C.phase_begin()
        alloc_norm(C)
        alloc_wstream(C, 3, 8)
        phase_qkv(C, l)
        C.phase_end()
        if STOP[0] == 'qkv':
            break
        C.phase_begin()
        phase_attn(C, l)
        C.phase_end()
        if STOP[0] == 'attn':
            break
        C.phase_begin()
        alloc_norm(C, 2)
        alloc_wstream(C, 2, 4)
        phase_post(C, l)
        C.phase_end()
        if STOP[0] == 'post':
            break
        C.phase_begin()
        alloc_norm(C, 2)
        alloc_wstream(C, 2, 4)
        phase_ffn(C, l)
        C.phase_end()
        if STOP[0] == 'ffn':
            break
    C.P.barrier()
    store_x(C)
    NAMES[:] = [k for k, v in C.dcache.items()]
    return C.finish()


NAMES = []


_PROG = []


def kernel(x, mem, positions, ln_mix, w_qkv, qk_gain, w_br_a, w_br_b, w_br_c, w_gate, b_gate, w_o,
           ln_mem_q, ln_mem_kv, wm_q, wm_kv, wm_o, mem_qk_gain, ln_ffn, w_up, conv_w, conv_b, w_down):
    f32 = lambda a: np.ascontiguousarray(np.asarray(a, dtype=np.float32))
    x = f32(x)
    mem = f32(mem)
    positions = np.asarray(positions).astype(np.int32)
    L = DEPTH
    bcl = lambda a: np.ascontiguousarray(np.broadcast_to(f32(a)[:, None], (L, 128) + tuple(np.asarray(a).shape[1:])))
    cst = consts()
    cwp = np.concatenate([f32(conv_w).reshape(L, 3, 2 * NCH, 128).transpose(0, 3, 2, 1),
                          f32(conv_b).reshape(L, 2 * NCH, 128).transpose(0, 2, 1)[..., None]], axis=3)
    com = dict(ident=cst['ident'], invf=cst['invf'],
               w_qkv=f32(w_qkv), w_gate=f32(w_gate), w_br_a=f32(w_br_a), w_br_b=f32(w_br_b), w_br_c=f32(w_br_c),
               w_o=f32(w_o), wm_q=f32(wm_q), wm_kv=f32(wm_kv), wm_o=f32(wm_o), w_up=f32(w_up), w_down=f32(w_down),
               ln_mix=bcl(ln_mix), ln_mem_q=bcl(ln_mem_q), ln_mem_kv=bcl(ln_mem_kv), ln_ffn=bcl(ln_ffn),
               qk_gain4=bcl(qk_gain), mem_gain=bcl(mem_qk_gain),
               b_gate_p=np.ascontiguousarray(f32(b_gate).reshape(L, 48, 128).transpose(0, 2, 1)),
               conv_wp=np.ascontiguousarray(cwp))
    acst = [attn_consts(j) for j in range(4)]
    cores = list(range(NCORE))
    toks = [own_tokens(c % 4) for c in cores]
    in_maps = []
    for c in cores:
        m = dict(com)
        m.update(acst[c % 4])
        m['x_in'] = np.ascontiguousarray(x[c // 4, toks[c]])
        m['pos'] = np.ascontiguousarray(positions[c // 4, toks[c]].reshape(8, 128).T)
        m['mem'] = mem[c // 4]
        in_maps.append(m)
    if not _PROG:
        _PROG.append(build_fused())
    in_maps = [{k: v for k, v in m.items() if k in NAMES} for m in in_maps]
    res = run(_PROG[0], in_maps)
    out = np.zeros(x.shape, np.float32)
    for c in cores:
        out[c // 4, toks[c]] = np.asarray(res[c]['x_out'])
    return out
```

```python
import contextlib
import numpy as np
import ml_dtypes
import concourse.bass as bass
import concourse.mybir as mybir
from concourse.bass_utils import run_bass_kernel_spmd

F32 = mybir.dt.float32
BF16 = mybir.dt.bfloat16
I32 = mybir.dt.int32
AF = mybir.ActivationFunctionType
ALU = mybir.AluOpType
AX = mybir.AxisListType
NPBF = ml_dtypes.bfloat16

D = 2048
NCORE = 8
TOK = 1024
NB = 8
EPS = 1e-6
DFF = 5504
SCALE = 128 ** -0.5
PI = 3.14159265358979
NEG = -30000.0
WCAP = 2048


class Prog:
    CE = ('scalar', 'tensor', 'vector', 'gpsimd', 'sync')

    def __init__(self, nc):
        self.nc = nc
        self.q = {e: [] for e in self.CE}
        self.lastw = {}
        self.rd = {}
        self.dcnt = {}
        self.known = {e: {} for e in self.CE}
        self.pending = {e: [] for e in self.CE}
        self.emitted = {e: 0 for e in self.CE}
        self.cum = {e: [] for e in self.CE}
        self.esem = None
        self.dsem = {}

    def barrier(self):
        evs = []
        for e in self.CE:
            if self.q[e] and e != 'sync':
                evs.append((('e', e, len(self.q[e]) - 1), True))
        for k, n in self.dcnt.items():
            evs.append((('d', k, n), True))
        for e in self.CE:
            self.pending[e] = [ev for ev in evs if not (ev[0][0] == 'e' and ev[0][1] == e)]

    def op(self, eng, fn, r=(), w=(), dma=None, extra=()):
        q = self.q[eng]
        idx = len(q)
        cand = list(extra) + self.pending[eng]
        self.pending[eng] = []
        for t in r:
            ev = self.lastw.get(t)
            if ev is not None:
                cand.append((ev, True))
        for t in w:
            ev = self.lastw.get(t)
            if ev is not None:
                cand.append((ev, False))
            for ev in self.rd.get(t, ()):
                cand.append((ev, False))
        best = {}
        for ev, raw in cand:
            if ev[0] == 'e':
                if ev[1] == eng and (eng in ('tensor', 'sync') or not raw):
                    continue
                k = ('e', ev[1])
                val = ev[2]
            else:
                k = ('d', ev[1])
                val = self.dcnt[ev[1]]
            if val > best.get(k, -1):
                best[k] = val
        fw = []
        for k, val in best.items():
            if self.known[eng].get(k, -1) >= val:
                continue
            self.known[eng][k] = val
            fw.append((k[0], k[1], val))
        rec = dict(fn=fn, waits=fw, inc=False, dma=dma)
        q.append(rec)
        if dma is not None:
            n = self.dcnt.get(dma, 0) + 1
            self.dcnt[dma] = n
            me = ('d', dma, n)
        else:
            me = ('e', eng, idx)
        for n_, ev in enumerate(fw):
            if ev[0] == 'e':
                pe, pidx = ev[1], ev[2]
                if pidx < self.emitted[pe]:
                    while not self.q[pe][pidx]['inc']:
                        pidx += 1
                    fw[n_] = ('e', pe, pidx)
                else:
                    self.q[pe][pidx]['inc'] = True
        for t in w:
            self.lastw[t] = me
            self.rd[t] = []
        for t in r:
            self.rd.setdefault(t, []).append(me)
        return me

    def emit(self, st=None):
        nc = self.nc
        if self.esem is None:
            self.esem = {e: nc.alloc_semaphore('s_' + e) for e in self.CE}
        for k in self.dcnt:
            if k not in self.dsem:
                self.dsem[k] = nc.alloc_semaphore('d_' + k)
        esem, dsem, cum = self.esem, self.dsem, self.cum
        for e in self.CE:
            q = self.q[e]
            if len(q) > self.emitted[e]:
                q[-1]['inc'] = True
            c = cum[e][-1] if cum[e] else 0
            for rec in q[len(cum[e]):]:
                if rec['inc']:
                    c += 1
                cum[e].append(c)
        with nc.Block() as block:
            def run(e):
                start = self.emitted[e]

                def f(eng):
                    for rec in self.q[e][start:]:
                        rec['wv'] = []
                        for ev in rec['waits']:
                            if ev[0] == 'e':
                                eng.wait_ge(esem[ev[1]], cum[ev[1]][ev[2]])
                                rec['wv'].append((('e', ev[1]), cum[ev[1]][ev[2]]))
                            else:
                                eng.wait_ge(dsem[ev[1]], 16 * ev[2])
                                rec['wv'].append((('d', ev[1]), ev[2]))
                        ins = rec['fn'](eng)
                        if rec['dma'] is not None:
                            ins.then_inc(dsem[rec['dma']], 16)
                        elif rec['inc']:
                            ins.then_inc(esem[e], 1)
                return f

            block.sync(run('sync'))
            block.scalar(run('scalar'))
            block.tensor(run('tensor'))
            block.vector(run('vector'))
            block.gpsimd(run('gpsimd'))
        for e in self.CE:
            self.emitted[e] = len(self.q[e])


class Ctx:
    def __init__(self):
        self.nc = bass.Bass("TRN2", target_bir_lowering=False)
        self.P = Prog(self.nc)
        self.st = contextlib.ExitStack()
        self.rot = {}
        self.outs = []
        self.dq = 0
        self.pst = None
        self.phase_id = 0
        self.dcache = {}

    def phase_begin(self):
        self.P.barrier()
        self.pst = contextlib.ExitStack()
        self.phase_id += 1

    def phase_end(self):
        self.P.emit()
        self.pst.close()
        self.pst = None

    def dram_in(self, name, shape, dt):
        if name not in self.dcache:
            self.dcache[name] = self.nc.dram_tensor(name, list(shape), dt, kind="ExternalInput").ap()
        return self.dcache[name]

    def dram_tmp(self, name, shape, dt):
        if name not in self.dcache:
            self.dcache[name] = self.nc.dram_tensor(name, list(shape), dt).ap()
        return self.dcache[name]

    def dram_out(self, name, shape, dt):
        return self.nc.dram_tensor(name, list(shape), dt, kind="ExternalOutput").ap()

    def sb(self, name, shape, dt):
        st = self.st if self.pst is None else self.pst
        return st.enter_context(self.nc.sbuf_tensor(f"{name}_p{self.phase_id}", list(shape), dt))

    def ps(self, name, shape, dt=F32):
        st = self.st if self.pst is None else self.pst
        return st.enter_context(self.nc.psum_tensor(f"{name}_p{self.phase_id}", list(shape), dt))

    def nxt(self, name, n):
        i = self.rot.get(name, 0)
        self.rot[name] = i + 1
        return i % n

    def dma(self, out, in_, r, w, key, eng=None):
        if eng is None:
            eng = ('sync', 'gpsimd')[self.dq % 2] if False else 'sync'
        def fn(e):
            try:
                return e.dma_start(out=out, in_=in_)
            except Exception:
                print("DMA FAILED", out, in_)
                raise
        return self.P.op(eng, fn, r, w, dma=key)

    def act(self, out, in_, func, r, w, **kw):
        return self.P.op('scalar', lambda e: e.activation(out=out, in_=in_, func=func, **kw), r, w)

    def mm(self, out, lhsT, rhs, start, stop, r, w):
        return self.P.op('tensor', lambda e: e.matmul(out, lhsT=lhsT, rhs=rhs, start=start, stop=stop), r, w)

    def tr(self, out, in_, ident, r, w):
        return self.P.op('tensor', lambda e: e.transpose(out, in_, ident), r, w)

    def tt(self, out, in0, in1, op, r, w, eng='vector'):
        return self.P.op(eng, lambda e: e.tensor_tensor(out=out, in0=in0, in1=in1, op=op), r, w)

    def ts(self, out, in0, s1, s2, op0, op1, r, w, eng='vector'):
        if op1 is None:
            return self.P.op(eng, lambda e: e.tensor_scalar(out=out, in0=in0, scalar1=s1, scalar2=None, op0=op0), r, w)
        return self.P.op(eng, lambda e: e.tensor_scalar(out=out, in0=in0, scalar1=s1, scalar2=s2, op0=op0, op1=op1), r, w)

    def stt(self, out, in0, scalar, in1, op0, op1, r, w):
        return self.P.op('vector', lambda e: e.scalar_tensor_tensor(out=out, in0=in0, scalar=scalar, in1=in1,
                                                                    op0=op0, op1=op1), r, w)

    def cp(self, out, in_, r, w, eng='vector'):
        if eng == 'scalar':
            return self.P.op(eng, lambda e: e.copy(out=out, in_=in_), r, w)
        return self.P.op(eng, lambda e: e.tensor_copy(out=out, in_=in_), r, w)

    def recip(self, out, in_, r, w):
        return self.P.op('vector', lambda e: e.reciprocal(out=out, in_=in_), r, w)

    def memset(self, ap, val, w, eng='gpsimd'):
        return self.P.op(eng, lambda e: e.memset(ap, val), (), w)

    def finish(self):
        self.P.op('sync', lambda e: e.nop(), r=tuple(self.outs), w=())
        self.P.emit(self.st)
        self.st.close()
        return self.nc


def load_consts(C, need_ident=True):
    nc = C.nc
    ident_d = C.dram_in("ident", [128, 128], BF16)
    C.ident = C.sb("ident_sb", [128, 128], BF16)
    C.dma(C.ident[:], ident_d, (), ('ident',), 'c0')


def load_x(C, name="x_in"):
    xd = C.dram_in(name, [TOK, D], F32)
    C.X = C.sb("X", [128, NB, D], F32)
    for m in range(NB):
        C.dma(C.X[:, m, :], xd[m * 128:(m + 1) * 128, :], (), (('X', m),), 'xl')


def store_x(C, name="x_out"):
    xo = C.dram_out(name, [TOK, D], F32)
    for m in range(NB):
        C.dma(xo[m * 128:(m + 1) * 128, :], C.X[:, m, :], (('X', m),), (('xo', m),), 'xs')
        C.outs.append(('xo', m))


def alloc_norm(C, nhb=2):
    C.nhb = nhb
    C.hT = C.sb("hT", [128, 16, TOK], BF16)
    C.hb = [C.sb(f"hb{i}", [128, D], BF16) for i in range(nhb)]
    C.nst = [C.sb(f"nst{i}", [128, 4], F32) for i in range(2)]
    C.tp = [C.ps(f"tp{i}", [128, 4, 128], BF16) for i in range(2)]


def rstd_from_ss(C, st, n, inv_n, tok):
    C.ts(st[:, 0:n], st[:, 0:n], inv_n, EPS, ALU.mult, ALU.add, (tok,), (tok,))
    C.act(st[:, 0:n], st[:, 0:n], AF.Sqrt, (tok,), (tok,))
    C.recip(st[:, 0:n], st[:, 0:n], (tok,), (tok,))


def make_hT(C, gain_tile, gain_tok, src=None, nblk=NB, dst=None, dst_tok='hT'):
    dst = C.hT if dst is None else dst
    for m in range(nblk):
        xin, xtok = (C.X[:, m, :], ('X', m)) if src is None else src(m)
        i = C.nxt('hb', C.nhb)
        st = C.nst[i]
        C.act(C.hb[i][:], xin, AF.Square, (xtok,), (('hb', i), ('nst', i)), accum_out=st[:, 0:1])
        rstd_from_ss(C, st, 1, 1.0 / D, ('nst', i))
        C.stt(C.hb[i][:], xin, st[:, 0:1], gain_tile, ALU.mult, ALU.mult, (xtok, ('nst', i), gain_tok), (('hb', i),))
        for g in range(4):
            j = C.nxt('tp', 2)
            for a in range(4):
                kc = g * 4 + a
                C.tr(C.tp[j][:, a, :], C.hb[i][:, kc * 128:(kc + 1) * 128], C.ident[:], (('hb', i), 'ident'),
                     (('tp', j),))
            eng = 'scalar' if g % 2 == 0 else 'vector'
            C.cp(dst[:, g * 4:(g + 1) * 4, m * 128:(m + 1) * 128], C.tp[j][:], (('tp', j),), ((dst_tok, m),), eng=eng)


def alloc_wstream(C, nst=3, nbf=6):
    C.nwst, C.nwbf = nst, nbf
    C.wst = [C.sb(f"wst{i}", [128, WCAP], F32) for i in range(nst)]
    C.wbf = [C.sb(f"wbf{i}", [128, WCAP], BF16) for i in range(nbf)]


def load_w(C, views):
    s = C.nxt('wst', C.nwst)
    b = C.nxt('wbf', C.nwbf)
    off = 0
    outs = []
    for v in views:
        k, n = v.shape[1], v.shape[2]
        sz = k * n
        dst = C.wst[s][:, off:off + sz].rearrange("p (k n) -> p k n", n=n)
        C.dma(dst, v, (), (('wst', s),), f'ws{s}')
        outs.append(C.wbf[b][:, off:off + sz].rearrange("p (k n) -> p k n", n=n))
        off += sz
    assert off <= WCAP
    h = (off // 2 + 127) // 128 * 128
    C.cp(C.wbf[b][:, 0:h], C.wst[s][:, 0:h], (('wst', s),), (('wbf', b, 0),), eng='gpsimd')
    C.cp(C.wbf[b][:, h:off], C.wst[s][:, h:off], (('wst', s),), (('wbf', b, 1),), eng='vector')
    return outs, (('wbf', b, 0), ('wbf', b, 1))


def rope_tables(C):
    pos_d = C.dram_in("pos", [128, NB], I32)
    invf_d = C.dram_in("invf", [128, 16], F32)
    posi = C.sb("posi", [128, NB], I32)
    posf = C.sb("posf", [128, NB], F32)
    invf = C.sb("invf_sb", [128, 16], F32)
    AC = C.sb("AC", [128, 2, NB * 16], F32)
    KF = C.sb("KF", [128, 2, NB * 16], F32)
    KI = C.sb("KI", [128, 2, NB * 16], I32)
    MK = C.sb("MK", [128, 2, NB * 16], F32)
    C.dma(posi[:], pos_d, (), ('posi',), 'c0')
    C.dma(invf[:], invf_d, (), ('invf',), 'c0')
    C.cp(posf[:], posi[:], ('posi',), ('posf',))
    for m in range(NB):
        C.ts(AC[:, 0, m * 16:(m + 1) * 16], invf[:], posf[:, m:m + 1], None, ALU.mult, None, ('posf', 'invf'), ('AC',))
    C.ts(AC[:, 1, :], AC[:, 0, :], PI / 2, None, ALU.add, None, ('AC',), ('AC',))
    C.ts(KF[:], AC[:], 1.0 / (2 * PI), None, ALU.mult, None, ('AC',), ('KF',))
    C.cp(KI[:], KF[:], ('KF',), ('KI',))
    C.cp(KF[:], KI[:], ('KI',), ('KF',))
    C.stt(AC[:], KF[:], -2 * PI, AC[:], ALU.mult, ALU.add, ('KF', 'AC'), ('AC',))
    C.ts(MK[:], AC[:], PI, None, ALU.is_gt, None, ('AC',), ('MK',))
    C.stt(AC[:], MK[:], -2 * PI, AC[:], ALU.mult, ALU.add, ('MK', 'AC'), ('AC',))
    C.ts(MK[:], AC[:], -PI, None, ALU.is_lt, None, ('AC',), ('MK',))
    C.stt(AC[:], MK[:], 2 * PI, AC[:], ALU.mult, ALU.add, ('MK', 'AC'), ('AC',))
    C.ts(AC[:], AC[:], PI, -PI, ALU.min, ALU.max, ('AC',), ('AC',))
    C.act(KF[:], AC[:], AF.Sin, ('AC',), ('KF',))
    for h in range(4):
        C.cp(C.CS[:, :, :, h, :], KF[:].rearrange("p a (m f) -> p a m f", f=16), ('KF',), ('CS',))


def gathered_bufs(C):
    C.KL = C.dram_tmp("KL", [4, 16, 128, 256], BF16)
    C.VL = C.dram_tmp("VL", [TOK, D], BF16)
    C.QL = C.dram_tmp("QL", [16, 128, TOK], BF16)
    C.KG = C.dram_tmp("KG", [NPADB + 16, 16, 128, 256], BF16)
    C.VG = C.dram_tmp("VG", [NPADB + 16, 256, D], BF16)
    C.OL = C.dram_tmp("OL", [12, 128, TOK], BF16)
    C.KW = C.dram_tmp("KW", [16, 16, 128, 256], BF16)
    C.VW = C.dram_tmp("VW", [16, 256, D], BF16)
    C.XW = C.dram_tmp("XW", [13, 2, D], F32)
    C.XHL = C.dram_tmp("XHL", [4, 2, D], F32)
    C.XG = C.dram_tmp("XG", [1 + 16, 2, D], F32)


def zero_pads(C):
    z = C.sb("zpad", [128, 4096], BF16)
    C.memset(z[:], 0.0, ('zpad',))
    for b in range(NPADB):
        C.dma(C.KG[b].rearrange("h d t -> d h t"), z[:].rearrange("p (h t) -> p h t", t=256), ('zpad',), ('KGpad',), 'c0')
        for hf in range(2):
            C.dma(C.VG[b, hf * 128:(hf + 1) * 128, :], z[:, 0:D], ('zpad',), ('VGpad',), 'c0')
    C.dma(C.XG[0], z[0:2, 0:2 * D].bitcast(F32), ('zpad',), ('XGpad',), 'c0')


RG = [[0, 1, 2, 3], [4, 5, 6, 7]]


def allgather(C, src2d, dst2d, r, w):
    C.P.op('gpsimd', lambda e: e.collective_compute("AllGather", ALU.bypass, replica_groups=RG, ins=[src2d],
                                                    outs=[dst2d]), r, w)
    C.P.q['gpsimd'][-1]['inc'] = True


def phase_qkv(C, l):
    nc = C.nc
    wq_d = C.dram_in("w_qkv", [DEPTH, D, 3 * D], F32)[l].rearrange("(kc p) n -> p kc n", p=128)
    ln_d = C.dram_in("ln_mix", [DEPTH, 128, D], F32)[l]
    g_d = C.dram_in("qk_gain4", [DEPTH, 128, 4, 128], F32)[l]
    KLv = C.KL.rearrange("m h d t -> h d m t")
    qT_o = C.QL
    v_o = C.VL

    lnw = C.sb("lnw", [128, D], F32)
    C.dma(lnw[:], ln_d, (), ('lnw',), 'c0')
    G6 = C.sb("G6", [128, 4, 128], F32)
    C.dma(G6[:], g_d, (), ('G6',), 'c0')
    make_hT(C, lnw[:], 'lnw')
    import os
    ccm = os.environ.get("CCMODE", "kv")
    if 'onlyht' in ccm:
        return

    acc = [C.ps(f"acc{i}", [128, 512], F32) for i in range(4)]
    sq = C.sb("sq", [128, 512], F32)
    qn = [C.sb(f"qn{i}", [128, 4, 128], F32) for i in range(2)]
    qb = [C.sb(f"qb{i}", [128, 4, 128], BF16) for i in range(3)]
    rt = [C.sb(f"rt{i}", [128, 4, 4, 16], F32) for i in range(2)]
    ss = [C.sb(f"ss{i}", [128, 4], F32) for i in range(2)]
    qTs = [C.sb(f"qTs{i}", [128, 4, TOK], BF16) for i in range(2)]
    vs = [t[:].rearrange("p h (m c) -> p (h m) c", c=512) for t in qTs]

    def load_chunk(c):
        wbs, wtok = [], ()
        for q4 in range(4):
            (wb_,), wt_ = load_w(C, [wq_d[:, q4 * 4:(q4 + 1) * 4, c * 512:(c + 1) * 512]])
            wbs.append(wb_)
            wtok = wtok + wt_
        return wbs, wtok

    nxt_w = load_chunk(0)
    for c in range(12):
        wbs, wtok = nxt_w
        if c + 1 < 12:
            nxt_w = load_chunk(c + 1)
        kind = c // 4
        cc = c % 4
        so = C.nxt('qTs', 2)
        deferred = []
        for m in range(NB):
            a = C.nxt('acc', 4)
            for kc in range(16):
                C.mm(acc[a][:], C.hT[:, kc, m * 128:(m + 1) * 128], wbs[kc // 4][:, kc % 4, :], kc == 0, kc == 15,
                     (('hT', m),) + wtok, (('acc', a),))
            if kind == 2:
                C.cp(vs[so][:, m, :], acc[a][:], (('acc', a),), (('qTs', so, m),), eng='scalar')
                continue
            i = C.nxt('qn', 2)
            ib = C.nxt('qbr', 3)
            av = acc[a][:].rearrange("p (h d) -> p h d", d=128)
            if cc < 3:
                C.act(sq[:], acc[a][:], AF.Square, (('acc', a),), ('sq',))
                C.P.op('vector', lambda e, o=ss[i][:], s=sq[:].rearrange("p (h d) -> p h d", d=128):
                       e.tensor_reduce(out=o, in_=s, axis=AX.X, op=ALU.add), ('sq',), (('ss', i),))
                rstd_from_ss(C, ss[i], 4, 1.0 / 128, ('ss', i))
                for h in range(4):
                    C.stt(qn[i][:, h, :], av[:, h, :], ss[i][:, h:h + 1],
                          G6[:, (kind if cc * 4 + h < 6 else 2 + kind), :], ALU.mult, ALU.mult,
                          (('acc', a), ('ss', i), 'G6'), (('qn', i),))
                x1 = qn[i][:, :, 0:16]
                x2 = qn[i][:, :, 16:32]
                sn = C.CS[:, 0, m]
                cs = C.CS[:, 1, m]
                R = rt[i]
                C.tt(R[:, 0], x1, cs, ALU.mult, (('qn', i), 'CS'), (('rt', i),))
                C.tt(R[:, 1], x2, sn, ALU.mult, (('qn', i), 'CS'), (('rt', i),))
                C.tt(R[:, 2], x2, cs, ALU.mult, (('qn', i), 'CS'), (('rt', i),))
                C.tt(R[:, 3], x1, sn, ALU.mult, (('qn', i), 'CS'), (('rt', i),))
                C.tt(qb[ib][:, :, 0:16], R[:, 0], R[:, 1], ALU.subtract, (('rt', i),), (('qb', ib),))
                C.tt(qb[ib][:, :, 16:32], R[:, 2], R[:, 3], ALU.add, (('rt', i),), (('qb', ib),))
                C.cp(qb[ib][:, :, 32:128], qn[i][:, :, 32:128], (('qn', i),), (('qb', ib),), eng='scalar')
            else:
                C.cp(qb[ib][:], av, (('acc', a),), (('qb', ib),), eng='scalar')
            def do_tr(ib=ib, m=m, so=so):
                j = C.nxt('tp', 2)
                for h in range(4):
                    C.tr(C.tp[j][:, h, :], qb[ib][:, h, :], C.ident[:], (('qb', ib), 'ident'), (('tp', j),))
                C.cp(qTs[so][:, :, m * 128:(m + 1) * 128], C.tp[j][:], (('tp', j),), (('qTs', so, m),), eng='vector')
            if deferred:
                deferred.pop()()
            deferred.append(do_tr)
        if deferred:
            deferred.pop()()
        if 'nostore' in ccm or ('nok' in ccm and kind == 1) or ('nov' in ccm and kind == 2) or ('noq' in ccm and kind == 0):
            continue
        if kind == 2:
            for m in range(NB):
                C.dma(v_o[m * 128:(m + 1) * 128, cc * 512:(cc + 1) * 512], vs[so][:, m, :], (('qTs', so, m),),
                      (('VL', m, cc),), f'os{so}')
        elif kind == 0:
            C.dma(qT_o[cc * 4:(cc + 1) * 4].rearrange("h d t -> d h t"), qTs[so][:],
                  tuple(('qTs', so, m) for m in range(NB)), (('QL', cc),), f'os{so}')
        else:
            for mq in range(4):
                C.dma(C.KL[mq, cc * 4:(cc + 1) * 4].rearrange("h d t -> d h t"), qTs[so][:, :, mq * 256:(mq + 1) * 256],
                      (('qTs', so, 2 * mq), ('qTs', so, 2 * mq + 1)), (('KL', cc, mq),), f'os{so}')
    KG2 = C.KG.rearrange("b h d t -> (b h d) t")
    VG2 = C.VG.rearrange("b t c -> (b t) c")
    for m in range(4):
        if 'cck' in ccm or ccm == 'kv':
          allgather(C, C.KL[m].rearrange("h d t -> (h d) t"), KG2[(NPADB + 4 * m) * 2048:(NPADB + 4 * m + 4) * 2048, :],
                  tuple(('KL', cc, m) for cc in range(4)), (('KG', m),))
        if 'ccv' in ccm or ccm == 'kv':
          allgather(C, C.VL[m * 256:(m + 1) * 256, :], VG2[(NPADB + 4 * m) * 256:(NPADB + 4 * m + 4) * 256, :],
                  tuple(('VL', mm_, cc) for mm_ in (2 * m, 2 * m + 1) for cc in range(4)), (('VG', m),))


def own_tokens(j):
    return np.concatenate([np.arange((4 * m + j) * 256, (4 * m + j + 1) * 256) for m in range(4)])


def bc128(v):
    v = np.asarray(v, np.float32).reshape(1, -1)
    return np.ascontiguousarray(np.broadcast_to(v, (128, v.shape[1])))


def consts():
    ident = np.eye(128, dtype=np.float32).astype(NPBF)
    invf = (500000.0 ** (-np.arange(0, 32, 2, dtype=np.float32) / 32)).astype(np.float32)
    return dict(ident=ident, invf=bc128(invf))


def gain4(qk_gain_l):
    return np.ascontiguousarray(np.broadcast_to(np.asarray(qk_gain_l, np.float32)[None], (128, 4, 128)))


def gain6(qk_gain_l):
    out = np.zeros((6, 512), np.float32)
    for kind in range(2):
        for cc in range(3):
            for h in range(4):
                head = cc * 4 + h
                row = kind if head < 6 else 2 + kind
                out[kind * 3 + cc, h * 128:(h + 1) * 128] = qk_gain_l[row]
    return np.ascontiguousarray(np.broadcast_to(out[None], (128, 6, 512)))


def run(nc, in_maps):
    res = run_bass_kernel_spmd(nc, in_maps, core_ids=list(range(NCORE)))
    return res.results


NKB = 32
NKB2 = 16
NPADB = 3
DEPTH = 2
DIL = ((128, 1), (512, 4), (2048, 16))


def amask_index(g, dl):
    nd = DIL[g][0] // 128
    if g == 0:
        return dl
    base = 2 + 3 * (g - 1)
    return base + (0 if dl == 0 else (2 if dl == nd else 1))


def phase_attn(C, l):
    nc = C.nc
    am_d = C.dram_in("amask", [128, 8, 128], BF16)
    tri_d = C.dram_in("tri", [128, 2, 128], BF16)
    neg_d = C.dram_in("neglt", [128, 128], F32)
    padv_d = C.dram_in("padv", [128, 6, 128], BF16)
    gm_d = C.dram_in("bsel", [128, 3, 4, NKB2], F32)
    e19_d = C.dram_in("e19", [NKB2, NKB2 * 128], BF16)
    u_d = C.dram_in("uones", [128, 2, 128], F32)
    idf_d = C.dram_in("identf", [128, 128], F32)
    oT_o = C.OL
    jv = C.jv
    kg_tok = ('KW',)
    vg_tok = ('VW',)
    f2 = lambda ap, pat: ap.rearrange(pat).rearrange("(r c) -> r c", c=16384)
    C.dma(f2(C.KW, "b h d t -> (b h d t)"), f2(C.KG[bass.ds(jv, 16)], "b h d t -> (b h d t)"),
          tuple(('KG', m) for m in range(4)) + ('KGpad',), ('KW',), 'kw')
    C.dma(f2(C.VW, "b t c -> (b t c)"), f2(C.VG[bass.ds(jv, 16)], "b t c -> (b t c)"),
          tuple(('VG', m) for m in range(4)) + ('VGpad',), ('VW',), 'kw')
    ql_tok = tuple(('QL', cc) for cc in range(4))

    def cload(name, d, shape, dt):
        t = C.sb(name, shape, dt)
        C.dma(t[:], d, (), (name,), 'c0')
        return t
    AM = cload("AM", am_d, [128, 8, 128], BF16)
    TRI = cload("TRI", tri_d, [128, 2, 128], BF16)
    NEGLT = cload("NEGLT", neg_d, [128, 128], F32)
    PADV = cload("PADV", padv_d, [128, 6, 128], BF16)
    BS = cload("BS", gm_d, [128, 3, 4, NKB2], F32)
    E19 = cload("E19", e19_d, [NKB2, NKB2 * 128], BF16)
    UO = cload("UO", u_d, [128, 2, 128], F32)
    IDF = cload("IDF", idf_d, [128, 128], F32)
    ONESB = C.sb("ONESB", [128, 128], BF16)
    C.memset(ONESB[:], 1.0, ('ONESB',))
    ctoks = ('AM', 'TRI', 'NEGLT', 'PADV', 'BS', 'E19', 'UO', 'IDF', 'ONESB')

    kTb = [C.sb(f"kTb{i}", [128, NKB * 128], BF16) for i in range(2)]
    vb = [C.sb(f"vb{i}", [128, NKB, 128], BF16) for i in range(2)]
    qb = [C.sb(f"qTb{i}", [128, TOK], BF16) for i in range(2)]
    ost = [C.sb(f"ost{i}", [128, TOK], BF16) for i in range(2)]
    PB = [C.ps(f"pb{i}", [128, 512], F32) for i in range(8)]
    Z = [PB[i][:, 0:128] for i in range(3)]
    ACCO = [PB[3][:, 0:128], PB[4][:, 0:128]]
    AFT = [PB[5][:, 0:128], PB[6][:, 0:128]]
    ACCS = AFT
    GATE = [PB[7][:, 0:NKB2]] * 2
    SBT = [PB[7][0:NKB2, 128:256]] * 2
    NT = 4
    tmp = {nm: [C.sb(f"{nm}{i}", [128, 128], F32) for i in range(NT)] for nm in ('et', 'spt', 'lmt', 't1', 't2')}
    at = [C.sb(f"at{i}", [128, 128], BF16) for i in range(NT)]
    Rt = [C.sb(f"Rt{i}", [128, 128], F32) for i in range(2)]
    AO = C.sb("AO", [128, NB, 128], F32)
    AS = C.sb("AS", [128, NB, 128], F32)
    rs = [C.sb(f"rs{i}", [128, 128], F32) for i in range(2)]
    kmf = C.sb("kmf", [128, NKB2], F32)
    kmb = C.sb("kmb", [128, NKB2], BF16)
    kml = C.sb("kml", [128, NKB2], BF16)
    gsb = [C.sb(f"gsb{i}", [128, NKB2], F32) for i in range(2)]
    top8 = [C.sb(f"top8{i}", [128, 8], F32) for i in range(2)]
    sbTb = [C.sb(f"sbTb{i}", [NKB2, 128], BF16) for i in range(NB)]

    def run_pipeline(jobs, stages):
        n, S = len(jobs), len(stages)
        for t in range(n + S - 1):
            for st_ in range(S - 1, -1, -1):
                k = t - st_
                if 0 <= k < n:
                    stages[st_](jobs[k], k)

    head_order = [12, 13, 14, 15] + [2 * g + sl for sl in range(2) for g in range(3)] + [6, 7, 8, 9, 10, 11]
    loaded = {}

    def load_head(h):
        if h not in loaded:
            loaded[h] = _load_head(h)
        k = head_order.index(h)
        if k + 1 < len(head_order) and head_order[k + 1] not in loaded:
            loaded[head_order[k + 1]] = _load_head(head_order[k + 1])
        return loaded[h]

    def _load_head(h):
        i = C.nxt('kv', 2)
        kdst = kTb[i][:].rearrange("d (n t) -> d n t", t=256)
        for q4 in range(4):
            C.dma(kdst[:, q4 * 4:(q4 + 1) * 4, :],
                  C.KW[q4 * 4:(q4 + 1) * 4, h].rearrange("n d t -> d n t"), kg_tok, (('kT', i),), f'kv{i}')
            C.dma(vb[i][:, q4 * 8:(q4 + 1) * 8, :],
                  C.VW[q4 * 4:(q4 + 1) * 4, :, h * 128:(h + 1) * 128].rearrange("n (hf s) d -> s (n hf) d", hf=2),
                  vg_tok, (('v', i),), f'kv{i}')
        C.dma(qb[i][:], C.QL[h], ql_tok, (('q', i),), f'kv{i}')
        return i

    def store_head(slot, oi):
        C.dma(oT_o[slot], ost[oi][:], (('ost', oi),), (('OL', slot),), f'oo{oi}')

    def kq(lm):
        return 2 * (4 * (lm // 2) + 3) + (lm % 2)

    for hc in range(4):
        hi = load_head(12 + hc)
        oi = C.nxt('ost', 2)
        kT, v, q = kTb[hi], vb[hi], qb[hi]
        hk = (('kT', hi), ('q', hi))
        jobs = [dict(lm=lm, kb=kb, first=(kb == kq(lm)), last=(kb == 0)) for lm in range(NB)
                for kb in range(kq(lm), -1, -1)]

        def c_s0(j, k):
            zi, ti = k % 3, k % NT
            C.mm(Z[zi], kT[:, j['kb'] * 128:(j['kb'] + 1) * 128], q[:, j['lm'] * 128:(j['lm'] + 1) * 128], True, True,
                 hk, (('z', zi),))
            C.act(tmp['et'][ti][:], Z[zi], AF.Exp, (('z', zi),), (('et', ti),), scale=SCALE)

        def c_s0b(j, k):
            zi, ti = k % 3, k % NT
            C.act(tmp['spt'][ti][:], tmp['et'][ti][:], AF.Ln, (('et', ti),), (('spt', ti),), bias=1.0, scale=1.0)
            if j['first']:
                C.tt(tmp['lmt'][ti][:], tmp['spt'][ti][:], TRI[:, 1, :], ALU.mult, (('spt', ti), 'TRI'),
                     (('lmt', ti),), eng='gpsimd')
            C.stt(tmp['t1'][ti][:], Z[zi], SCALE, tmp['spt'][ti][:], ALU.mult, ALU.subtract, (('z', zi), ('spt', ti)),
                  (('t1', ti),))

        def c_s1(j, k):
            zi, ti, ai = k % 3, k % NT, k % 2
            first, last = j['first'], j['last']
            L, Ltok = (tmp['lmt'][ti], ('lmt', ti)) if first else (tmp['spt'][ti], ('spt', ti))
            rp, rn = (k + 1) % 2, k % 2
            C.mm(AFT[ai], UO[:, 0, :], L[:], True, first, (Ltok, 'UO'), (('ps56', ai),))
            if not first:
                C.mm(AFT[ai], UO[:, 1, :], Rt[rp][:], False, True, (('Rt', rp), 'UO'), (('ps56', ai),))
            if not last:
                if first:
                    C.cp(Rt[rn][:], L[:], (Ltok,), (('Rt', rn),), eng='gpsimd')
                else:
                    C.tt(Rt[rn][:], Rt[rp][:], L[:], ALU.add, (('Rt', rp), Ltok), (('Rt', rn),), eng='gpsimd')
            C.tt(tmp['t2'][ti][:], tmp['t1'][ti][:], AFT[ai], ALU.subtract, (('t1', ti), ('ps56', ai)), (('t2', ti),))
            if first:
                C.tt(tmp['t2'][ti][:], tmp['t2'][ti][:], NEGLT[:], ALU.add, (('t2', ti), 'NEGLT'), (('t2', ti),))
            C.act(at[ti][:], tmp['t2'][ti][:], AF.Exp, (('t2', ti),), (('at', ti),))

        def c_s2(j, k, oi=oi, v=v, hi=hi):
            ti, ao = k % NT, j['lm'] % 2
            C.mm(ACCO[ao], v[:, j['kb'], :], at[ti][:], j['first'], j['last'], (('v', hi), ('at', ti)), (('acco', ao),))
            if j['last']:
                C.cp(ost[oi][:, j['lm'] * 128:(j['lm'] + 1) * 128], ACCO[ao], (('acco', ao),), (('ost', oi),),
                     eng='scalar')

        run_pipeline(jobs, [c_s0, c_s0b, c_s1, c_s2])
        store_head(8 + hc, oi)

    for slot in range(2):
        oi = C.nxt('ost', 2)
        for g in range(3):
            hi = load_head(2 * g + slot)
            kT, v, q = kTb[hi], vb[hi], qb[hi]
            hk = (('kT', hi), ('q', hi))
            nd = DIL[g][0] // 128
            jobs = []
            for lm in range(NB):
                kbs = [kq(lm) - dl for dl in range(nd + 1) if kq(lm) - dl >= 0]
                for kb in kbs:
                    jobs.append(dict(lm=lm, kb=kb, first=(kb == kbs[0]), last=(kb == kbs[-1]), dl=kq(lm) - kb))

            def a_s0(j, k, g=g):
                zi, ti = k % 3, k % NT
                C.mm(Z[zi], kT[:, j['kb'] * 128:(j['kb'] + 1) * 128], q[:, j['lm'] * 128:(j['lm'] + 1) * 128], True,
                     True, hk, (('z', zi),))
                C.act(tmp['et'][ti][:], Z[zi], AF.Exp, (('z', zi),), (('et', ti),), scale=SCALE)
                C.tt(at[ti][:], tmp['et'][ti][:], AM[:, amask_index(g, j['dl']), :], ALU.mult, (('et', ti), 'AM'),
                     (('at', ti),), eng='gpsimd' if (k % 2) else 'vector')

            def a_s1(j, k, g=g):
                ti, ao = k % NT, j['lm'] % 2
                kb, lm = j['kb'], j['lm']
                C.mm(ACCO[ao], v[:, kb, :], at[ti][:], j['first'], j['last'], (('v', hi), ('at', ti)), (('acco', ao),))
                C.mm(ACCS[ao], PADV[:, kb, :] if kb < 6 else ONESB[:], at[ti][:], j['first'], j['last'],
                     (('at', ti), 'PADV', 'ONESB'), (('ps56', ao),))
                if j['last']:
                    if g == 0:
                        C.cp(AO[:, lm, :], ACCO[ao], (('acco', ao),), (('AO', lm),), eng='scalar')
                        C.cp(AS[:, lm, :], ACCS[ao], (('ps56', ao),), (('AS', lm),), eng='scalar')
                    else:
                        C.tt(AO[:, lm, :], AO[:, lm, :], ACCO[ao], ALU.add, (('AO', lm), ('acco', ao)), (('AO', lm),))
                        C.tt(AS[:, lm, :], AS[:, lm, :], ACCS[ao], ALU.add, (('AS', lm), ('ps56', ao)), (('AS', lm),))

            run_pipeline(jobs, [a_s0, a_s1])
        for lm in range(NB):
            C.recip(AS[:, lm, :], AS[:, lm, :], (('AS', lm),), (('AS', lm),))
            C.tt(ost[oi][:, lm * 128:(lm + 1) * 128], AO[:, lm, :], AS[:, lm, :], ALU.mult, (('AO', lm), ('AS', lm)),
                 (('ost', oi),))
        store_head(slot, oi)

    for hb_ in range(6):
        hi = load_head(6 + hb_)
        oi = C.nxt('ost', 2)
        kT, v, q = kTb[hi], vb[hi], qb[hi]
        hk = (('kT', hi), ('q', hi))
        C.P.op('vector', lambda e, o=kmf[:], s=kT[:].rearrange("p (n k) -> p n k", k=256):
               e.tensor_reduce(out=o, in_=s, axis=AX.X, op=ALU.add), (('kT', hi),), ('kmf',))
        C.ts(kmf[:], kmf[:], 1.0 / 256, None, ALU.mult, None, ('kmf',), ('kmf',))
        C.cp(kmb[:], kmf[:], ('kmf',), ('kmb',))
        C.tt(kml[:], kmf[:], kmb[:], ALU.subtract, ('kmf', 'kmb'), ('kml',))
        for lm in range(NB):
            mp = lm // 2
            qs = q[:, lm * 128:(lm + 1) * 128]
            gi = lm % 2
            G = gsb[gi]
            C.mm(GATE[gi], qs, kmb[:], True, False, (('q', hi), 'kmb'), ('pb7',))
            C.mm(GATE[gi], qs, kml[:], False, True, (('q', hi), 'kml'), ('pb7',))
            C.tt(G[:], GATE[gi], BS[:, 0, mp, :], ALU.add, ('pb7', 'BS'), (('gsb', gi),))
            C.P.op('vector', lambda e, o=top8[gi][:], s=G[:]: e.max(out=o, in_=s), (('gsb', gi),), (('top8', gi),))
            C.ts(G[:], G[:], top8[gi][:, 2:3], None, ALU.is_ge, None, (('gsb', gi), ('top8', gi)), (('gsb', gi),))
            C.tt(G[:], G[:], BS[:, 1, mp, :], ALU.mult, (('gsb', gi), 'BS'), (('gsb', gi),))
            C.tt(G[:], G[:], BS[:, 2, mp, :], ALU.add, (('gsb', gi), 'BS'), (('gsb', gi),))
            C.tr(SBT[gi], G[:], IDF[:], (('gsb', gi), 'IDF'), ('pb7',))
            C.cp(sbTb[lm][:], SBT[gi], ('pb7',), (('sbTb', lm),), eng='scalar')
        jobs = [dict(lm=lm, kb=kb, first=(kb == 0), last=(kb == kq(lm))) for lm in range(NB)
                for kb in range(0, kq(lm) + 1)]

        def b_s0(j, k):
            zi, ti = k % 3, k % NT
            kb, lm = j['kb'], j['lm']
            n2 = kb // 2
            C.mm(Z[zi], kT[:, kb * 128:(kb + 1) * 128], q[:, lm * 128:(lm + 1) * 128], True, False, hk, (('z', zi),))
            C.mm(Z[zi], E19[:, n2 * 128:(n2 + 1) * 128], sbTb[lm][:], False, True, (('sbTb', lm), 'E19'), (('z', zi),))
            if j['last']:
                C.act(tmp['et'][ti][:], Z[zi], AF.Exp, (('z', zi),), (('et', ti),), scale=SCALE)
                C.tt(at[ti][:], tmp['et'][ti][:], TRI[:, 0, :], ALU.mult, (('et', ti), 'TRI'), (('at', ti),))
            else:
                C.act(at[ti][:], Z[zi], AF.Exp, (('z', zi),), (('at', ti),), scale=SCALE)

        def b_s1(j, k, oi=oi):
            ti, ao = k % NT, j['lm'] % 2
            kb, lm = j['kb'], j['lm']
            C.mm(ACCO[ao], v[:, kb, :], at[ti][:], j['first'], j['last'], (('v', hi), ('at', ti)), (('acco', ao),))
            C.mm(ACCS[ao], ONESB[:], at[ti][:], j['first'], j['last'], (('at', ti), 'ONESB'), (('ps56', ao),))
            if j['last']:
                C.recip(rs[ao][:], ACCS[ao], (('ps56', ao),), (('rs', ao),))
                C.tt(ost[oi][:, lm * 128:(lm + 1) * 128], ACCO[ao], rs[ao][:], ALU.mult, (('acco', ao), ('rs', ao)),
                     (('ost', oi),))

        run_pipeline(jobs, [b_s0, b_s1])
        store_head(2 + hb_, oi)


def attn_consts(j):
    s = np.arange(128)[:, None]
    t = np.arange(128)[None, :]
    am = np.zeros((8, 128, 128), np.float32)
    for g, (w, dil) in enumerate(DIL):
        nd = w // 128
        for dl in range(nd + 1):
            dist = t - s + 128 * dl
            am[amask_index(g, dl)] = ((dist >= 0) & (dist <= w) & (dist % dil == 0))
    tri = np.stack([(s <= t), (s < t)]).astype(np.float32)
    neglt = np.where(s < t, 0.0, NEG).astype(np.float32)
    npad = 2 * (3 - j)
    padv = np.zeros((6, 128, 128), np.float32)
    padv[npad:] = 1.0
    bsel = np.zeros((3, 4, NKB2), np.float32)
    for mp in range(4):
        own = 4 * mp + 3
        for n in range(NKB2):
            valid_past = (n >= 3 - j) and (n < own)
            bsel[0, mp, n] = 0.0 if valid_past else -1e30
            bsel[1, mp, n] = -NEG if valid_past else 0.0
            bsel[2, mp, n] = 0.0 if n == own else NEG
    e19 = np.zeros((NKB2, NKB2, 128), np.float32)
    for n in range(NKB2):
        e19[n, n, :] = 1.0
    uo = np.stack([(s > t).astype(np.float32), np.ones((128, 128), np.float32)])
    return dict(amask=np.ascontiguousarray(am.transpose(1, 0, 2)).astype(NPBF),
                tri=np.ascontiguousarray(tri.transpose(1, 0, 2)).astype(NPBF),
                neglt=neglt,
                padv=np.ascontiguousarray(padv.transpose(1, 0, 2)).astype(NPBF),
                bsel=np.ascontiguousarray(np.broadcast_to(bsel[None], (128, 3, 4, NKB2))),
                e19=e19.reshape(NKB2, NKB2 * 128).astype(NPBF),
                uones=np.ascontiguousarray(uo.transpose(1, 0, 2)),
                identf=np.eye(128, dtype=np.float32))


def head_norm(C, accap, acctok, gain_ap, gain_tok, out_bf, out_tok, sq, ss, sstok):
    av = accap.rearrange("p (h d) -> p h d", d=128)
    C.act(sq[:], accap, AF.Square, (acctok,), ('sq',))
    C.P.op('vector', lambda e, o=ss[:], s_=sq[:].rearrange("p (h d) -> p h d", d=128):
           e.tensor_reduce(out=o, in_=s_, axis=AX.X, op=ALU.add), ('sq',), (sstok,))
    rstd_from_ss(C, ss, 4, 1.0 / 128, sstok)
    for h in range(4):
        C.stt(out_bf[:, h, :], av[:, h, :], ss[:, h:h + 1], gain_ap, ALU.mult, ALU.mult,
              (acctok, sstok, gain_tok), (out_tok,))


def phase_post(C, l):
    r3 = lambda ap: ap.rearrange("(kc p) n -> p kc n", p=128)
    wg_d = r3(C.dram_in("w_gate", [DEPTH, D, 3 * D], F32)[l])
    wbr_d = [r3(C.dram_in("w_br_a", [DEPTH, 256, D], F32)[l]), r3(C.dram_in("w_br_b", [DEPTH, 768, D], F32)[l]),
             r3(C.dram_in("w_br_c", [DEPTH, 512, D], F32)[l])]
    wo_d = r3(C.dram_in("w_o", [DEPTH, D, D], F32)[l])
    bg_d = C.dram_in("b_gate_p", [DEPTH, 128, 48], F32)[l]
    ln_d = C.dram_in("ln_mix", [DEPTH, 128, D], F32)[l]
    lnq_d = C.dram_in("ln_mem_q", [DEPTH, 128, D], F32)[l]
    lnkv_d = C.dram_in("ln_mem_kv", [DEPTH, 128, D], F32)[l]
    wmq_d = r3(C.dram_in("wm_q", [DEPTH, D, 512], F32)[l])
    wmkv_d = r3(C.dram_in("wm_kv", [DEPTH, D, 1024], F32)[l])
    wmo_d = r3(C.dram_in("wm_o", [DEPTH, 512, D], F32)[l])
    mg_d = C.dram_in("mem_gain", [DEPTH, 128, 2, 128], F32)[l]
    mem_d = C.dram_in("mem", [256, D], F32)
    oT_d = C.OL

    lnw = C.sb("lnw", [128, D], F32)
    C.dma(lnw[:], ln_d, (), ('lnw',), 'c0')
    BG = C.sb("BG", [128, 48], F32)
    C.dma(BG[:], bg_d, (), ('BG',), 'c0')
    MG = C.sb("MG", [128, 2, 128], F32)
    C.dma(MG[:], mg_d, (), ('MG',), 'c0')
    oT = C.sb("oTs", [128, 12, TOK], BF16)
    C.dma(oT[:], oT_d.rearrange("s d t -> d s t"), tuple(('OL', sl) for sl in range(12)), ('oT',), 'c1')
    ONESB = C.sb("ONESB", [128, 128], BF16)
    C.memset(ONESB[:], 1.0, ('ONESB',))

    make_hT(C, lnw[:], 'lnw')
    hT_all = tuple(('hT', m) for m in range(NB))

    PSB = [C.ps(f"psb{i}", [128, 512], F32) for i in range(6)]
    mT = [C.sb(f"mT{i}", [128, 4, TOK], BF16) for i in range(1)]
    gs = [C.sb(f"gs{i}", [128, 512], F32) for i in range(2)]
    tq = [C.sb(f"tq{i}", [128, 512], F32) for i in range(2)]
    macc = [C.sb(f"macc{i}", [128, 512], F32) for i in range(2)]
    brslot = (0, 2, 8)
    brk = (2, 6, 4)

    for grp in range(4):
        mi = 0
        for ncl in range(4):
            ncg = grp * 4 + ncl
            for br in range(3):
                (wg,), wgt = load_w(C, [wg_d[:, :, br * D + ncg * 128: br * D + (ncg + 1) * 128]])
                (wb,), wbt = load_w(C, [wbr_d[br][:, :, ncg * 128:(ncg + 1) * 128]])
                for tg in range(2):
                    pg = C.nxt('psg', 2)
                    pp = 2 + C.nxt('psp', 2)
                    tsl = slice(tg * 512, (tg + 1) * 512)
                    for kc in range(16):
                        C.mm(PSB[pg][:], wg[:, kc, :], C.hT[:, kc, tsl], kc == 0, kc == 15, hT_all + wgt,
                             (('psb', pg),))
                    for kc in range(brk[br]):
                        C.mm(PSB[pp][:], wb[:, kc, :], oT[:, brslot[br] + kc, tsl], kc == 0, kc == brk[br] - 1,
                             ('oT',) + wbt, (('psb', pp),))
                    gi = C.nxt('gs', 2)
                    C.act(gs[gi][:], PSB[pg][:], AF.Sigmoid, (('psb', pg), 'BG'), (('gs', gi),),
                          bias=BG[:, br * 16 + ncg: br * 16 + ncg + 1], scale=1.0)
                    if br == 0:
                        C.tt(macc[tg][:], gs[gi][:], PSB[pp][:], ALU.mult, (('gs', gi), ('psb', pp)), (('macc', tg),))
                    else:
                        C.tt(tq[gi][:], gs[gi][:], PSB[pp][:], ALU.mult, (('gs', gi), ('psb', pp)), (('tq', gi),))
                        if br == 1:
                            C.tt(macc[tg][:], macc[tg][:], tq[gi][:], ALU.add, (('macc', tg), ('tq', gi)),
                                 (('macc', tg),), eng='gpsimd')
                        else:
                            C.tt(mT[mi][:, ncl, tsl], macc[tg][:], tq[gi][:], ALU.add, (('macc', tg), ('tq', gi)),
                                 (('mT', mi),), eng='gpsimd')
        for n in range(4):
            (wo,), wot = load_w(C, [wo_d[:, grp * 4:(grp + 1) * 4, n * 512:(n + 1) * 512]])
            for m in range(NB):
                pa = 4 + C.nxt('psa', 2)
                for kc in range(4):
                    C.mm(PSB[pa][:], mT[mi][:, kc, m * 128:(m + 1) * 128], wo[:, kc, :], kc == 0, kc == 3,
                         (('mT', mi),) + wot, (('psb', pa),))
                xs = C.X[:, m, n * 512:(n + 1) * 512]
                C.tt(xs, xs, PSB[pa][:], ALU.add, (('X', m), ('psb', pa)), (('X', m),))

    lnq = lnw
    C.dma(lnq[:], lnq_d, (), ('lnw',), 'c0')
    make_hT(C, lnq[:], 'lnw')
    lnkv = lnw
    C.dma(lnkv[:], lnkv_d, (), ('lnw',), 'c0')
    C.P.op('vector', lambda e: e.memset(ONESB[:, 0:1], 1.0), ('ONESB',), ('oT', 'oTfree'))
    oTf = oT[:].rearrange("p s t -> p (s t)")
    memT = oT[:].rearrange("p s t -> p (s t)")[:, 0:4096].rearrange("p (k t) -> p k t", t=256)
    memx = mT[0][:].rearrange("p a t -> p (a t)").bitcast(F32)
    for mb in range(2):
        C.dma(memx, mem_d[mb * 128:(mb + 1) * 128, :], (), (('mT', 0),), 'c1')
        i = C.nxt('hb', C.nhb)
        st = C.nst[i]
        C.act(C.hb[i][:], memx, AF.Square, (('mT', 0),), (('hb', i), ('nst', i)), accum_out=st[:, 0:1])
        rstd_from_ss(C, st, 1, 1.0 / D, ('nst', i))
        C.stt(C.hb[i][:], memx, st[:, 0:1], lnkv[:], ALU.mult, ALU.mult, (('mT', 0), ('nst', i), 'lnw'), (('hb', i),))
        for g in range(4):
            j = C.nxt('tp', 2)
            for a in range(4):
                kc = g * 4 + a
                C.tr(C.tp[j][:, a, :], C.hb[i][:, kc * 128:(kc + 1) * 128], C.ident[:], (('hb', i), 'ident'),
                     (('tp', j),))
            C.cp(memT[:, g * 4:(g + 1) * 4, mb * 128:(mb + 1) * 128], C.tp[j][:], (('tp', j), 'oTfree'), ('memT',),
                 eng='scalar' if g % 2 == 0 else 'vector')
    memT = oTf[:, 0:4096].rearrange("p (k t) -> p k t", t=256)
    sq = C.sb("sq", [128, 512], F32)
    ss = [C.sb(f"ss{i}", [128, 4], F32) for i in range(2)]
    nb16 = [C.sb(f"nb16_{i}", [128, 4, 128], BF16) for i in range(2)]
    kmT = C.sb("kmT", [128, 4, 256], BF16)
    vm = C.sb("vm", [128, 2, 512], BF16)
    qmT = oTf[:, 4096:8192].rearrange("p (h t) -> p h t", t=TOK)
    omT = oTf[:, 8192:12288].rearrange("p (h t) -> p h t", t=TOK)
    for c in range(2):
        ws, wt = [], ()
        for q4 in range(4):
            (w_,), t_ = load_w(C, [wmkv_d[:, q4 * 4:(q4 + 1) * 4, c * 512:(c + 1) * 512]])
            ws.append(w_)
            wt = wt + t_
        for mb in range(2):
            pa = 4 + C.nxt('psa', 2)
            for kc in range(16):
                C.mm(PSB[pa][:], memT[:, kc, mb * 128:(mb + 1) * 128], ws[kc // 4][:, kc % 4, :], kc == 0, kc == 15,
                     ('memT',) + wt, (('psb', pa),))
            if c == 1:
                C.cp(vm[:, mb, :], PSB[pa][:], (('psb', pa),), ('vm',), eng='scalar')
            else:
                i = C.nxt('nb16', 2)
                head_norm(C, PSB[pa][:], ('psb', pa), MG[:, 1, :], 'MG', nb16[i], ('nb16', i), sq, ss[i], ('ss', i))
                j = C.nxt('tp', 2)
                for h in range(4):
                    C.tr(C.tp[j][:, h, :], nb16[i][:, h, :], C.ident[:], (('nb16', i), 'ident'), (('tp', j),))
                C.cp(kmT[:, :, mb * 128:(mb + 1) * 128], C.tp[j][:], (('tp', j),), ('kmT',))
    ws, wt = [], ()
    for q4 in range(4):
        (w_,), t_ = load_w(C, [wmq_d[:, q4 * 4:(q4 + 1) * 4, :]])
        ws.append(w_)
        wt = wt + t_
    for m in range(NB):
        pa = 4 + C.nxt('psa', 2)
        for kc in range(16):
            C.mm(PSB[pa][:], C.hT[:, kc, m * 128:(m + 1) * 128], ws[kc // 4][:, kc % 4, :], kc == 0, kc == 15,
                 (('hT', m),) + wt, (('psb', pa),))
        i = C.nxt('nb16', 2)
        head_norm(C, PSB[pa][:], ('psb', pa), MG[:, 0, :], 'MG', nb16[i], ('nb16', i), sq, ss[i], ('ss', i))
        j = C.nxt('tp', 2)
        for h in range(4):
            C.tr(C.tp[j][:, h, :], nb16[i][:, h, :], C.ident[:], (('nb16', i), 'ident'), (('tp', j),))
        C.cp(qmT[:, :, m * 128:(m + 1) * 128], C.tp[j][:], (('tp', j), 'oTfree'), ('qmT',))
    pb = [C.sb(f"pbm{i}", [128, 512], BF16) for i in range(2)]
    for h in range(4):
        for tg in range(2):
            tsl = slice(tg * 512, (tg + 1) * 512)
            for nb_ in range(2):
                pg = C.nxt('psg', 2)
                C.mm(PSB[pg][:], kmT[:, h, nb_ * 128:(nb_ + 1) * 128], qmT[:, h, tsl], True, True, ('kmT', 'qmT'),
                     (('psb', pg),))
                pi = C.nxt('pbm', 2)
                C.act(pb[pi][:], PSB[pg][:], AF.Exp, (('psb', pg),), (('pbm', pi),), scale=SCALE)
                C.mm(PSB[2][:], vm[:, nb_, h * 128:(h + 1) * 128], pb[pi][:], nb_ == 0, nb_ == 1, ('vm', ('pbm', pi)),
                     (('psb', 2),))
                C.mm(PSB[3][:], ONESB[:], pb[pi][:], nb_ == 0, nb_ == 1, ('ONESB', ('pbm', pi)), (('psb', 3),))
            gi = C.nxt('gs', 2)
            C.recip(gs[gi][:], PSB[3][:], (('psb', 3),), (('gs', gi),))
            C.tt(omT[:, h, tsl], PSB[2][:], gs[gi][:], ALU.mult, (('psb', 2), ('gs', gi), 'oTfree'), ('omT',))
    for n in range(4):
        (wo,), wot = load_w(C, [wmo_d[:, :, n * 512:(n + 1) * 512]])
        for m in range(NB):
            pa = 4 + C.nxt('psa', 2)
            for h in range(4):
                C.mm(PSB[pa][:], omT[:, h, m * 128:(m + 1) * 128], wo[:, h, :], h == 0, h == 3, ('omT',) + wot,
                     (('psb', pa),))
            xs = C.X[:, m, n * 512:(n + 1) * 512]
            C.tt(xs, xs, PSB[pa][:], ALU.add, (('X', m), ('psb', pa)), (('X', m),))


    XG2 = C.XG.rearrange("b r c -> (b r) c")
    for mp in range(4):
        C.dma(C.XHL[mp], C.X[126:128, 2 * mp + 1, :], (('X', 2 * mp + 1),), (('XHL', mp),), 'xh')
        allgather(C, C.XHL[mp], XG2[(1 + 4 * mp) * 2:(1 + 4 * mp + 4) * 2, :], (('XHL', mp),), (('XG', mp),))


NCH = DFF // 128


def phase_ffn(C, l):
    wu_d = C.dram_in("w_up", [DEPTH, D, 2 * DFF], F32)[l].rearrange("(kc p) n -> p kc n", p=128)
    wd_d = C.dram_in("w_down", [DEPTH, DFF, D], F32)[l].rearrange("(kc p) n -> p kc n", p=128)
    ln_d = C.dram_in("ln_ffn", [DEPTH, 128, D], F32)[l]
    cw_d = C.dram_in("conv_wp", [DEPTH, 128, 2 * NCH, 4], F32)[l]

    lnw = C.sb("lnw", [128, D], F32)
    C.dma(lnw[:], ln_d, (), ('lnw',), 'c0')
    CW = C.sb("CW", [128, 2 * NCH, 4], F32)
    C.dma(CW[:], cw_d, (), ('CW',), 'c0')
    xh = C.wst[0]
    C.memset(xh[:], 0.0, (('wst', 0),))
    C.dma(C.XW.rearrange("b r c -> b (r c)"), C.XG[bass.ds(C.jv, 13)].rearrange("b r c -> b (r c)"),
          tuple(('XG', m) for m in range(4)) + ('XGpad',), ('XW',), 'kw')
    for mp in range(4):
        C.dma(xh[2 * mp:2 * mp + 2, :], C.XW[4 * mp], ('XW',), (('wst', 0),), 'ws0')
    hTh = C.sb("hTh", [128, 16, 128], BF16)
    make_hT(C, lnw[:], 'lnw')
    make_hT(C, lnw[:], 'lnw', src=lambda m: (xh[:], ('wst', 0)), nblk=1, dst=hTh, dst_tok='hTh')
    hT_all = tuple(('hT', m) for m in range(NB))

    PSB = [C.ps(f"psb{i}", [128, 512], F32) for i in range(6)]
    UH = C.tp[0][:].rearrange("p a b -> p (a b)").bitcast(F32)
    ub = [[C.sb(f"ub{a}{i}", [128, 4, 258], F32) for i in range(2)] for a in range(2)]
    Y = [[C.sb(f"Y{a}{i}", [128, 4, 256], F32) for i in range(2)] for a in range(2)]
    aT = [C.sb(f"aT{i}", [128, 8, TOK], BF16) for i in range(1)]

    ngrp = (NCH + 7) // 8
    for cg in range(ngrp):
        ai = 0
        chunks = list(range(cg * 8, min(NCH, cg * 8 + 8)))
        for ci, ch in enumerate(chunks):
            bi = C.nxt('ub', 2)
            for a in range(2):
                col = a * DFF + ch * 128
                (w,), wt = load_w(C, [wu_d[:, :, col:col + 128]])
                U = ub[a][bi]
                utok = ('ub', a, bi)
                for tg in range(2):
                    pg = a * 2 + tg
                    for kc in range(16):
                        C.mm(PSB[pg][:], w[:, kc, :], C.hT[:, kc, tg * 512:(tg + 1) * 512], kc == 0, kc == 15,
                             hT_all + wt, (('psb', pg),))
                    C.cp(U[:, 2 * tg:2 * tg + 2, 2:258], PSB[pg][:].rearrange("p (m t) -> p m t", t=256),
                         (('psb', pg),), (utok,), eng='scalar')
                for kc in range(16):
                    C.mm(UH[:, a * 8:a * 8 + 8], w[:, kc, :], hTh[:, kc, 0:8], kc == 0, kc == 15, ('hTh',) + wt,
                         (('uh', a),))
                C.cp(U[:, :, 0:2], UH[:, a * 8:a * 8 + 8].rearrange("p (m r) -> p m r", r=2), (('uh', a),), (utok,))
                cwc = CW[:, a * NCH + ch, :]
                Yt = Y[a][bi]
                ytok = ('Y', a, bi)
                C.act(Yt[:], U[:, :, 2:258], AF.Identity, (utok, 'CW'), (ytok,), scale=cwc[:, 2:3], bias=cwc[:, 3:4])
                C.stt(Yt[:], U[:, :, 1:257], cwc[:, 1:2], Yt[:], ALU.mult, ALU.add, (utok, 'CW', ytok), (ytok,))
                C.stt(Yt[:], U[:, :, 0:256], cwc[:, 0:1], Yt[:], ALU.mult, ALU.add, (utok, 'CW', ytok), (ytok,))
            C.act(Y[0][bi][:], Y[0][bi][:], AF.Silu, (('Y', 0, bi),), (('Y', 0, bi),))
            C.tt(aT[ai][:, ci, :].rearrange("p (m t) -> p m t", t=256), Y[0][bi][:], Y[1][bi][:], ALU.mult,
                 (('Y', 0, bi), ('Y', 1, bi)), (('aT', ai),), eng='gpsimd')
        ng = len(chunks)
        for n in range(4):
            ws, wt = [], ()
            for q4 in range((ng + 3) // 4):
                k0 = cg * 8 + q4 * 4
                k1 = min(cg * 8 + ng, k0 + 4)
                (w_,), t_ = load_w(C, [wd_d[:, k0:k1, n * 512:(n + 1) * 512]])
                ws.append(w_)
                wt = wt + t_
            for m in range(NB):
                pa = 4 + C.nxt('psd', 2)
                for i in range(ng):
                    C.mm(PSB[pa][:], aT[ai][:, i, m * 128:(m + 1) * 128], ws[i // 4][:, i % 4, :], i == 0, i == ng - 1,
                         (('aT', ai),) + wt, (('psb', pa),))
                xs = C.X[:, m, n * 512:(n + 1) * 512]
                C.tt(xs, xs, PSB[pa][:], ALU.add, (('X', m), ('psb', pa)), (('X', m),))


import os as _os
STOP = [_os.environ.get('STOPAT')]


def build_fused():
    C = Ctx()
    load_consts(C)
    load_x(C)
    C.CS = C.sb("CS", [128, 2, NB, 4, 16], F32)
    gathered_bufs(C)
    C.jv = C.nc.partition_id() % 4
    C.phase_begin()
    rope_tables(C)
    zero_pads(C)
    C.phase_end()
    for l in range(DEPTH):
        if STOP[0] == 'init':
            break
        C.phase_begin()
        alloc_norm(C)
        alloc_wstream(C, 3, 8)
        phase_qkv(C, l)
        C.phase_end()
        if STOP[0] == 'qkv':
            break
        C.phase_begin()
        phase_attn(C, l)
        C.phase_end()
        if STOP[0] == 'attn':
            break
        C.phase_begin()
        alloc_norm(C, 1)
        alloc_wstream(C, 2, 4)
        phase_post(C, l)
        C.phase_end()
        if STOP[0] == 'post':
            break
        C.phase_begin()
        alloc_norm(C, 1)
        alloc_wstream(C, 2, 4)
        phase_ffn(C, l)
        C.phase_end()
        if STOP[0] == 'ffn':
            break
    C.P.barrier()
    store_x(C)
    NAMES[:] = [k for k, v in C.dcache.items()]
    return C.finish()


NAMES = []


_PROG = []


def kernel(x, mem, positions, ln_mix, w_qkv, qk_gain, w_br_a, w_br_b, w_br_c, w_gate, b_gate, w_o,
           ln_mem_q, ln_mem_kv, wm_q, wm_kv, wm_o, mem_qk_gain, ln_ffn, w_up, conv_w, conv_b, w_down):
    f32 = lambda a: np.ascontiguousarray(np.asarray(a, dtype=np.float32))
    x = f32(x)
    mem = f32(mem)
    positions = np.asarray(positions).astype(np.int32)
    L = DEPTH
    bcl = lambda a: np.ascontiguousarray(np.broadcast_to(f32(a)[:, None], (L, 128) + tuple(np.asarray(a).shape[1:])))
    cst = consts()
    cwp = np.concatenate([f32(conv_w).reshape(L, 3, 2 * NCH, 128).transpose(0, 3, 2, 1),
                          f32(conv_b).reshape(L, 2 * NCH, 128).transpose(0, 2, 1)[..., None]], axis=3)
    com = dict(ident=cst['ident'], invf=cst['invf'],
               w_qkv=f32(w_qkv), w_gate=f32(w_gate), w_br_a=f32(w_br_a), w_br_b=f32(w_br_b), w_br_c=f32(w_br_c),
               w_o=f32(w_o), wm_q=f32(wm_q), wm_kv=f32(wm_kv), wm_o=f32(wm_o), w_up=f32(w_up), w_down=f32(w_down),
               ln_mix=bcl(ln_mix), ln_mem_q=bcl(ln_mem_q), ln_mem_kv=bcl(ln_mem_kv), ln_ffn=bcl(ln_ffn),
               qk_gain4=bcl(qk_gain), mem_gain=bcl(mem_qk_gain),
               b_gate_p=np.ascontiguousarray(f32(b_gate).reshape(L, 48, 128).transpose(0, 2, 1)),
               conv_wp=np.ascontiguousarray(cwp))
    acst = [attn_consts(j) for j in range(4)]
    cores = list(range(NCORE))
    toks = [own_tokens(c % 4) for c in cores]
    in_maps = []
    for c in cores:
        m = dict(com)
        m.update(acst[c % 4])
        m['x_in'] = np.ascontiguousarray(x[c // 4, toks[c]])
        m['pos'] = np.ascontiguousarray(positions[c // 4, toks[c]].reshape(8, 128).T)
        m['mem'] = mem[c // 4]
        in_maps.append(m)
    if not _PROG:
        _PROG.append(build_fused())
    in_maps = [{k: v for k, v in m.items() if k in NAMES} for m in in_maps]
    res = run(_PROG[0], in_maps)
    out = np.zeros(x.shape, np.float32)
    for c in cores:
        out[c // 4, toks[c]] = np.asarray(res[c]['x_out'])
    return out
```

```python
import contextlib
import numpy as np
import ml_dtypes
import concourse.bass as bass
import concourse.mybir as mybir
from concourse.bass_utils import run_bass_kernel_spmd

F32 = mybir.dt.float32
BF16 = mybir.dt.bfloat16
I32 = mybir.dt.int32
AF = mybir.ActivationFunctionType
ALU = mybir.AluOpType
AX = mybir.AxisListType
NPBF = ml_dtypes.bfloat16

D = 2048
NCORE = 8
TOK = 1024
NB = 8
EPS = 1e-6
DFF = 5504
SCALE = 128 ** -0.5
PI = 3.14159265358979
NEG = -30000.0
WCAP = 2048


class Prog:
    CE = ('scalar', 'tensor', 'vector', 'gpsimd', 'sync')

    def __init__(self, nc):
        self.nc = nc
        self.q = {e: [] for e in self.CE}
        self.lastw = {}
        self.rd = {}
        self.dcnt = {}
        self.known = {e: {} for e in self.CE}
        self.pending = {e: [] for e in self.CE}
        self.emitted = {e: 0 for e in self.CE}
        self.cum = {e: [] for e in self.CE}
        self.esem = None
        self.dsem = {}

    def barrier(self):
        evs = []
        for e in self.CE:
            if self.q[e] and e != 'sync':
                evs.append((('e', e, len(self.q[e]) - 1), True))
        for k, n in self.dcnt.items():
            evs.append((('d', k, n), True))
        for e in self.CE:
            self.pending[e] = [ev for ev in evs if not (ev[0][0] == 'e' and ev[0][1] == e)]

    def op(self, eng, fn, r=(), w=(), dma=None, extra=()):
        q = self.q[eng]
        idx = len(q)
        cand = list(extra) + self.pending[eng]
        self.pending[eng] = []
        for t in r:
            ev = self.lastw.get(t)
            if ev is not None:
                cand.append((ev, True))
        for t in w:
            ev = self.lastw.get(t)
            if ev is not None:
                cand.append((ev, False))
            for ev in self.rd.get(t, ()):
                cand.append((ev, False))
        best = {}
        for ev, raw in cand:
            if ev[0] == 'e':
                if ev[1] == eng and (eng in ('tensor', 'sync') or not raw):
                    continue
                k = ('e', ev[1])
                val = ev[2]
            else:
                k = ('d', ev[1])
                val = self.dcnt[ev[1]]
            if val > best.get(k, -1):
                best[k] = val
        fw = []
        for k, val in best.items():
            if self.known[eng].get(k, -1) >= val:
                continue
            self.known[eng][k] = val
            fw.append((k[0], k[1], val))
        rec = dict(fn=fn, waits=fw, inc=False, dma=dma)
        q.append(rec)
        if dma is not None:
            n = self.dcnt.get(dma, 0) + 1
            self.dcnt[dma] = n
            me = ('d', dma, n)
        else:
            me = ('e', eng, idx)
        for n_, ev in enumerate(fw):
            if ev[0] == 'e':
                pe, pidx = ev[1], ev[2]
                if pidx < self.emitted[pe]:
                    while not self.q[pe][pidx]['inc']:
                        pidx += 1
                    fw[n_] = ('e', pe, pidx)
                else:
                    self.q[pe][pidx]['inc'] = True
        for t in w:
            self.lastw[t] = me
            self.rd[t] = []
        for t in r:
            self.rd.setdefault(t, []).append(me)
        return me

    def emit(self, st=None):
        nc = self.nc
        if self.esem is None:
            self.esem = {e: nc.alloc_semaphore('s_' + e) for e in self.CE}
        for k in self.dcnt:
            if k not in self.dsem:
                self.dsem[k] = nc.alloc_semaphore('d_' + k)
        esem, dsem, cum = self.esem, self.dsem, self.cum
        for e in self.CE:
            q = self.q[e]
            if len(q) > self.emitted[e]:
                q[-1]['inc'] = True
            c = cum[e][-1] if cum[e] else 0
            for rec in q[len(cum[e]):]:
                if rec['inc']:
                    c += 1
                cum[e].append(c)
        with nc.Block() as block:
            def run(e):
                start = self.emitted[e]

                def f(eng):
                    for rec in self.q[e][start:]:
                        rec['wv'] = []
                        for ev in rec['waits']:
                            if ev[0] == 'e':
                                eng.wait_ge(esem[ev[1]], cum[ev[1]][ev[2]])
                                rec['wv'].append((('e', ev[1]), cum[ev[1]][ev[2]]))
                            else:
                                eng.wait_ge(dsem[ev[1]], 16 * ev[2])
                                rec['wv'].append((('d', ev[1]), ev[2]))
                        ins = rec['fn'](eng)
                        if rec['dma'] is not None:
                            ins.then_inc(dsem[rec['dma']], 16)
                        elif rec['inc']:
                            ins.then_inc(esem[e], 1)
                return f

            block.sync(run('sync'))
            block.scalar(run('scalar'))
            block.tensor(run('tensor'))
            block.vector(run('vector'))
            block.gpsimd(run('gpsimd'))
        for e in self.CE:
            self.emitted[e] = len(self.q[e])


class Ctx:
    def __init__(self):
        self.nc = bass.Bass("TRN2", target_bir_lowering=False)
        self.P = Prog(self.nc)
        self.st = contextlib.ExitStack()
        self.rot = {}
        self.outs = []
        self.dq = 0
        self.pst = None
        self.phase_id = 0
        self.dcache = {}

    def phase_begin(self):
        self.P.barrier()
        self.pst = contextlib.ExitStack()
        self.phase_id += 1

    def phase_end(self):
        self.P.emit()
        self.pst.close()
        self.pst = None

    def dram_in(self, name, shape, dt):
        if name not in self.dcache:
            self.dcache[name] = self.nc.dram_tensor(name, list(shape), dt, kind="ExternalInput").ap()
        return self.dcache[name]

    def dram_tmp(self, name, shape, dt):
        if name not in self.dcache:
            self.dcache[name] = self.nc.dram_tensor(name, list(shape), dt).ap()
        return self.dcache[name]

    def dram_out(self, name, shape, dt):
        return self.nc.dram_tensor(name, list(shape), dt, kind="ExternalOutput").ap()

    def sb(self, name, shape, dt):
        st = self.st if self.pst is None else self.pst
        return st.enter_context(self.nc.sbuf_tensor(f"{name}_p{self.phase_id}", list(shape), dt))

    def ps(self, name, shape, dt=F32):
        st = self.st if self.pst is None else self.pst
        return st.enter_context(self.nc.psum_tensor(f"{name}_p{self.phase_id}", list(shape), dt))

    def nxt(self, name, n):
        i = self.rot.get(name, 0)
        self.rot[name] = i + 1
        return i % n

    def dma(self, out, in_, r, w, key, eng=None):
        if eng is None:
            eng = ('sync', 'gpsimd')[self.dq % 2] if False else 'sync'
        def fn(e):
            try:
                return e.dma_start(out=out, in_=in_)
            except Exception:
                print("DMA FAILED", out, in_)
                raise
        return self.P.op(eng, fn, r, w, dma=key)

    def act(self, out, in_, func, r, w, **kw):
        return self.P.op('scalar', lambda e: e.activation(out=out, in_=in_, func=func, **kw), r, w)

    def mm(self, out, lhsT, rhs, start, stop, r, w):
        return self.P.op('tensor', lambda e: e.matmul(out, lhsT=lhsT, rhs=rhs, start=start, stop=stop), r, w)

    def tr(self, out, in_, ident, r, w):
        return self.P.op('tensor', lambda e: e.transpose(out, in_, ident), r, w)

    def tt(self, out, in0, in1, op, r, w, eng='vector'):
        return self.P.op(eng, lambda e: e.tensor_tensor(out=out, in0=in0, in1=in1, op=op), r, w)

    def ts(self, out, in0, s1, s2, op0, op1, r, w, eng='vector'):
        if op1 is None:
            return self.P.op(eng, lambda e: e.tensor_scalar(out=out, in0=in0, scalar1=s1, scalar2=None, op0=op0), r, w)
        return self.P.op(eng, lambda e: e.tensor_scalar(out=out, in0=in0, scalar1=s1, scalar2=s2, op0=op0, op1=op1), r, w)

    def stt(self, out, in0, scalar, in1, op0, op1, r, w):
        return self.P.op('vector', lambda e: e.scalar_tensor_tensor(out=out, in0=in0, scalar=scalar, in1=in1,
                                                                    op0=op0, op1=op1), r, w)

    def cp(self, out, in_, r, w, eng='vector'):
        if eng == 'scalar':
            return self.P.op(eng, lambda e: e.copy(out=out, in_=in_), r, w)
        return self.P.op(eng, lambda e: e.tensor_copy(out=out, in_=in_), r, w)

    def recip(self, out, in_, r, w):
        return self.P.op('vector', lambda e: e.reciprocal(out=out, in_=in_), r, w)

    def memset(self, ap, val, w, eng='gpsimd'):
        return self.P.op(eng, lambda e: e.memset(ap, val), (), w)

    def finish(self):
        self.P.op('sync', lambda e: e.nop(), r=tuple(self.outs), w=())
        self.P.emit(self.st)
        self.st.close()
        return self.nc


def load_consts(C, need_ident=True):
    nc = C.nc
    ident_d = C.dram_in("ident", [128, 128], BF16)
    C.ident = C.sb("ident_sb", [128, 128], BF16)
    C.dma(C.ident[:], ident_d, (), ('ident',), 'c0')


def load_x(C, name="x_in"):
    xd = C.dram_in(name, [TOK, D], F32)
    C.X = C.sb("X", [128, NB, D], F32)
    for m in range(NB):
        C.dma(C.X[:, m, :], xd[m * 128:(m + 1) * 128, :], (), (('X', m),), 'xl')


def store_x(C, name="x_out"):
    xo = C.dram_out(name, [TOK, D], F32)
    for m in range(NB):
        C.dma(xo[m * 128:(m + 1) * 128, :], C.X[:, m, :], (('X', m),), (('xo', m),), 'xs')
        C.outs.append(('xo', m))


def alloc_norm(C, nhb=2):
    C.nhb = nhb
    C.hT = C.sb("hT", [128, 16, TOK], BF16)
    C.hb = [C.sb(f"hb{i}", [128, D], BF16) for i in range(nhb)]
    C.nst = [C.sb(f"nst{i}", [128, 4], F32) for i in range(2)]
    C.tp = [C.ps(f"tp{i}", [128, 4, 128], BF16) for i in range(2)]


def rstd_from_ss(C, st, n, inv_n, tok):
    C.ts(st[:, 0:n], st[:, 0:n], inv_n, EPS, ALU.mult, ALU.add, (tok,), (tok,))
    C.act(st[:, 0:n], st[:, 0:n], AF.Sqrt, (tok,), (tok,))
    C.recip(st[:, 0:n], st[:, 0:n], (tok,), (tok,))


def make_hT(C, gain_tile, gain_tok, src=None, nblk=NB, dst=None, dst_tok='hT'):
    dst = C.hT if dst is None else dst
    for m in range(nblk):
        xin, xtok = (C.X[:, m, :], ('X', m)) if src is None else src(m)
        i = C.nxt('hb', C.nhb)
        st = C.nst[i]
        C.act(C.hb[i][:], xin, AF.Square, (xtok,), (('hb', i), ('nst', i)), accum_out=st[:, 0:1])
        rstd_from_ss(C, st, 1, 1.0 / D, ('nst', i))
        C.stt(C.hb[i][:], xin, st[:, 0:1], gain_tile, ALU.mult, ALU.mult, (xtok, ('nst', i), gain_tok), (('hb', i),))
        for g in range(4):
            j = C.nxt('tp', 2)
            for a in range(4):
                kc = g * 4 + a
                C.tr(C.tp[j][:, a, :], C.hb[i][:, kc * 128:(kc + 1) * 128], C.ident[:], (('hb', i), 'ident'),
                     (('tp', j),))
            eng = 'scalar' if g % 2 == 0 else 'vector'
            C.cp(dst[:, g * 4:(g + 1) * 4, m * 128:(m + 1) * 128], C.tp[j][:], (('tp', j),), ((dst_tok, m),), eng=eng)


def alloc_wstream(C, nst=3, nbf=6):
    C.nwst, C.nwbf = nst, nbf
    C.wst = [C.sb(f"wst{i}", [128, WCAP], F32) for i in range(nst)]
    C.wbf = [C.sb(f"wbf{i}", [128, WCAP], BF16) for i in range(nbf)]


def load_w(C, views):
    s = C.nxt('wst', C.nwst)
    b = C.nxt('wbf', C.nwbf)
    off = 0
    outs = []
    for v in views:
        k, n = v.shape[1], v.shape[2]
        sz = k * n
        dst = C.wst[s][:, off:off + sz].rearrange("p (k n) -> p k n", n=n)
        C.dma(dst, v, (), (('wst', s),), f'ws{s}')
        outs.append(C.wbf[b][:, off:off + sz].rearrange("p (k n) -> p k n", n=n))
        off += sz
    assert off <= WCAP
    h = (off // 2 + 127) // 128 * 128
    C.cp(C.wbf[b][:, 0:h], C.wst[s][:, 0:h], (('wst', s),), (('wbf', b, 0),), eng='gpsimd')
    C.cp(C.wbf[b][:, h:off], C.wst[s][:, h:off], (('wst', s),), (('wbf', b, 1),), eng='vector')
    return outs, (('wbf', b, 0), ('wbf', b, 1))


def rope_tables(C):
    pos_d = C.dram_in("pos", [128, NB], I32)
    invf_d = C.dram_in("invf", [128, 16], F32)
    posi = C.sb("posi", [128, NB], I32)
    posf = C.sb("posf", [128, NB], F32)
    invf = C.sb("invf_sb", [128, 16], F32)
    AC = C.sb("AC", [128, 2, NB * 16], F32)
    KF = C.sb("KF", [128, 2, NB * 16], F32)
    KI = C.sb("KI", [128, 2, NB * 16], I32)
    MK = C.sb("MK", [128, 2, NB * 16], F32)
    C.dma(posi[:], pos_d, (), ('posi',), 'c0')
    C.dma(invf[:], invf_d, (), ('invf',), 'c0')
    C.cp(posf[:], posi[:], ('posi',), ('posf',))
    for m in range(NB):
        C.ts(AC[:, 0, m * 16:(m + 1) * 16], invf[:], posf[:, m:m + 1], None, ALU.mult, None, ('posf', 'invf'), ('AC',))
    C.ts(AC[:, 1, :], AC[:, 0, :], PI / 2, None, ALU.add, None, ('AC',), ('AC',))
    C.ts(KF[:], AC[:], 1.0 / (2 * PI), None, ALU.mult, None, ('AC',), ('KF',))
    C.cp(KI[:], KF[:], ('KF',), ('KI',))
    C.cp(KF[:], KI[:], ('KI',), ('KF',))
    C.stt(AC[:], KF[:], -2 * PI, AC[:], ALU.mult, ALU.add, ('KF', 'AC'), ('AC',))
    C.ts(MK[:], AC[:], PI, None, ALU.is_gt, None, ('AC',), ('MK',))
    C.stt(AC[:], MK[:], -2 * PI, AC[:], ALU.mult, ALU.add, ('MK', 'AC'), ('AC',))
    C.ts(MK[:], AC[:], -PI, None, ALU.is_lt, None, ('AC',), ('MK',))
    C.stt(AC[:], MK[:], 2 * PI, AC[:], ALU.mult, ALU.add, ('MK', 'AC'), ('AC',))
    C.ts(AC[:], AC[:], PI, -PI, ALU.min, ALU.max, ('AC',), ('AC',))
    C.act(KF[:], AC[:], AF.Sin, ('AC',), ('KF',))
    for h in range(4):
        C.cp(C.CS[:, :, :, h, :], KF[:].rearrange("p a (m f) -> p a m f", f=16), ('KF',), ('CS',))


def gathered_bufs(C):
    C.KL = C.dram_tmp("KL", [4, 16, 128, 256], BF16)
    C.VL = C.dram_tmp("VL", [TOK, D], BF16)
    C.QL = C.dram_tmp("QL", [16, 128, TOK], BF16)
    C.KG = C.dram_tmp("KG", [NPADB + 16, 16, 128, 256], BF16)
    C.VG = C.dram_tmp("VG", [NPADB + 16, 256, D], BF16)
    C.OL = C.dram_tmp("OL", [12, 128, TOK], BF16)
    C.KW = C.dram_tmp("KW", [16, 16, 128, 256], BF16)
    C.VW = C.dram_tmp("VW", [16, 256, D], BF16)
    C.XW = C.dram_tmp("XW", [13, 2, D], F32)
    C.XHL = C.dram_tmp("XHL", [4, 2, D], F32)
    C.XG = C.dram_tmp("XG", [1 + 16, 2, D], F32)


def zero_pads(C):
    z = C.sb("zpad", [128, 4096], BF16)
    C.memset(z[:], 0.0, ('zpad',))
    for b in range(NPADB):
        C.dma(C.KG[b].rearrange("h d t -> d h t"), z[:].rearrange("p (h t) -> p h t", t=256), ('zpad',), ('KGpad',), 'c0')
        for hf in range(2):
            C.dma(C.VG[b, hf * 128:(hf + 1) * 128, :], z[:, 0:D], ('zpad',), ('VGpad',), 'c0')
    C.dma(C.XG[0], z[0:2, 0:2 * D].bitcast(F32), ('zpad',), ('XGpad',), 'c0')


RG = [[0, 1, 2, 3], [4, 5, 6, 7]]


def allgather(C, src2d, dst2d, r, w):
    C.P.op('gpsimd', lambda e: e.collective_compute("AllGather", ALU.bypass, replica_groups=RG, ins=[src2d],
                                                    outs=[dst2d]), r, w)
    C.P.q['gpsimd'][-1]['inc'] = True


def phase_qkv(C, l):
    nc = C.nc
    wq_d = C.dram_in("w_qkv", [DEPTH, D, 3 * D], F32)[l].rearrange("(kc p) n -> p kc n", p=128)
    ln_d = C.dram_in("ln_mix", [DEPTH, 128, D], F32)[l]
    g_d = C.dram_in("qk_gain4", [DEPTH, 128, 4, 128], F32)[l]
    KLv = C.KL.rearrange("m h d t -> h d m t")
    qT_o = C.QL
    v_o = C.VL

    lnw = C.sb("lnw", [128, D], F32)
    C.dma(lnw[:], ln_d, (), ('lnw',), 'c0')
    G6 = C.sb("G6", [128, 4, 128], F32)
    C.dma(G6[:], g_d, (), ('G6',), 'c0')
    make_hT(C, lnw[:], 'lnw')
    import os
    ccm = os.environ.get("CCMODE", "kv")
    if 'onlyht' in ccm:
        return

    acc = [C.ps(f"acc{i}", [128, 512], F32) for i in range(4)]
    sq = C.sb("sq", [128, 512], F32)
    qn = [C.sb(f"qn{i}", [128, 4, 128], F32) for i in range(2)]
    qb = [C.sb(f"qb{i}", [128, 4, 128], BF16) for i in range(3)]
    rt = [C.sb(f"rt{i}", [128, 4, 4, 16], F32) for i in range(2)]
    ss = [C.sb(f"ss{i}", [128, 4], F32) for i in range(2)]
    qTs = [C.sb(f"qTs{i}", [128, 4, TOK], BF16) for i in range(2)]
    vs = [t[:].rearrange("p h (m c) -> p (h m) c", c=512) for t in qTs]

    def load_chunk(c):
        wbs, wtok = [], ()
        for q4 in range(4):
            (wb_,), wt_ = load_w(C, [wq_d[:, q4 * 4:(q4 + 1) * 4, c * 512:(c + 1) * 512]])
            wbs.append(wb_)
            wtok = wtok + wt_
        return wbs, wtok

    KG2 = C.KG.rearrange("b h d t -> (b h d) t")
    VG2 = C.VG.rearrange("b t c -> (b t) c")
    order = [4, 5, 6, 7, 8, 9, 10, 11, 0, 1, 2, 3]
    nxt_w = load_chunk(order[0])
    for ci, c in enumerate(order):
        wbs, wtok = nxt_w
        if ci + 1 < 12:
            nxt_w = load_chunk(order[ci + 1])
        kind = c // 4
        cc = c % 4
        so = C.nxt('qTs', 2)
        deferred = []
        for m in range(NB):
            a = C.nxt('acc', 4)
            for kc in range(16):
                C.mm(acc[a][:], C.hT[:, kc, m * 128:(m + 1) * 128], wbs[kc // 4][:, kc % 4, :], kc == 0, kc == 15,
                     (('hT', m),) + wtok, (('acc', a),))
            if kind == 2:
                C.cp(vs[so][:, m, :], acc[a][:], (('acc', a),), (('qTs', so, m),), eng='scalar')
                continue
            i = C.nxt('qn', 2)
            ib = C.nxt('qbr', 3)
            av = acc[a][:].rearrange("p (h d) -> p h d", d=128)
            if cc < 3:
                C.act(sq[:], acc[a][:], AF.Square, (('acc', a),), ('sq',))
                C.P.op('vector', lambda e, o=ss[i][:], s=sq[:].rearrange("p (h d) -> p h d", d=128):
                       e.tensor_reduce(out=o, in_=s, axis=AX.X, op=ALU.add), ('sq',), (('ss', i),))
                rstd_from_ss(C, ss[i], 4, 1.0 / 128, ('ss', i))
                for h in range(4):
                    C.stt(qn[i][:, h, :], av[:, h, :], ss[i][:, h:h + 1],
                          G6[:, (kind if cc * 4 + h < 6 else 2 + kind), :], ALU.mult, ALU.mult,
                          (('acc', a), ('ss', i), 'G6'), (('qn', i),))
                x1 = qn[i][:, :, 0:16]
                x2 = qn[i][:, :, 16:32]
                sn = C.CS[:, 0, m]
                cs = C.CS[:, 1, m]
                R = rt[i]
                C.tt(R[:, 0], x1, cs, ALU.mult, (('qn', i), 'CS'), (('rt', i),))
                C.tt(R[:, 1], x2, sn, ALU.mult, (('qn', i), 'CS'), (('rt', i),))
                C.tt(R[:, 2], x2, cs, ALU.mult, (('qn', i), 'CS'), (('rt', i),))
                C.tt(R[:, 3], x1, sn, ALU.mult, (('qn', i), 'CS'), (('rt', i),))
                C.tt(qb[ib][:, :, 0:16], R[:, 0], R[:, 1], ALU.subtract, (('rt', i),), (('qb', ib),))
                C.tt(qb[ib][:, :, 16:32], R[:, 2], R[:, 3], ALU.add, (('rt', i),), (('qb', ib),))
                C.cp(qb[ib][:, :, 32:128], qn[i][:, :, 32:128], (('qn', i),), (('qb', ib),), eng='scalar')
            else:
                C.cp(qb[ib][:], av, (('acc', a),), (('qb', ib),), eng='scalar')
            def do_tr(ib=ib, m=m, so=so):
                j = C.nxt('tp', 2)
                for h in range(4):
                    C.tr(C.tp[j][:, h, :], qb[ib][:, h, :], C.ident[:], (('qb', ib), 'ident'), (('tp', j),))
                C.cp(qTs[so][:, :, m * 128:(m + 1) * 128], C.tp[j][:], (('tp', j),), (('qTs', so, m),), eng='vector')
            if deferred:
                deferred.pop()()
            deferred.append(do_tr)
        if deferred:
            deferred.pop()()
        if kind == 2:
            for m in range(NB):
                C.dma(v_o[m * 128:(m + 1) * 128, cc * 512:(cc + 1) * 512], vs[so][:, m, :], (('qTs', so, m),),
                      (('VL', m, cc),), f'os{so}')
        elif kind == 0:
            C.dma(qT_o[cc * 4:(cc + 1) * 4].rearrange("h d t -> d h t"), qTs[so][:],
                  tuple(('qTs', so, m) for m in range(NB)), (('QL', cc),), f'os{so}')
        else:
            for mq in range(4):
                C.dma(C.KL[mq, cc * 4:(cc + 1) * 4].rearrange("h d t -> d h t"), qTs[so][:, :, mq * 256:(mq + 1) * 256],
                      (('qTs', so, 2 * mq), ('qTs', so, 2 * mq + 1)), (('KL', cc, mq),), f'os{so}')
        if c == 7:
            for m in range(4):
                allgather(C, C.KL[m].rearrange("h d t -> (h d) t"),
                          KG2[(NPADB + 4 * m) * 2048:(NPADB + 4 * m + 4) * 2048, :],
                          tuple(('KL', cc_, m) for cc_ in range(4)), (('KG', m),))
        if c == 11:
            for m in range(4):
                allgather(C, C.VL[m * 256:(m + 1) * 256, :], VG2[(NPADB + 4 * m) * 256:(NPADB + 4 * m + 4) * 256, :],
                          tuple(('VL', mm_, cc_) for mm_ in (2 * m, 2 * m + 1) for cc_ in range(4)), (('VG', m),))


def own_tokens(j):
    return np.concatenate([np.arange((4 * m + j) * 256, (4 * m + j + 1) * 256) for m in range(4)])


def bc128(v):
    v = np.asarray(v, np.float32).reshape(1, -1)
    return np.ascontiguousarray(np.broadcast_to(v, (128, v.shape[1])))


def consts():
    ident = np.eye(128, dtype=np.float32).astype(NPBF)
    invf = (500000.0 ** (-np.arange(0, 32, 2, dtype=np.float32) / 32)).astype(np.float32)
    return dict(ident=ident, invf=bc128(invf))


def gain4(qk_gain_l):
    return np.ascontiguousarray(np.broadcast_to(np.asarray(qk_gain_l, np.float32)[None], (128, 4, 128)))


def gain6(qk_gain_l):
    out = np.zeros((6, 512), np.float32)
    for kind in range(2):
        for cc in range(3):
            for h in range(4):
                head = cc * 4 + h
                row = kind if head < 6 else 2 + kind
                out[kind * 3 + cc, h * 128:(h + 1) * 128] = qk_gain_l[row]
    return np.ascontiguousarray(np.broadcast_to(out[None], (128, 6, 512)))


def run(nc, in_maps):
    res = run_bass_kernel_spmd(nc, in_maps, core_ids=list(range(NCORE)))
    return res.results


NKB = 32
NKB2 = 16
NPADB = 3
DEPTH = 2
DIL = ((128, 1), (512, 4), (2048, 16))


def amask_index(g, dl):
    nd = DIL[g][0] // 128
    if g == 0:
        return dl
    base = 2 + 3 * (g - 1)
    return base + (0 if dl == 0 else (2 if dl == nd else 1))


def phase_attn(C, l):
    nc = C.nc
    am_d = C.dram_in("amask", [128, 8, 128], BF16)
    tri_d = C.dram_in("tri", [128, 2, 128], BF16)
    neg_d = C.dram_in("neglt", [128, 128], F32)
    padv_d = C.dram_in("padv", [128, 6, 128], BF16)
    gm_d = C.dram_in("bsel", [128, 3, 4, NKB2], F32)
    e19_d = C.dram_in("e19", [NKB2, NKB2 * 128], BF16)
    u_d = C.dram_in("uones", [128, 2, 128], F32)
    idf_d = C.dram_in("identf", [128, 128], F32)
    oT_o = C.OL
    jv = C.jv
    kg_tok = ('KW',)
    vg_tok = ('VW',)
    f2 = lambda ap, pat: ap.rearrange(pat).rearrange("(r c) -> r c", c=16384)
    C.dma(f2(C.KW, "b h d t -> (b h d t)"), f2(C.KG[bass.ds(jv, 16)], "b h d t -> (b h d t)"),
          tuple(('KG', m) for m in range(4)) + ('KGpad',), ('KW',), 'kw')
    C.dma(f2(C.VW, "b t c -> (b t c)"), f2(C.VG[bass.ds(jv, 16)], "b t c -> (b t c)"),
          tuple(('VG', m) for m in range(4)) + ('VGpad',), ('VW',), 'vw', eng='scalar')
    ql_tok = tuple(('QL', cc) for cc in range(4))

    def cload(name, d, shape, dt):
        t = C.sb(name, shape, dt)
        C.dma(t[:], d, (), (name,), 'c0')
        return t
    AM = cload("AM", am_d, [128, 8, 128], BF16)
    TRI = cload("TRI", tri_d, [128, 2, 128], BF16)
    NEGLT = cload("NEGLT", neg_d, [128, 128], F32)
    PADV = cload("PADV", padv_d, [128, 6, 128], BF16)
    BS = cload("BS", gm_d, [128, 3, 4, NKB2], F32)
    E19 = cload("E19", e19_d, [NKB2, NKB2 * 128], BF16)
    UO = cload("UO", u_d, [128, 2, 128], F32)
    IDF = cload("IDF", idf_d, [128, 128], F32)
    ONESB = C.sb("ONESB", [128, 128], BF16)
    C.memset(ONESB[:], 1.0, ('ONESB',))
    ctoks = ('AM', 'TRI', 'NEGLT', 'PADV', 'BS', 'E19', 'UO', 'IDF', 'ONESB')

    kTb = [C.sb(f"kTb{i}", [128, NKB * 128], BF16) for i in range(2)]
    vb = [C.sb(f"vb{i}", [128, NKB, 128], BF16) for i in range(2)]
    qb = [C.sb(f"qTb{i}", [128, TOK], BF16) for i in range(2)]
    ost = [C.sb(f"ost{i}", [128, TOK], BF16) for i in range(2)]
    PB = [C.ps(f"pb{i}", [128, 512], F32) for i in range(8)]
    Z = [PB[i][:, 0:128] for i in range(3)]
    ACCO = [PB[3][:, 0:128], PB[4][:, 0:128]]
    AFT = [PB[5][:, 0:128], PB[6][:, 0:128]]
    ACCS = AFT
    GATE = [PB[7][:, 0:NKB2]] * 2
    SBT = [PB[7][0:NKB2, 128:256]] * 2
    NT = 4
    tmp = {nm: [C.sb(f"{nm}{i}", [128, 128], F32) for i in range(NT)] for nm in ('et', 'spt', 'lmt', 't1', 't2')}
    at = [C.sb(f"at{i}", [128, 128], BF16) for i in range(NT)]
    Rt = [C.sb(f"Rt{i}", [128, 128], F32) for i in range(2)]
    AO = C.sb("AO", [128, NB, 128], F32)
    AS = C.sb("AS", [128, NB, 128], F32)
    rs = [C.sb(f"rs{i}", [128, 128], F32) for i in range(2)]
    kmf = C.sb("kmf", [128, NKB2], F32)
    kmb = C.sb("kmb", [128, NKB2], BF16)
    kml = C.sb("kml", [128, NKB2], BF16)
    gsb = [C.sb(f"gsb{i}", [128, NKB2], F32) for i in range(2)]
    top8 = [C.sb(f"top8{i}", [128, 8], F32) for i in range(2)]
    sbTb = [C.sb(f"sbTb{i}", [NKB2, 128], BF16) for i in range(NB)]

    def run_pipeline(jobs, stages):
        n, S = len(jobs), len(stages)
        for t in range(n + S - 1):
            for st_ in range(S - 1, -1, -1):
                k = t - st_
                if 0 <= k < n:
                    stages[st_](jobs[k], k)

    head_order = [12, 13, 14, 15] + [2 * g + sl for sl in range(2) for g in range(3)] + [6, 7, 8, 9, 10, 11]
    loaded = {}

    def load_head(h):
        if h not in loaded:
            loaded[h] = _load_head(h)
        k = head_order.index(h)
        if k + 1 < len(head_order) and head_order[k + 1] not in loaded:
            loaded[head_order[k + 1]] = _load_head(head_order[k + 1])
        return loaded[h]

    def _load_head(h):
        i = C.nxt('kv', 2)
        kdst = kTb[i][:].rearrange("d (n t) -> d n t", t=256)
        for q4 in range(4):
            C.dma(kdst[:, q4 * 4:(q4 + 1) * 4, :],
                  C.KW[q4 * 4:(q4 + 1) * 4, h].rearrange("n d t -> d n t"), kg_tok, (('kT', i),), f'kv{i}')
            C.dma(vb[i][:, q4 * 8:(q4 + 1) * 8, :],
                  C.VW[q4 * 4:(q4 + 1) * 4, :, h * 128:(h + 1) * 128].rearrange("n (hf s) d -> s (n hf) d", hf=2),
                  vg_tok, (('v', i),), f'kv{i}')
        C.dma(qb[i][:], C.QL[h], ql_tok, (('q', i),), f'kv{i}')
        return i

    def store_head(slot, oi):
        C.dma(oT_o[slot], ost[oi][:], (('ost', oi),), (('OL', slot),), f'oo{oi}')

    def kq(lm):
        return 2 * (4 * (lm // 2) + 3) + (lm % 2)

    for hc in range(4):
        hi = load_head(12 + hc)
        oi = C.nxt('ost', 2)
        kT, v, q = kTb[hi], vb[hi], qb[hi]
        hk = (('kT', hi), ('q', hi))
        jobs = [dict(lm=lm, kb=kb, first=(kb == kq(lm)), last=(kb == 0)) for lm in range(NB)
                for kb in range(kq(lm), -1, -1)]

        def c_s0(j, k):
            zi, ti = k % 3, k % NT
            C.mm(Z[zi], kT[:, j['kb'] * 128:(j['kb'] + 1) * 128], q[:, j['lm'] * 128:(j['lm'] + 1) * 128], True, True,
                 hk, (('z', zi),))
            C.act(tmp['et'][ti][:], Z[zi], AF.Exp, (('z', zi),), (('et', ti),), scale=SCALE)

        def c_s0b(j, k):
            zi, ti = k % 3, k % NT
            C.act(tmp['spt'][ti][:], tmp['et'][ti][:], AF.Ln, (('et', ti),), (('spt', ti),), bias=1.0, scale=1.0)
            if j['first']:
                C.tt(tmp['lmt'][ti][:], tmp['spt'][ti][:], TRI[:, 1, :], ALU.mult, (('spt', ti), 'TRI'),
                     (('lmt', ti),), eng='gpsimd')
            C.stt(tmp['t1'][ti][:], Z[zi], SCALE, tmp['spt'][ti][:], ALU.mult, ALU.subtract, (('z', zi), ('spt', ti)),
                  (('t1', ti),))

        def c_s1(j, k):
            zi, ti, ai = k % 3, k % NT, k % 2
            first, last = j['first'], j['last']
            L, Ltok = (tmp['lmt'][ti], ('lmt', ti)) if first else (tmp['spt'][ti], ('spt', ti))
            rp, rn = (k + 1) % 2, k % 2
            C.mm(AFT[ai], UO[:, 0, :], L[:], True, first, (Ltok, 'UO'), (('ps56', ai),))
            if not first:
                C.mm(AFT[ai], UO[:, 1, :], Rt[rp][:], False, True, (('Rt', rp), 'UO'), (('ps56', ai),))
            if not last:
                if first:
                    C.cp(Rt[rn][:], L[:], (Ltok,), (('Rt', rn),), eng='gpsimd')
                else:
                    C.tt(Rt[rn][:], Rt[rp][:], L[:], ALU.add, (('Rt', rp), Ltok), (('Rt', rn),), eng='gpsimd')
            C.tt(tmp['t2'][ti][:], tmp['t1'][ti][:], AFT[ai], ALU.subtract, (('t1', ti), ('ps56', ai)), (('t2', ti),))
            if first:
                C.tt(tmp['t2'][ti][:], tmp['t2'][ti][:], NEGLT[:], ALU.add, (('t2', ti), 'NEGLT'), (('t2', ti),))
            C.act(at[ti][:], tmp['t2'][ti][:], AF.Exp, (('t2', ti),), (('at', ti),))

        def c_s2(j, k, oi=oi, v=v, hi=hi):
            ti, ao = k % NT, j['lm'] % 2
            C.mm(ACCO[ao], v[:, j['kb'], :], at[ti][:], j['first'], j['last'], (('v', hi), ('at', ti)), (('acco', ao),))
            if j['last']:
                C.cp(ost[oi][:, j['lm'] * 128:(j['lm'] + 1) * 128], ACCO[ao], (('acco', ao),), (('ost', oi),),
                     eng='scalar')

        run_pipeline(jobs, [c_s0, c_s0b, c_s1, c_s2])
        store_head(8 + hc, oi)

    for slot in range(2):
        oi = C.nxt('ost', 2)
        for g in range(3):
            hi = load_head(2 * g + slot)
            kT, v, q = kTb[hi], vb[hi], qb[hi]
            hk = (('kT', hi), ('q', hi))
            nd = DIL[g][0] // 128
            jobs = []
            for lm in range(NB):
                kbs = [kq(lm) - dl for dl in range(nd + 1) if kq(lm) - dl >= 0]
                for kb in kbs:
                    jobs.append(dict(lm=lm, kb=kb, first=(kb == kbs[0]), last=(kb == kbs[-1]), dl=kq(lm) - kb))

            def a_s0(j, k, g=g):
                zi, ti = k % 3, k % NT
                C.mm(Z[zi], kT[:, j['kb'] * 128:(j['kb'] + 1) * 128], q[:, j['lm'] * 128:(j['lm'] + 1) * 128], True,
                     True, hk, (('z', zi),))
                C.act(tmp['et'][ti][:], Z[zi], AF.Exp, (('z', zi),), (('et', ti),), scale=SCALE)
                C.tt(at[ti][:], tmp['et'][ti][:], AM[:, amask_index(g, j['dl']), :], ALU.mult, (('et', ti), 'AM'),
                     (('at', ti),), eng='gpsimd' if (k % 2) else 'vector')

            def a_s1(j, k, g=g):
                ti, ao = k % NT, j['lm'] % 2
                kb, lm = j['kb'], j['lm']
                C.mm(ACCO[ao], v[:, kb, :], at[ti][:], j['first'], j['last'], (('v', hi), ('at', ti)), (('acco', ao),))
                C.mm(ACCS[ao], PADV[:, kb, :] if kb < 6 else ONESB[:], at[ti][:], j['first'], j['last'],
                     (('at', ti), 'PADV', 'ONESB'), (('ps56', ao),))
                if j['last']:
                    if g == 0:
                        C.cp(AO[:, lm, :], ACCO[ao], (('acco', ao),), (('AO', lm),), eng='scalar')
                        C.cp(AS[:, lm, :], ACCS[ao], (('ps56', ao),), (('AS', lm),), eng='scalar')
                    else:
                        C.tt(AO[:, lm, :], AO[:, lm, :], ACCO[ao], ALU.add, (('AO', lm), ('acco', ao)), (('AO', lm),))
                        C.tt(AS[:, lm, :], AS[:, lm, :], ACCS[ao], ALU.add, (('AS', lm), ('ps56', ao)), (('AS', lm),))

            run_pipeline(jobs, [a_s0, a_s1])
        for lm in range(NB):
            C.recip(AS[:, lm, :], AS[:, lm, :], (('AS', lm),), (('AS', lm),))
            C.tt(ost[oi][:, lm * 128:(lm + 1) * 128], AO[:, lm, :], AS[:, lm, :], ALU.mult, (('AO', lm), ('AS', lm)),
                 (('ost', oi),))
        store_head(slot, oi)

    for hb_ in range(6):
        hi = load_head(6 + hb_)
        oi = C.nxt('ost', 2)
        kT, v, q = kTb[hi], vb[hi], qb[hi]
        hk = (('kT', hi), ('q', hi))
        C.P.op('vector', lambda e, o=kmf[:], s=kT[:].rearrange("p (n k) -> p n k", k=256):
               e.tensor_reduce(out=o, in_=s, axis=AX.X, op=ALU.add), (('kT', hi),), ('kmf',))
        C.ts(kmf[:], kmf[:], 1.0 / 256, None, ALU.mult, None, ('kmf',), ('kmf',))
        C.cp(kmb[:], kmf[:], ('kmf',), ('kmb',))
        C.tt(kml[:], kmf[:], kmb[:], ALU.subtract, ('kmf', 'kmb'), ('kml',))
        for lm in range(NB):
            mp = lm // 2
            qs = q[:, lm * 128:(lm + 1) * 128]
            gi = lm % 2
            G = gsb[gi]
            C.mm(GATE[gi], qs, kmb[:], True, False, (('q', hi), 'kmb'), ('pb7',))
            C.mm(GATE[gi], qs, kml[:], False, True, (('q', hi), 'kml'), ('pb7',))
            C.tt(G[:], GATE[gi], BS[:, 0, mp, :], ALU.add, ('pb7', 'BS'), (('gsb', gi),))
            C.P.op('vector', lambda e, o=top8[gi][:], s=G[:]: e.max(out=o, in_=s), (('gsb', gi),), (('top8', gi),))
            C.ts(G[:], G[:], top8[gi][:, 2:3], None, ALU.is_ge, None, (('gsb', gi), ('top8', gi)), (('gsb', gi),))
            C.tt(G[:], G[:], BS[:, 1, mp, :], ALU.mult, (('gsb', gi), 'BS'), (('gsb', gi),))
            C.tt(G[:], G[:], BS[:, 2, mp, :], ALU.add, (('gsb', gi), 'BS'), (('gsb', gi),))
            C.tr(SBT[gi], G[:], IDF[:], (('gsb', gi), 'IDF'), ('pb7',))
            C.cp(sbTb[lm][:], SBT[gi], ('pb7',), (('sbTb', lm),), eng='scalar')
        jobs = [dict(lm=lm, kb=kb, first=(kb == 0), last=(kb == kq(lm))) for lm in range(NB)
                for kb in range(0, kq(lm) + 1)]

        def b_s0(j, k):
            zi, ti = k % 3, k % NT
            kb, lm = j['kb'], j['lm']
            n2 = kb // 2
            C.mm(Z[zi], kT[:, kb * 128:(kb + 1) * 128], q[:, lm * 128:(lm + 1) * 128], True, False, hk, (('z', zi),))
            C.mm(Z[zi], E19[:, n2 * 128:(n2 + 1) * 128], sbTb[lm][:], False, True, (('sbTb', lm), 'E19'), (('z', zi),))
            if j['last']:
                C.act(tmp['et'][ti][:], Z[zi], AF.Exp, (('z', zi),), (('et', ti),), scale=SCALE)
                C.tt(at[ti][:], tmp['et'][ti][:], TRI[:, 0, :], ALU.mult, (('et', ti), 'TRI'), (('at', ti),))
            else:
                C.act(at[ti][:], Z[zi], AF.Exp, (('z', zi),), (('at', ti),), scale=SCALE)

        def b_s1(j, k, oi=oi):
            ti, ao = k % NT, j['lm'] % 2
            kb, lm = j['kb'], j['lm']
            C.mm(ACCO[ao], v[:, kb, :], at[ti][:], j['first'], j['last'], (('v', hi), ('at', ti)), (('acco', ao),))
            C.mm(ACCS[ao], ONESB[:], at[ti][:], j['first'], j['last'], (('at', ti), 'ONESB'), (('ps56', ao),))
            if j['last']:
                C.recip(rs[ao][:], ACCS[ao], (('ps56', ao),), (('rs', ao),))
                C.tt(ost[oi][:, lm * 128:(lm + 1) * 128], ACCO[ao], rs[ao][:], ALU.mult, (('acco', ao), ('rs', ao)),
                     (('ost', oi),))

        run_pipeline(jobs, [b_s0, b_s1])
        store_head(2 + hb_, oi)


def attn_consts(j):
    s = np.arange(128)[:, None]
    t = np.arange(128)[None, :]
    am = np.zeros((8, 128, 128), np.float32)
    for g, (w, dil) in enumerate(DIL):
        nd = w // 128
        for dl in range(nd + 1):
            dist = t - s + 128 * dl
            am[amask_index(g, dl)] = ((dist >= 0) & (dist <= w) & (dist % dil == 0))
    tri = np.stack([(s <= t), (s < t)]).astype(np.float32)
    neglt = np.where(s < t, 0.0, NEG).astype(np.float32)
    npad = 2 * (3 - j)
    padv = np.zeros((6, 128, 128), np.float32)
    padv[npad:] = 1.0
    bsel = np.zeros((3, 4, NKB2), np.float32)
    for mp in range(4):
        own = 4 * mp + 3
        for n in range(NKB2):
            valid_past = (n >= 3 - j) and (n < own)
            bsel[0, mp, n] = 0.0 if valid_past else -1e30
            bsel[1, mp, n] = -NEG if valid_past else 0.0
            bsel[2, mp, n] = 0.0 if n == own else NEG
    e19 = np.zeros((NKB2, NKB2, 128), np.float32)
    for n in range(NKB2):
        e19[n, n, :] = 1.0
    uo = np.stack([(s > t).astype(np.float32), np.ones((128, 128), np.float32)])
    return dict(amask=np.ascontiguousarray(am.transpose(1, 0, 2)).astype(NPBF),
                tri=np.ascontiguousarray(tri.transpose(1, 0, 2)).astype(NPBF),
                neglt=neglt,
                padv=np.ascontiguousarray(padv.transpose(1, 0, 2)).astype(NPBF),
                bsel=np.ascontiguousarray(np.broadcast_to(bsel[None], (128, 3, 4, NKB2))),
                e19=e19.reshape(NKB2, NKB2 * 128).astype(NPBF),
                uones=np.ascontiguousarray(uo.transpose(1, 0, 2)),
                identf=np.eye(128, dtype=np.float32))


def head_norm(C, accap, acctok, gain_ap, gain_tok, out_bf, out_tok, sq, ss, sstok):
    av = accap.rearrange("p (h d) -> p h d", d=128)
    C.act(sq[:], accap, AF.Square, (acctok,), ('sq',))
    C.P.op('vector', lambda e, o=ss[:], s_=sq[:].rearrange("p (h d) -> p h d", d=128):
           e.tensor_reduce(out=o, in_=s_, axis=AX.X, op=ALU.add), ('sq',), (sstok,))
    rstd_from_ss(C, ss, 4, 1.0 / 128, sstok)
    for h in range(4):
        C.stt(out_bf[:, h, :], av[:, h, :], ss[:, h:h + 1], gain_ap, ALU.mult, ALU.mult,
              (acctok, sstok, gain_tok), (out_tok,))


def phase_post(C, l):
    r3 = lambda ap: ap.rearrange("(kc p) n -> p kc n", p=128)
    wg_d = r3(C.dram_in("w_gate", [DEPTH, D, 3 * D], F32)[l])
    wbr_d = [r3(C.dram_in("w_br_a", [DEPTH, 256, D], F32)[l]), r3(C.dram_in("w_br_b", [DEPTH, 768, D], F32)[l]),
             r3(C.dram_in("w_br_c", [DEPTH, 512, D], F32)[l])]
    wo_d = r3(C.dram_in("w_o", [DEPTH, D, D], F32)[l])
    bg_d = C.dram_in("b_gate_p", [DEPTH, 128, 48], F32)[l]
    ln_d = C.dram_in("ln_mix", [DEPTH, 128, D], F32)[l]
    lnq_d = C.dram_in("ln_mem_q", [DEPTH, 128, D], F32)[l]
    lnkv_d = C.dram_in("ln_mem_kv", [DEPTH, 128, D], F32)[l]
    wmq_d = r3(C.dram_in("wm_q", [DEPTH, D, 512], F32)[l])
    wmkv_d = r3(C.dram_in("wm_kv", [DEPTH, D, 1024], F32)[l])
    wmo_d = r3(C.dram_in("wm_o", [DEPTH, 512, D], F32)[l])
    mg_d = C.dram_in("mem_gain", [DEPTH, 128, 2, 128], F32)[l]
    mem_d = C.dram_in("mem", [256, D], F32)
    oT_d = C.OL

    lnw = C.sb("lnw", [128, D], F32)
    C.dma(lnw[:], ln_d, (), ('lnw',), 'c0')
    BG = C.sb("BG", [128, 48], F32)
    C.dma(BG[:], bg_d, (), ('BG',), 'c0')
    MG = C.sb("MG", [128, 2, 128], F32)
    C.dma(MG[:], mg_d, (), ('MG',), 'c0')
    oT = C.sb("oTs", [128, 12, TOK], BF16)
    C.dma(oT[:], oT_d.rearrange("s d t -> d s t"), tuple(('OL', sl) for sl in range(12)), ('oT',), 'c1')
    ONESB = C.sb("ONESB", [128, 128], BF16)
    C.memset(ONESB[:], 1.0, ('ONESB',))

    make_hT(C, lnw[:], 'lnw')
    hT_all = tuple(('hT', m) for m in range(NB))

    PSB = [C.ps(f"psb{i}", [128, 512], F32) for i in range(6)]
    mT = [C.sb(f"mT{i}", [128, 4, TOK], BF16) for i in range(1)]
    gs = [C.sb(f"gs{i}", [128, 512], F32) for i in range(2)]
    tq = [C.sb(f"tq{i}", [128, 512], F32) for i in range(2)]
    macc = [C.sb(f"macc{i}", [128, 512], F32) for i in range(2)]
    brslot = (0, 2, 8)
    brk = (2, 6, 4)

    for grp in range(4):
        mi = 0
        for ncl in range(4):
            ncg = grp * 4 + ncl
            for br in range(3):
                (wg,), wgt = load_w(C, [wg_d[:, :, br * D + ncg * 128: br * D + (ncg + 1) * 128]])
                (wb,), wbt = load_w(C, [wbr_d[br][:, :, ncg * 128:(ncg + 1) * 128]])
                for tg in range(2):
                    pg = C.nxt('psg', 2)
                    pp = 2 + C.nxt('psp', 2)
                    tsl = slice(tg * 512, (tg + 1) * 512)
                    for kc in range(16):
                        C.mm(PSB[pg][:], wg[:, kc, :], C.hT[:, kc, tsl], kc == 0, kc == 15, hT_all + wgt,
                             (('psb', pg),))
                    for kc in range(brk[br]):
                        C.mm(PSB[pp][:], wb[:, kc, :], oT[:, brslot[br] + kc, tsl], kc == 0, kc == brk[br] - 1,
                             ('oT',) + wbt, (('psb', pp),))
                    gi = C.nxt('gs', 2)
                    C.act(gs[gi][:], PSB[pg][:], AF.Sigmoid, (('psb', pg), 'BG'), (('gs', gi),),
                          bias=BG[:, br * 16 + ncg: br * 16 + ncg + 1], scale=1.0)
                    if br == 0:
                        C.tt(macc[tg][:], gs[gi][:], PSB[pp][:], ALU.mult, (('gs', gi), ('psb', pp)), (('macc', tg),))
                    else:
                        C.tt(tq[gi][:], gs[gi][:], PSB[pp][:], ALU.mult, (('gs', gi), ('psb', pp)), (('tq', gi),))
                        if br == 1:
                            C.tt(macc[tg][:], macc[tg][:], tq[gi][:], ALU.add, (('macc', tg), ('tq', gi)),
                                 (('macc', tg),), eng='gpsimd')
                        else:
                            C.tt(mT[mi][:, ncl, tsl], macc[tg][:], tq[gi][:], ALU.add, (('macc', tg), ('tq', gi)),
                                 (('mT', mi),), eng='gpsimd')
        for n in range(4):
            (wo,), wot = load_w(C, [wo_d[:, grp * 4:(grp + 1) * 4, n * 512:(n + 1) * 512]])
            for m in range(NB):
                pa = 4 + C.nxt('psa', 2)
                for kc in range(4):
                    C.mm(PSB[pa][:], mT[mi][:, kc, m * 128:(m + 1) * 128], wo[:, kc, :], kc == 0, kc == 3,
                         (('mT', mi),) + wot, (('psb', pa),))
                xs = C.X[:, m, n * 512:(n + 1) * 512]
                C.tt(xs, xs, PSB[pa][:], ALU.add, (('X', m), ('psb', pa)), (('X', m),))

    lnq = lnw
    C.dma(lnq[:], lnq_d, (), ('lnw',), 'c0')
    make_hT(C, lnq[:], 'lnw')
    lnkv = lnw
    C.dma(lnkv[:], lnkv_d, (), ('lnw',), 'c0')
    C.P.op('vector', lambda e: e.memset(ONESB[:, 0:1], 1.0), ('ONESB',), ('oT', 'oTfree'))
    oTf = oT[:].rearrange("p s t -> p (s t)")
    memT = oT[:].rearrange("p s t -> p (s t)")[:, 0:4096].rearrange("p (k t) -> p k t", t=256)
    memx = mT[0][:].rearrange("p a t -> p (a t)").bitcast(F32)
    for mb in range(2):
        C.dma(memx, mem_d[mb * 128:(mb + 1) * 128, :], (), (('mT', 0),), 'c1')
        i = C.nxt('hb', C.nhb)
        st = C.nst[i]
        C.act(C.hb[i][:], memx, AF.Square, (('mT', 0),), (('hb', i), ('nst', i)), accum_out=st[:, 0:1])
        rstd_from_ss(C, st, 1, 1.0 / D, ('nst', i))
        C.stt(C.hb[i][:], memx, st[:, 0:1], lnkv[:], ALU.mult, ALU.mult, (('mT', 0), ('nst', i), 'lnw'), (('hb', i),))
        for g in range(4):
            j = C.nxt('tp', 2)
            for a in range(4):
                kc = g * 4 + a
                C.tr(C.tp[j][:, a, :], C.hb[i][:, kc * 128:(kc + 1) * 128], C.ident[:], (('hb', i), 'ident'),
                     (('tp', j),))
            C.cp(memT[:, g * 4:(g + 1) * 4, mb * 128:(mb + 1) * 128], C.tp[j][:], (('tp', j), 'oTfree'), ('memT',),
                 eng='scalar' if g % 2 == 0 else 'vector')
    memT = oTf[:, 0:4096].rearrange("p (k t) -> p k t", t=256)
    sq = C.sb("sq", [128, 512], F32)
    ss = [C.sb(f"ss{i}", [128, 4], F32) for i in range(2)]
    nb16 = [C.sb(f"nb16_{i}", [128, 4, 128], BF16) for i in range(2)]
    kmT = C.sb("kmT", [128, 4, 256], BF16)
    vm = C.sb("vm", [128, 2, 512], BF16)
    qmT = oTf[:, 4096:8192].rearrange("p (h t) -> p h t", t=TOK)
    omT = oTf[:, 8192:12288].rearrange("p (h t) -> p h t", t=TOK)
    for c in range(2):
        ws, wt = [], ()
        for q4 in range(4):
            (w_,), t_ = load_w(C, [wmkv_d[:, q4 * 4:(q4 + 1) * 4, c * 512:(c + 1) * 512]])
            ws.append(w_)
            wt = wt + t_
        for mb in range(2):
            pa = 4 + C.nxt('psa', 2)
            for kc in range(16):
                C.mm(PSB[pa][:], memT[:, kc, mb * 128:(mb + 1) * 128], ws[kc // 4][:, kc % 4, :], kc == 0, kc == 15,
                     ('memT',) + wt, (('psb', pa),))
            if c == 1:
                C.cp(vm[:, mb, :], PSB[pa][:], (('psb', pa),), ('vm',), eng='scalar')
            else:
                i = C.nxt('nb16', 2)
                head_norm(C, PSB[pa][:], ('psb', pa), MG[:, 1, :], 'MG', nb16[i], ('nb16', i), sq, ss[i], ('ss', i))
                j = C.nxt('tp', 2)
                for h in range(4):
                    C.tr(C.tp[j][:, h, :], nb16[i][:, h, :], C.ident[:], (('nb16', i), 'ident'), (('tp', j),))
                C.cp(kmT[:, :, mb * 128:(mb + 1) * 128], C.tp[j][:], (('tp', j),), ('kmT',))
    ws, wt = [], ()
    for q4 in range(4):
        (w_,), t_ = load_w(C, [wmq_d[:, q4 * 4:(q4 + 1) * 4, :]])
        ws.append(w_)
        wt = wt + t_
    for m in range(NB):
        pa = 4 + C.nxt('psa', 2)
        for kc in range(16):
            C.mm(PSB[pa][:], C.hT[:, kc, m * 128:(m + 1) * 128], ws[kc // 4][:, kc % 4, :], kc == 0, kc == 15,
                 (('hT', m),) + wt, (('psb', pa),))
        i = C.nxt('nb16', 2)
        head_norm(C, PSB[pa][:], ('psb', pa), MG[:, 0, :], 'MG', nb16[i], ('nb16', i), sq, ss[i], ('ss', i))
        j = C.nxt('tp', 2)
        for h in range(4):
            C.tr(C.tp[j][:, h, :], nb16[i][:, h, :], C.ident[:], (('nb16', i), 'ident'), (('tp', j),))
        C.cp(qmT[:, :, m * 128:(m + 1) * 128], C.tp[j][:], (('tp', j), 'oTfree'), ('qmT',))
    pb = [C.sb(f"pbm{i}", [128, 512], BF16) for i in range(2)]
    for h in range(4):
        for tg in range(2):
            tsl = slice(tg * 512, (tg + 1) * 512)
            for nb_ in range(2):
                pg = C.nxt('psg', 2)
                C.mm(PSB[pg][:], kmT[:, h, nb_ * 128:(nb_ + 1) * 128], qmT[:, h, tsl], True, True, ('kmT', 'qmT'),
                     (('psb', pg),))
                pi = C.nxt('pbm', 2)
                C.act(pb[pi][:], PSB[pg][:], AF.Exp, (('psb', pg),), (('pbm', pi),), scale=SCALE)
                C.mm(PSB[2][:], vm[:, nb_, h * 128:(h + 1) * 128], pb[pi][:], nb_ == 0, nb_ == 1, ('vm', ('pbm', pi)),
                     (('psb', 2),))
                C.mm(PSB[3][:], ONESB[:], pb[pi][:], nb_ == 0, nb_ == 1, ('ONESB', ('pbm', pi)), (('psb', 3),))
            gi = C.nxt('gs', 2)
            C.recip(gs[gi][:], PSB[3][:], (('psb', 3),), (('gs', gi),))
            C.tt(omT[:, h, tsl], PSB[2][:], gs[gi][:], ALU.mult, (('psb', 2), ('gs', gi), 'oTfree'), ('omT',))
    for n in range(4):
        (wo,), wot = load_w(C, [wmo_d[:, :, n * 512:(n + 1) * 512]])
        for m in range(NB):
            pa = 4 + C.nxt('psa', 2)
            for h in range(4):
                C.mm(PSB[pa][:], omT[:, h, m * 128:(m + 1) * 128], wo[:, h, :], h == 0, h == 3, ('omT',) + wot,
                     (('psb', pa),))
            xs = C.X[:, m, n * 512:(n + 1) * 512]
            C.tt(xs, xs, PSB[pa][:], ALU.add, (('X', m), ('psb', pa)), (('X', m),))


    XG2 = C.XG.rearrange("b r c -> (b r) c")
    for mp in range(4):
        C.dma(C.XHL[mp], C.X[126:128, 2 * mp + 1, :], (('X', 2 * mp + 1),), (('XHL', mp),), 'xh')
        allgather(C, C.XHL[mp], XG2[(1 + 4 * mp) * 2:(1 + 4 * mp + 4) * 2, :], (('XHL', mp),), (('XG', mp),))


NCH = DFF // 128


def phase_ffn(C, l):
    wu_d = C.dram_in("w_up", [DEPTH, D, 2 * DFF], F32)[l].rearrange("(kc p) n -> p kc n", p=128)
    wd_d = C.dram_in("w_down", [DEPTH, DFF, D], F32)[l].rearrange("(kc p) n -> p kc n", p=128)
    ln_d = C.dram_in("ln_ffn", [DEPTH, 128, D], F32)[l]
    cw_d = C.dram_in("conv_wp", [DEPTH, 128, 2 * NCH, 4], F32)[l]

    lnw = C.sb("lnw", [128, D], F32)
    C.dma(lnw[:], ln_d, (), ('lnw',), 'c0')
    CW = C.sb("CW", [128, 2 * NCH, 4], F32)
    C.dma(CW[:], cw_d, (), ('CW',), 'c0')
    xh = C.wst[0]
    C.memset(xh[:], 0.0, (('wst', 0),))
    C.dma(C.XW.rearrange("b r c -> b (r c)"), C.XG[bass.ds(C.jv, 13)].rearrange("b r c -> b (r c)"),
          tuple(('XG', m) for m in range(4)) + ('XGpad',), ('XW',), 'kw')
    for mp in range(4):
        C.dma(xh[2 * mp:2 * mp + 2, :], C.XW[4 * mp], ('XW',), (('wst', 0),), 'ws0')
    hTh = C.sb("hTh", [128, 16, 128], BF16)
    make_hT(C, lnw[:], 'lnw')
    make_hT(C, lnw[:], 'lnw', src=lambda m: (xh[:], ('wst', 0)), nblk=1, dst=hTh, dst_tok='hTh')
    hT_all = tuple(('hT', m) for m in range(NB))

    PSB = [C.ps(f"psb{i}", [128, 512], F32) for i in range(6)]
    UH = C.tp[0][:].rearrange("p a b -> p (a b)").bitcast(F32)
    ub = [[C.sb(f"ub{a}{i}", [128, 4, 258], F32) for i in range(2)] for a in range(2)]
    Y = [[C.sb(f"Y{a}{i}", [128, 4, 256], F32) for i in range(2)] for a in range(2)]
    aT = [C.sb(f"aT{i}", [128, 8, TOK], BF16) for i in range(1)]

    ngrp = (NCH + 7) // 8
    for cg in range(ngrp):
        ai = 0
        chunks = list(range(cg * 8, min(NCH, cg * 8 + 8)))
        for ci, ch in enumerate(chunks):
            bi = C.nxt('ub', 2)
            for a in range(2):
                col = a * DFF + ch * 128
                (w,), wt = load_w(C, [wu_d[:, :, col:col + 128]])
                U = ub[a][bi]
                utok = ('ub', a, bi)
                for tg in range(2):
                    pg = a * 2 + tg
                    for kc in range(16):
                        C.mm(PSB[pg][:], w[:, kc, :], C.hT[:, kc, tg * 512:(tg + 1) * 512], kc == 0, kc == 15,
                             hT_all + wt, (('psb', pg),))
                    C.cp(U[:, 2 * tg:2 * tg + 2, 2:258], PSB[pg][:].rearrange("p (m t) -> p m t", t=256),
                         (('psb', pg),), (utok,), eng='scalar')
                for kc in range(16):
                    C.mm(UH[:, a * 8:a * 8 + 8], w[:, kc, :], hTh[:, kc, 0:8], kc == 0, kc == 15, ('hTh',) + wt,
                         (('uh', a),))
                C.cp(U[:, :, 0:2], UH[:, a * 8:a * 8 + 8].rearrange("p (m r) -> p m r", r=2), (('uh', a),), (utok,))
                cwc = CW[:, a * NCH + ch, :]
                Yt = Y[a][bi]
                ytok = ('Y', a, bi)
                C.act(Yt[:], U[:, :, 2:258], AF.Identity, (utok, 'CW'), (ytok,), scale=cwc[:, 2:3], bias=cwc[:, 3:4])
                C.stt(Yt[:], U[:, :, 1:257], cwc[:, 1:2], Yt[:], ALU.mult, ALU.add, (utok, 'CW', ytok), (ytok,))
                C.stt(Yt[:], U[:, :, 0:256], cwc[:, 0:1], Yt[:], ALU.mult, ALU.add, (utok, 'CW', ytok), (ytok,))
            C.act(Y[0][bi][:], Y[0][bi][:], AF.Silu, (('Y', 0, bi),), (('Y', 0, bi),))
            C.tt(aT[ai][:, ci, :].rearrange("p (m t) -> p m t", t=256), Y[0][bi][:], Y[1][bi][:], ALU.mult,
                 (('Y', 0, bi), ('Y', 1, bi)), (('aT', ai),), eng='gpsimd')
        ng = len(chunks)
        for n in range(4):
            ws, wt = [], ()
            for q4 in range((ng + 3) // 4):
                k0 = cg * 8 + q4 * 4
                k1 = min(cg * 8 + ng, k0 + 4)
                (w_,), t_ = load_w(C, [wd_d[:, k0:k1, n * 512:(n + 1) * 512]])
                ws.append(w_)
                wt = wt + t_
            for m in range(NB):
                pa = 4 + C.nxt('psd', 2)
                for i in range(ng):
                    C.mm(PSB[pa][:], aT[ai][:, i, m * 128:(m + 1) * 128], ws[i // 4][:, i % 4, :], i == 0, i == ng - 1,
                         (('aT', ai),) + wt, (('psb', pa),))
                xs = C.X[:, m, n * 512:(n + 1) * 512]
                C.tt(xs, xs, PSB[pa][:], ALU.add, (('X', m), ('psb', pa)), (('X', m),))


import os as _os
STOP = [_os.environ.get('STOPAT')]


def build_fused():
    C = Ctx()
    load_consts(C)
    load_x(C)
    C.CS = C.sb("CS", [128, 2, NB, 4, 16], F32)
    gathered_bufs(C)
    C.jv = C.nc.partition_id() % 4
    C.phase_begin()
    rope_tables(C)
    zero_pads(C)
    C.phase_end()
    for l in range(DEPTH):
        if STOP[0] == 'init':
            break
        C.phase_begin()
        alloc_norm(C)
        alloc_wstream(C, 3, 8)
        phase_qkv(C, l)
        C.phase_end()
        if STOP[0] == 'qkv':
            break
        C.phase_begin()
        phase_attn(C, l)
        C.phase_end()
        if STOP[0] == 'attn':
            break
        C.phase_begin()
        alloc_norm(C, 1)
        alloc_wstream(C, 2, 4)
        phase_post(C, l)
        C.phase_end()
        if STOP[0] == 'post':
            break
        C.phase_begin()
        alloc_norm(C, 1)
        alloc_wstream(C, 2, 4)
        phase_ffn(C, l)
        C.phase_end()
        if STOP[0] == 'ffn':
            break
    C.P.barrier()
    store_x(C)
    NAMES[:] = [k for k, v in C.dcache.items()]
    return C.finish()


NAMES = []


_PROG = []


def kernel(x, mem, positions, ln_mix, w_qkv, qk_gain, w_br_a, w_br_b, w_br_c, w_gate, b_gate, w_o,
           ln_mem_q, ln_mem_kv, wm_q, wm_kv, wm_o, mem_qk_gain, ln_ffn, w_up, conv_w, conv_b, w_down):
    f32 = lambda a: np.ascontiguousarray(np.asarray(a, dtype=np.float32))
    x = f32(x)
    mem = f32(mem)
    positions = np.asarray(positions).astype(np.int32)
    L = DEPTH
    bcl = lambda a: np.ascontiguousarray(np.broadcast_to(f32(a)[:, None], (L, 128) + tuple(np.asarray(a).shape[1:])))
    cst = consts()
    cwp = np.concatenate([f32(conv_w).reshape(L, 3, 2 * NCH, 128).transpose(0, 3, 2, 1),
                          f32(conv_b).reshape(L, 2 * NCH, 128).transpose(0, 2, 1)[..., None]], axis=3)
    com = dict(ident=cst['ident'], invf=cst['invf'],
               w_qkv=f32(w_qkv), w_gate=f32(w_gate), w_br_a=f32(w_br_a), w_br_b=f32(w_br_b), w_br_c=f32(w_br_c),
               w_o=f32(w_o), wm_q=f32(wm_q), wm_kv=f32(wm_kv), wm_o=f32(wm_o), w_up=f32(w_up), w_down=f32(w_down),
               ln_mix=bcl(ln_mix), ln_mem_q=bcl(ln_mem_q), ln_mem_kv=bcl(ln_mem_kv), ln_ffn=bcl(ln_ffn),
               qk_gain4=bcl(qk_gain), mem_gain=bcl(mem_qk_gain),
               b_gate_p=np.ascontiguousarray(f32(b_gate).reshape(L, 48, 128).transpose(0, 2, 1)),
               conv_wp=np.ascontiguousarray(cwp))
    acst = [attn_consts(j) for j in range(4)]
    cores = list(range(NCORE))
    toks = [own_tokens(c % 4) for c in cores]
    in_maps = []
    for c in cores:
        m = dict(com)
        m.update(acst[c % 4])
        m['x_in'] = np.ascontiguousarray(x[c // 4, toks[c]])
        m['pos'] = np.ascontiguousarray(positions[c // 4, toks[c]].reshape(8, 128).T)
        m['mem'] = mem[c // 4]
        in_maps.append(m)
    if not _PROG:
        _PROG.append(build_fused())
    in_maps = [{k: v for k, v in m.items() if k in NAMES} for m in in_maps]
    res = run(_PROG[0], in_maps)
    out = np.zeros(x.shape, np.float32)
    for c in cores:
        out[c // 4, toks[c]] = np.asarray(res[c]['x_out'])
    return out
```

```python
import contextlib
import numpy as np
import ml_dtypes
import concourse.bass as bass
import concourse.mybir as mybir
from concourse.bass_utils import run_bass_kernel_spmd

F32 = mybir.dt.float32
BF16 = mybir.dt.bfloat16
I32 = mybir.dt.int32
AF = mybir.ActivationFunctionType
ALU = mybir.AluOpType
AX = mybir.AxisListType
NPBF = ml_dtypes.bfloat16

D = 2048
NCORE = 8
TOK = 1024
NB = 8
EPS = 1e-6
DFF = 5504
SCALE = 128 ** -0.5
PI = 3.14159265358979
NEG = -30000.0
WCAP = 2048


class Prog:
    CE = ('scalar', 'tensor', 'vector', 'gpsimd', 'sync')

    def __init__(self, nc):
        self.nc = nc
        self.q = {e: [] for e in self.CE}
        self.lastw = {}
        self.rd = {}
        self.dcnt = {}
        self.known = {e: {} for e in self.CE}
        self.pending = {e: [] for e in self.CE}
        self.emitted = {e: 0 for e in self.CE}
        self.cum = {e: [] for e in self.CE}
        self.esem = None
        self.dsem = {}

    def barrier(self):
        evs = []
        for e in self.CE:
            if self.q[e] and e != 'sync':
                evs.append((('e', e, len(self.q[e]) - 1), True))
        for k, n in self.dcnt.items():
            evs.append((('d', k, n), True))
        for e in self.CE:
            self.pending[e] = [ev for ev in evs if not (ev[0][0] == 'e' and ev[0][1] == e)]

    def op(self, eng, fn, r=(), w=(), dma=None, extra=()):
        q = self.q[eng]
        idx = len(q)
        cand = list(extra) + self.pending[eng]
        self.pending[eng] = []
        for t in r:
            ev = self.lastw.get(t)
            if ev is not None:
                cand.append((ev, True))
        for t in w:
            ev = self.lastw.get(t)
            if ev is not None:
                cand.append((ev, False))
            for ev in self.rd.get(t, ()):
                cand.append((ev, False))
        best = {}
        for ev, raw in cand:
            if ev[0] == 'e':
                if ev[1] == eng and (eng in ('tensor', 'sync') or not raw):
                    continue
                k = ('e', ev[1])
                val = ev[2]
            else:
                k = ('d', ev[1])
                val = self.dcnt[ev[1]]
            if val > best.get(k, -1):
                best[k] = val
        fw = []
        for k, val in best.items():
            if self.known[eng].get(k, -1) >= val:
                continue
            self.known[eng][k] = val
            fw.append((k[0], k[1], val))
        rec = dict(fn=fn, waits=fw, inc=False, dma=dma)
        q.append(rec)
        if dma is not None:
            n = self.dcnt.get(dma, 0) + 1
            self.dcnt[dma] = n
            me = ('d', dma, n)
        else:
            me = ('e', eng, idx)
        for n_, ev in enumerate(fw):
            if ev[0] == 'e':
                pe, pidx = ev[1], ev[2]
                if pidx < self.emitted[pe]:
                    while not self.q[pe][pidx]['inc']:
                        pidx += 1
                    fw[n_] = ('e', pe, pidx)
                else:
                    self.q[pe][pidx]['inc'] = True
        for t in w:
            self.lastw[t] = me
            self.rd[t] = []
        for t in r:
            self.rd.setdefault(t, []).append(me)
        return me

    def emit(self, st=None):
        nc = self.nc
        if self.esem is None:
            self.esem = {e: nc.alloc_semaphore('s_' + e) for e in self.CE}
        for k in self.dcnt:
            if k not in self.dsem:
                self.dsem[k] = nc.alloc_semaphore('d_' + k)
        esem, dsem, cum = self.esem, self.dsem, self.cum
        for e in self.CE:
            q = self.q[e]
            if len(q) > self.emitted[e]:
                q[-1]['inc'] = True
            c = cum[e][-1] if cum[e] else 0
            for rec in q[len(cum[e]):]:
                if rec['inc']:
                    c += 1
                cum[e].append(c)
        with nc.Block() as block:
            def run(e):
                start = self.emitted[e]

                def f(eng):
                    for rec in self.q[e][start:]:
                        rec['wv'] = []
                        for ev in rec['waits']:
                            if ev[0] == 'e':
                                eng.wait_ge(esem[ev[1]], cum[ev[1]][ev[2]])
                                rec['wv'].append((('e', ev[1]), cum[ev[1]][ev[2]]))
                            else:
                                eng.wait_ge(dsem[ev[1]], 16 * ev[2])
                                rec['wv'].append((('d', ev[1]), ev[2]))
                        ins = rec['fn'](eng)
                        if rec['dma'] is not None:
                            ins.then_inc(dsem[rec['dma']], 16)
                        elif rec['inc']:
                            ins.then_inc(esem[e], 1)
                return f

            block.sync(run('sync'))
            block.scalar(run('scalar'))
            block.tensor(run('tensor'))
            block.vector(run('vector'))
            block.gpsimd(run('gpsimd'))
        for e in self.CE:
            self.emitted[e] = len(self.q[e])


class Ctx:
    def __init__(self):
        self.nc = bass.Bass("TRN2", target_bir_lowering=False)
        self.P = Prog(self.nc)
        self.st = contextlib.ExitStack()
        self.rot = {}
        self.outs = []
        self.dq = 0
        self.pst = None
        self.phase_id = 0
        self.dcache = {}

    def phase_begin(self):
        self.P.barrier()
        self.pst = contextlib.ExitStack()
        self.phase_id += 1

    def phase_end(self):
        self.P.emit()
        self.pst.close()
        self.pst = None

    def dram_in(self, name, shape, dt):
        if name not in self.dcache:
            self.dcache[name] = self.nc.dram_tensor(name, list(shape), dt, kind="ExternalInput").ap()
        return self.dcache[name]

    def dram_tmp(self, name, shape, dt):
        if name not in self.dcache:
            self.dcache[name] = self.nc.dram_tensor(name, list(shape), dt).ap()
        return self.dcache[name]

    def dram_out(self, name, shape, dt):
        return self.nc.dram_tensor(name, list(shape), dt, kind="ExternalOutput").ap()

    def sb(self, name, shape, dt):
        st = self.st if self.pst is None else self.pst
        return st.enter_context(self.nc.sbuf_tensor(f"{name}_p{self.phase_id}", list(shape), dt))

    def ps(self, name, shape, dt=F32):
        st = self.st if self.pst is None else self.pst
        return st.enter_context(self.nc.psum_tensor(f"{name}_p{self.phase_id}", list(shape), dt))

    def nxt(self, name, n):
        i = self.rot.get(name, 0)
        self.rot[name] = i + 1
        return i % n

    def dma(self, out, in_, r, w, key, eng=None):
        if eng is None:
            eng = ('sync', 'gpsimd')[self.dq % 2] if False else 'sync'
        def fn(e):
            try:
                return e.dma_start(out=out, in_=in_)
            except Exception:
                print("DMA FAILED", out, in_)
                raise
        return self.P.op(eng, fn, r, w, dma=key)

    def act(self, out, in_, func, r, w, **kw):
        return self.P.op('scalar', lambda e: e.activation(out=out, in_=in_, func=func, **kw), r, w)

    def mm(self, out, lhsT, rhs, start, stop, r, w):
        return self.P.op('tensor', lambda e: e.matmul(out, lhsT=lhsT, rhs=rhs, start=start, stop=stop), r, w)

    def tr(self, out, in_, ident, r, w):
        return self.P.op('tensor', lambda e: e.transpose(out, in_, ident), r, w)

    def tt(self, out, in0, in1, op, r, w, eng='vector'):
        return self.P.op(eng, lambda e: e.tensor_tensor(out=out, in0=in0, in1=in1, op=op), r, w)

    def ts(self, out, in0, s1, s2, op0, op1, r, w, eng='vector'):
        if op1 is None:
            return self.P.op(eng, lambda e: e.tensor_scalar(out=out, in0=in0, scalar1=s1, scalar2=None, op0=op0), r, w)
        return self.P.op(eng, lambda e: e.tensor_scalar(out=out, in0=in0, scalar1=s1, scalar2=s2, op0=op0, op1=op1), r, w)

    def stt(self, out, in0, scalar, in1, op0, op1, r, w):
        return self.P.op('vector', lambda e: e.scalar_tensor_tensor(out=out, in0=in0, scalar=scalar, in1=in1,
                                                                    op0=op0, op1=op1), r, w)

    def cp(self, out, in_, r, w, eng='vector'):
        if eng == 'scalar':
            return self.P.op(eng, lambda e: e.copy(out=out, in_=in_), r, w)
        return self.P.op(eng, lambda e: e.tensor_copy(out=out, in_=in_), r, w)

    def recip(self, out, in_, r, w):
        return self.P.op('vector', lambda e: e.reciprocal(out=out, in_=in_), r, w)

    def memset(self, ap, val, w, eng='gpsimd'):
        return self.P.op(eng, lambda e: e.memset(ap, val), (), w)

    def finish(self):
        self.P.op('sync', lambda e: e.nop(), r=tuple(self.outs), w=())
        self.P.emit(self.st)
        self.st.close()
        return self.nc


def load_consts(C, need_ident=True):
    nc = C.nc
    ident_d = C.dram_in("ident", [128, 128], BF16)
    C.ident = C.sb("ident_sb", [128, 128], BF16)
    C.dma(C.ident[:], ident_d, (), ('ident',), 'c0')


def load_x(C, name="x_in"):
    xd = C.dram_in(name, [TOK, D], F32)
    C.X = C.sb("X", [128, NB, D], F32)
    for m in range(NB):
        C.dma(C.X[:, m, :], xd[m * 128:(m + 1) * 128, :], (), (('X', m),), 'xl')


def store_x(C, name="x_out"):
    xo = C.dram_out(name, [TOK, D], F32)
    for m in range(NB):
        C.dma(xo[m * 128:(m + 1) * 128, :], C.X[:, m, :], (('X', m),), (('xo', m),), 'xs')
        C.outs.append(('xo', m))


def alloc_norm(C, nhb=2):
    C.nhb = nhb
    C.hT = C.sb("hT", [128, 16, TOK], BF16)
    C.hb = [C.sb(f"hb{i}", [128, D], BF16) for i in range(nhb)]
    C.nst = [C.sb(f"nst{i}", [128, 4], F32) for i in range(2)]
    C.tp = [C.ps(f"tp{i}", [128, 4, 128], BF16) for i in range(2)]


def rstd_from_ss(C, st, n, inv_n, tok):
    C.ts(st[:, 0:n], st[:, 0:n], inv_n, EPS, ALU.mult, ALU.add, (tok,), (tok,))
    C.act(st[:, 0:n], st[:, 0:n], AF.Sqrt, (tok,), (tok,))
    C.recip(st[:, 0:n], st[:, 0:n], (tok,), (tok,))


def make_hT(C, gain_tile, gain_tok, src=None, nblk=NB, dst=None, dst_tok='hT'):
    dst = C.hT if dst is None else dst
    for m in range(nblk):
        xin, xtok = (C.X[:, m, :], ('X', m)) if src is None else src(m)
        i = C.nxt('hb', C.nhb)
        st = C.nst[i]
        C.act(C.hb[i][:], xin, AF.Square, (xtok,), (('hb', i), ('nst', i)), accum_out=st[:, 0:1])
        rstd_from_ss(C, st, 1, 1.0 / D, ('nst', i))
        C.stt(C.hb[i][:], xin, st[:, 0:1], gain_tile, ALU.mult, ALU.mult, (xtok, ('nst', i), gain_tok), (('hb', i),))
        for g in range(4):
            j = C.nxt('tp', 2)
            for a in range(4):
                kc = g * 4 + a
                C.tr(C.tp[j][:, a, :], C.hb[i][:, kc * 128:(kc + 1) * 128], C.ident[:], (('hb', i), 'ident'),
                     (('tp', j),))
            eng = 'scalar' if g % 2 == 0 else 'vector'
            C.cp(dst[:, g * 4:(g + 1) * 4, m * 128:(m + 1) * 128], C.tp[j][:], (('tp', j),), ((dst_tok, m),), eng=eng)


def alloc_wstream(C, nst=3, nbf=6):
    C.nwst, C.nwbf = nst, nbf
    C.wst = [C.sb(f"wst{i}", [128, WCAP], F32) for i in range(nst)]
    C.wbf = [C.sb(f"wbf{i}", [128, WCAP], BF16) for i in range(nbf)]


def load_w(C, views):
    s = C.nxt('wst', C.nwst)
    b = C.nxt('wbf', C.nwbf)
    off = 0
    outs = []
    for v in views:
        k, n = v.shape[1], v.shape[2]
        sz = k * n
        dst = C.wst[s][:, off:off + sz].rearrange("p (k n) -> p k n", n=n)
        C.dma(dst, v, (), (('wst', s),), f'ws{s}')
        outs.append(C.wbf[b][:, off:off + sz].rearrange("p (k n) -> p k n", n=n))
        off += sz
    assert off <= WCAP
    h = (off // 4 + 127) // 128 * 128
    C.cp(C.wbf[b][:, 0:h], C.wst[s][:, 0:h], (('wst', s),), (('wbf', b, 0),), eng='gpsimd')
    C.cp(C.wbf[b][:, h:off], C.wst[s][:, h:off], (('wst', s),), (('wbf', b, 1),), eng='vector')
    return outs, (('wbf', b, 0), ('wbf', b, 1))


def rope_tables(C):
    pos_d = C.dram_in("pos", [128, NB], I32)
    invf_d = C.dram_in("invf", [128, 16], F32)
    posi = C.sb("posi", [128, NB], I32)
    posf = C.sb("posf", [128, NB], F32)
    invf = C.sb("invf_sb", [128, 16], F32)
    AC = C.sb("AC", [128, 2, NB * 16], F32)
    KF = C.sb("KF", [128, 2, NB * 16], F32)
    KI = C.sb("KI", [128, 2, NB * 16], I32)
    MK = C.sb("MK", [128, 2, NB * 16], F32)
    C.dma(posi[:], pos_d, (), ('posi',), 'c0')
    C.dma(invf[:], invf_d, (), ('invf',), 'c0')
    C.cp(posf[:], posi[:], ('posi',), ('posf',))
    for m in range(NB):
        C.ts(AC[:, 0, m * 16:(m + 1) * 16], invf[:], posf[:, m:m + 1], None, ALU.mult, None, ('posf', 'invf'), ('AC',))
    C.ts(AC[:, 1, :], AC[:, 0, :], PI / 2, None, ALU.add, None, ('AC',), ('AC',))
    C.ts(KF[:], AC[:], 1.0 / (2 * PI), None, ALU.mult, None, ('AC',), ('KF',))
    C.cp(KI[:], KF[:], ('KF',), ('KI',))
    C.cp(KF[:], KI[:], ('KI',), ('KF',))
    C.stt(AC[:], KF[:], -2 * PI, AC[:], ALU.mult, ALU.add, ('KF', 'AC'), ('AC',))
    C.ts(MK[:], AC[:], PI, None, ALU.is_gt, None, ('AC',), ('MK',))
    C.stt(AC[:], MK[:], -2 * PI, AC[:], ALU.mult, ALU.add, ('MK', 'AC'), ('AC',))
    C.ts(MK[:], AC[:], -PI, None, ALU.is_lt, None, ('AC',), ('MK',))
    C.stt(AC[:], MK[:], 2 * PI, AC[:], ALU.mult, ALU.add, ('MK', 'AC'), ('AC',))
    C.ts(AC[:], AC[:], PI, -PI, ALU.min, ALU.max, ('AC',), ('AC',))
    C.act(KF[:], AC[:], AF.Sin, ('AC',), ('KF',))
    for h in range(4):
        C.cp(C.CS[:, :, :, h, :], KF[:].rearrange("p a (m f) -> p a m f", f=16), ('KF',), ('CS',))


def gathered_bufs(C):
    C.KL = C.dram_tmp("KL", [4, 16, 128, 256], BF16)
    C.VL = C.dram_tmp("VL", [TOK, D], BF16)
    C.QL = C.dram_tmp("QL", [16, 128, TOK], BF16)
    C.KG = C.dram_tmp("KG", [NPADB + 16, 16, 128, 256], BF16)
    C.VG = C.dram_tmp("VG", [NPADB + 16, 256, D], BF16)
    C.OL = C.dram_tmp("OL", [12, 128, TOK], BF16)
    C.KW = C.dram_tmp("KW", [16, 16, 128, 256], BF16)
    C.VW = C.dram_tmp("VW", [16, 256, D], BF16)
    C.XW = C.dram_tmp("XW", [13, 2, D], F32)
    C.XHL = C.dram_tmp("XHL", [4, 2, D], F32)
    C.XG = C.dram_tmp("XG", [1 + 16, 2, D], F32)


def zero_pads(C):
    z = C.sb("zpad", [128, 4096], BF16)
    C.memset(z[:], 0.0, ('zpad',))
    for b in range(NPADB):
        C.dma(C.KG[b].rearrange("h d t -> d h t"), z[:].rearrange("p (h t) -> p h t", t=256), ('zpad',), ('KGpad',), 'c0')
        for hf in range(2):
            C.dma(C.VG[b, hf * 128:(hf + 1) * 128, :], z[:, 0:D], ('zpad',), ('VGpad',), 'c0')
    C.dma(C.XG[0], z[0:2, 0:2 * D].bitcast(F32), ('zpad',), ('XGpad',), 'c0')


RG = [[0, 1, 2, 3], [4, 5, 6, 7]]


def allgather(C, src2d, dst2d, r, w):
    C.P.op('gpsimd', lambda e: e.collective_compute("AllGather", ALU.bypass, replica_groups=RG, ins=[src2d],
                                                    outs=[dst2d]), r, w)
    C.P.q['gpsimd'][-1]['inc'] = True


def phase_qkv(C, l):
    nc = C.nc
    wq_d = C.dram_in("w_qkv", [DEPTH, D, 3 * D], F32)[l].rearrange("(kc p) n -> p kc n", p=128)
    ln_d = C.dram_in("ln_mix", [DEPTH, 128, D], F32)[l]
    g_d = C.dram_in("qk_gain4", [DEPTH, 128, 4, 128], F32)[l]
    KLv = C.KL.rearrange("m h d t -> h d m t")
    qT_o = C.QL
    v_o = C.VL

    lnw = C.sb("lnw", [128, D], F32)
    C.dma(lnw[:], ln_d, (), ('lnw',), 'c0')
    G6 = C.sb("G6", [128, 4, 128], F32)
    C.dma(G6[:], g_d, (), ('G6',), 'c0')
    make_hT(C, lnw[:], 'lnw')
    import os
    ccm = os.environ.get("CCMODE", "kv")
    if 'onlyht' in ccm:
        return

    acc = [C.ps(f"acc{i}", [128, 512], F32) for i in range(4)]
    sq = C.sb("sq", [128, 512], F32)
    qn = [C.sb(f"qn{i}", [128, 4, 128], F32) for i in range(2)]
    qb = [C.sb(f"qb{i}", [128, 4, 128], BF16) for i in range(3)]
    rt = [C.sb(f"rt{i}", [128, 4, 4, 16], F32) for i in range(2)]
    ss = [C.sb(f"ss{i}", [128, 4], F32) for i in range(2)]
    qTs = [C.sb(f"qTs{i}", [128, 4, TOK], BF16) for i in range(2)]
    vs = [t[:].rearrange("p h (m c) -> p (h m) c", c=512) for t in qTs]

    def load_chunk(c):
        wbs, wtok = [], ()
        for q4 in range(4):
            (wb_,), wt_ = load_w(C, [wq_d[:, q4 * 4:(q4 + 1) * 4, c * 512:(c + 1) * 512]])
            wbs.append(wb_)
            wtok = wtok + wt_
        return wbs, wtok

    KG2 = C.KG.rearrange("b h d t -> (b h d) t")
    VG2 = C.VG.rearrange("b t c -> (b t) c")
    order = [4, 5, 6, 7, 8, 9, 10, 11, 0, 1, 2, 3]
    nxt_w = load_chunk(order[0])
    for ci, c in enumerate(order):
        wbs, wtok = nxt_w
        if ci + 1 < 12:
            nxt_w = load_chunk(order[ci + 1])
        kind = c // 4
        cc = c % 4
        so = C.nxt('qTs', 2)
        deferred = []
        for m in range(NB):
            a = C.nxt('acc', 4)
            for kc in range(16):
                C.mm(acc[a][:], C.hT[:, kc, m * 128:(m + 1) * 128], wbs[kc // 4][:, kc % 4, :], kc == 0, kc == 15,
                     (('hT', m),) + wtok, (('acc', a),))
            if kind == 2:
                C.cp(vs[so][:, m, :], acc[a][:], (('acc', a),), (('qTs', so, m),), eng='scalar')
                continue
            i = C.nxt('qn', 2)
            ib = C.nxt('qbr', 3)
            av = acc[a][:].rearrange("p (h d) -> p h d", d=128)
            if cc < 3:
                C.act(sq[:], acc[a][:], AF.Square, (('acc', a),), ('sq',))
                C.P.op('vector', lambda e, o=ss[i][:], s=sq[:].rearrange("p (h d) -> p h d", d=128):
                       e.tensor_reduce(out=o, in_=s, axis=AX.X, op=ALU.add), ('sq',), (('ss', i),))
                rstd_from_ss(C, ss[i], 4, 1.0 / 128, ('ss', i))
                for h in range(4):
                    C.stt(qn[i][:, h, :], av[:, h, :], ss[i][:, h:h + 1],
                          G6[:, (kind if cc * 4 + h < 6 else 2 + kind), :], ALU.mult, ALU.mult,
                          (('acc', a), ('ss', i), 'G6'), (('qn', i),))
                x1 = qn[i][:, :, 0:16]
                x2 = qn[i][:, :, 16:32]
                sn = C.CS[:, 0, m]
                cs = C.CS[:, 1, m]
                R = rt[i]
                C.tt(R[:, 0], x1, cs, ALU.mult, (('qn', i), 'CS'), (('rt', i),))
                C.tt(R[:, 1], x2, sn, ALU.mult, (('qn', i), 'CS'), (('rt', i),))
                C.tt(R[:, 2], x2, cs, ALU.mult, (('qn', i), 'CS'), (('rt', i),))
                C.tt(R[:, 3], x1, sn, ALU.mult, (('qn', i), 'CS'), (('rt', i),))
                C.tt(qb[ib][:, :, 0:16], R[:, 0], R[:, 1], ALU.subtract, (('rt', i),), (('qb', ib),))
                C.tt(qb[ib][:, :, 16:32], R[:, 2], R[:, 3], ALU.add, (('rt', i),), (('qb', ib),))
                C.cp(qb[ib][:, :, 32:128], qn[i][:, :, 32:128], (('qn', i),), (('qb', ib),), eng='scalar')
            else:
                C.cp(qb[ib][:], av, (('acc', a),), (('qb', ib),), eng='scalar')
            def do_tr(ib=ib, m=m, so=so):
                j = C.nxt('tp', 2)
                for h in range(4):
                    C.tr(C.tp[j][:, h, :], qb[ib][:, h, :], C.ident[:], (('qb', ib), 'ident'), (('tp', j),))
                C.cp(qTs[so][:, :, m * 128:(m + 1) * 128], C.tp[j][:], (('tp', j),), (('qTs', so, m),), eng='vector')
            if deferred:
                deferred.pop()()
            deferred.append(do_tr)
        if deferred:
            deferred.pop()()
        if kind == 2:
            for m in range(NB):
                C.dma(v_o[m * 128:(m + 1) * 128, cc * 512:(cc + 1) * 512], vs[so][:, m, :], (('qTs', so, m),),
                      (('VL', m, cc),), f'os{so}')
        elif kind == 0:
            C.dma(qT_o[cc * 4:(cc + 1) * 4].rearrange("h d t -> d h t"), qTs[so][:],
                  tuple(('qTs', so, m) for m in range(NB)), (('QL', cc),), f'os{so}')
        else:
            for mq in range(4):
                C.dma(C.KL[mq, cc * 4:(cc + 1) * 4].rearrange("h d t -> d h t"), qTs[so][:, :, mq * 256:(mq + 1) * 256],
                      (('qTs', so, 2 * mq), ('qTs', so, 2 * mq + 1)), (('KL', cc, mq),), f'os{so}')
        if c == 7:
            for m in range(4):
                allgather(C, C.KL[m].rearrange("h d t -> (h d) t"),
                          KG2[(NPADB + 4 * m) * 2048:(NPADB + 4 * m + 4) * 2048, :],
                          tuple(('KL', cc_, m) for cc_ in range(4)), (('KG', m),))
        if c == 11:
            for m in range(4):
                allgather(C, C.VL[m * 256:(m + 1) * 256, :], VG2[(NPADB + 4 * m) * 256:(NPADB + 4 * m + 4) * 256, :],
                          tuple(('VL', mm_, cc_) for mm_ in (2 * m, 2 * m + 1) for cc_ in range(4)), (('VG', m),))


def own_tokens(j):
    return np.concatenate([np.arange((4 * m + j) * 256, (4 * m + j + 1) * 256) for m in range(4)])


def bc128(v):
    v = np.asarray(v, np.float32).reshape(1, -1)
    return np.ascontiguousarray(np.broadcast_to(v, (128, v.shape[1])))


def consts():
    ident = np.eye(128, dtype=np.float32).astype(NPBF)
    invf = (500000.0 ** (-np.arange(0, 32, 2, dtype=np.float32) / 32)).astype(np.float32)
    return dict(ident=ident, invf=bc128(invf))


def gain4(qk_gain_l):
    return np.ascontiguousarray(np.broadcast_to(np.asarray(qk_gain_l, np.float32)[None], (128, 4, 128)))


def gain6(qk_gain_l):
    out = np.zeros((6, 512), np.float32)
    for kind in range(2):
        for cc in range(3):
            for h in range(4):
                head = cc * 4 + h
                row = kind if head < 6 else 2 + kind
                out[kind * 3 + cc, h * 128:(h + 1) * 128] = qk_gain_l[row]
    return np.ascontiguousarray(np.broadcast_to(out[None], (128, 6, 512)))


def run(nc, in_maps):
    res = run_bass_kernel_spmd(nc, in_maps, core_ids=list(range(NCORE)))
    return res.results


NKB = 32
NKB2 = 16
NPADB = 3
DEPTH = 2
DIL = ((128, 1), (512, 4), (2048, 16))


def amask_index(g, dl):
    nd = DIL[g][0] // 128
    if g == 0:
        return dl
    base = 2 + 3 * (g - 1)
    return base + (0 if dl == 0 else (2 if dl == nd else 1))


def phase_attn(C, l):
    nc = C.nc
    am_d = C.dram_in("amask", [128, 8, 128], BF16)
    tri_d = C.dram_in("tri", [128, 2, 128], BF16)
    neg_d = C.dram_in("neglt", [128, 128], F32)
    padv_d = C.dram_in("padv", [128, 6, 128], BF16)
    gm_d = C.dram_in("bsel", [128, 3, 4, NKB2], F32)
    e19_d = C.dram_in("e19", [NKB2, NKB2 * 128], BF16)
    u_d = C.dram_in("uones", [128, 2, 128], F32)
    idf_d = C.dram_in("identf", [128, 128], F32)
    oT_o = C.OL
    jv = C.jv
    kg_tok = ('KW',)
    vg_tok = ('VW',)
    f2 = lambda ap, pat: ap.rearrange(pat).rearrange("(r c) -> r c", c=16384)
    C.dma(f2(C.KW, "b h d t -> (b h d t)"), f2(C.KG[bass.ds(jv, 16)], "b h d t -> (b h d t)"),
          tuple(('KG', m) for m in range(4)) + ('KGpad',), ('KW',), 'kw')
    C.dma(f2(C.VW, "b t c -> (b t c)"), f2(C.VG[bass.ds(jv, 16)], "b t c -> (b t c)"),
          tuple(('VG', m) for m in range(4)) + ('VGpad',), ('VW',), 'vw', eng='scalar')
    ql_tok = tuple(('QL', cc) for cc in range(4))

    def cload(name, d, shape, dt):
        t = C.sb(name, shape, dt)
        C.dma(t[:], d, (), (name,), 'c0')
        return t
    AM = cload("AM", am_d, [128, 8, 128], BF16)
    TRI = cload("TRI", tri_d, [128, 2, 128], BF16)
    NEGLT = cload("NEGLT", neg_d, [128, 128], F32)
    PADV = cload("PADV", padv_d, [128, 6, 128], BF16)
    BS = cload("BS", gm_d, [128, 3, 4, NKB2], F32)
    E19 = cload("E19", e19_d, [NKB2, NKB2 * 128], BF16)
    UO = cload("UO", u_d, [128, 2, 128], F32)
    IDF = cload("IDF", idf_d, [128, 128], F32)
    ONESB = C.sb("ONESB", [128, 128], BF16)
    C.memset(ONESB[:], 1.0, ('ONESB',))
    ctoks = ('AM', 'TRI', 'NEGLT', 'PADV', 'BS', 'E19', 'UO', 'IDF', 'ONESB')

    kTb = [C.sb(f"kTb{i}", [128, NKB * 128], BF16) for i in range(2)]
    vb = [C.sb(f"vb{i}", [128, NKB, 128], BF16) for i in range(2)]
    qb = [C.sb(f"qTb{i}", [128, TOK], BF16) for i in range(2)]
    ost = [C.sb(f"ost{i}", [128, TOK], BF16) for i in range(2)]
    PB = [C.ps(f"pb{i}", [128, 512], F32) for i in range(8)]
    Z = [PB[i][:, 0:128] for i in range(3)]
    ACCO = [PB[3][:, 0:128], PB[4][:, 0:128]]
    AFT = [PB[5][:, 0:128], PB[6][:, 0:128]]
    ACCS = AFT
    GATE = [PB[7][:, 0:NKB2]] * 2
    SBT = [PB[7][0:NKB2, 128:256]] * 2
    NT = 4
    tmp = {nm: [C.sb(f"{nm}{i}", [128, 128], F32) for i in range(NT)] for nm in ('et', 'spt', 'lmt', 't1', 't2')}
    at = [C.sb(f"at{i}", [128, 128], BF16) for i in range(NT)]
    Rt = [C.sb(f"Rt{i}", [128, 128], F32) for i in range(2)]
    AO = C.sb("AO", [128, NB, 128], F32)
    AS = C.sb("AS", [128, NB, 128], F32)
    rs = [C.sb(f"rs{i}", [128, 128], F32) for i in range(2)]
    kmf = C.sb("kmf", [128, NKB2], F32)
    kmb = C.sb("kmb", [128, NKB2], BF16)
    kml = C.sb("kml", [128, NKB2], BF16)
    gsb = [C.sb(f"gsb{i}", [128, NKB2], F32) for i in range(2)]
    top8 = [C.sb(f"top8{i}", [128, 8], F32) for i in range(2)]
    sbTb = [C.sb(f"sbTb{i}", [NKB2, 128], BF16) for i in range(NB)]

    def run_pipeline(jobs, stages):
        n, S = len(jobs), len(stages)
        for t in range(n + S - 1):
            for st_ in range(S - 1, -1, -1):
                k = t - st_
                if 0 <= k < n:
                    stages[st_](jobs[k], k)

    head_order = [12, 13, 14, 15] + [2 * g + sl for sl in range(2) for g in range(3)] + [6, 7, 8, 9, 10, 11]
    loaded = {}

    def load_head(h):
        if h not in loaded:
            loaded[h] = _load_head(h)
        k = head_order.index(h)
        if k + 1 < len(head_order) and head_order[k + 1] not in loaded:
            loaded[head_order[k + 1]] = _load_head(head_order[k + 1])
        return loaded[h]

    def _load_head(h):
        i = C.nxt('kv', 2)
        kdst = kTb[i][:].rearrange("d (n t) -> d n t", t=256)
        for q4 in range(4):
            C.dma(kdst[:, q4 * 4:(q4 + 1) * 4, :],
                  C.KW[q4 * 4:(q4 + 1) * 4, h].rearrange("n d t -> d n t"), kg_tok, (('kT', i),), f'kv{i}')
            C.dma(vb[i][:, q4 * 8:(q4 + 1) * 8, :],
                  C.VW[q4 * 4:(q4 + 1) * 4, :, h * 128:(h + 1) * 128].rearrange("n (hf s) d -> s (n hf) d", hf=2),
                  vg_tok, (('v', i),), f'kv{i}')
        C.dma(qb[i][:], C.QL[h], ql_tok, (('q', i),), f'kv{i}')
        return i

    def store_head(slot, oi):
        C.dma(oT_o[slot], ost[oi][:], (('ost', oi),), (('OL', slot),), f'oo{oi}')

    def kq(lm):
        return 2 * (4 * (lm // 2) + 3) + (lm % 2)

    for hc in range(4):
        hi = load_head(12 + hc)
        oi = C.nxt('ost', 2)
        kT, v, q = kTb[hi], vb[hi], qb[hi]
        hk = (('kT', hi), ('q', hi))
        jobs = [dict(lm=lm, kb=kb, first=(kb == kq(lm)), last=(kb == 0)) for lm in range(NB)
                for kb in range(kq(lm), -1, -1)]

        def c_s0(j, k):
            zi, ti = k % 3, k % NT
            C.mm(Z[zi], kT[:, j['kb'] * 128:(j['kb'] + 1) * 128], q[:, j['lm'] * 128:(j['lm'] + 1) * 128], True, True,
                 hk, (('z', zi),))
            C.act(tmp['et'][ti][:], Z[zi], AF.Exp, (('z', zi),), (('et', ti),), scale=SCALE)

        def c_s0b(j, k):
            zi, ti = k % 3, k % NT
            C.act(tmp['spt'][ti][:], tmp['et'][ti][:], AF.Ln, (('et', ti),), (('spt', ti),), bias=1.0, scale=1.0)
            if j['first']:
                C.tt(tmp['lmt'][ti][:], tmp['spt'][ti][:], TRI[:, 1, :], ALU.mult, (('spt', ti), 'TRI'),
                     (('lmt', ti),), eng='gpsimd')
            C.stt(tmp['t1'][ti][:], Z[zi], SCALE, tmp['spt'][ti][:], ALU.mult, ALU.subtract, (('z', zi), ('spt', ti)),
                  (('t1', ti),))

        def c_s1(j, k):
            zi, ti, ai = k % 3, k % NT, k % 2
            first, last = j['first'], j['last']
            L, Ltok = (tmp['lmt'][ti], ('lmt', ti)) if first else (tmp['spt'][ti], ('spt', ti))
            rp, rn = (k + 1) % 2, k % 2
            C.mm(AFT[ai], UO[:, 0, :], L[:], True, first, (Ltok, 'UO'), (('ps56', ai),))
            if not first:
                C.mm(AFT[ai], UO[:, 1, :], Rt[rp][:], False, True, (('Rt', rp), 'UO'), (('ps56', ai),))
            if not last:
                if first:
                    C.cp(Rt[rn][:], L[:], (Ltok,), (('Rt', rn),), eng='gpsimd')
                else:
                    C.tt(Rt[rn][:], Rt[rp][:], L[:], ALU.add, (('Rt', rp), Ltok), (('Rt', rn),), eng='gpsimd')
            C.tt(tmp['t2'][ti][:], tmp['t1'][ti][:], AFT[ai], ALU.subtract, (('t1', ti), ('ps56', ai)), (('t2', ti),))
            if first:
                C.tt(tmp['t2'][ti][:], tmp['t2'][ti][:], NEGLT[:], ALU.add, (('t2', ti), 'NEGLT'), (('t2', ti),))
            C.act(at[ti][:], tmp['t2'][ti][:], AF.Exp, (('t2', ti),), (('at', ti),))

        def c_s2(j, k, oi=oi, v=v, hi=hi):
            ti, ao = k % NT, j['lm'] % 2
            C.mm(ACCO[ao], v[:, j['kb'], :], at[ti][:], j['first'], j['last'], (('v', hi), ('at', ti)), (('acco', ao),))
            if j['last']:
                C.cp(ost[oi][:, j['lm'] * 128:(j['lm'] + 1) * 128], ACCO[ao], (('acco', ao),), (('ost', oi),),
                     eng='scalar')

        run_pipeline(jobs, [c_s0, c_s0b, c_s1, c_s2])
        store_head(8 + hc, oi)

    for slot in range(2):
        oi = C.nxt('ost', 2)
        for g in range(3):
            hi = load_head(2 * g + slot)
            kT, v, q = kTb[hi], vb[hi], qb[hi]
            hk = (('kT', hi), ('q', hi))
            nd = DIL[g][0] // 128
            jobs = []
            for lm in range(NB):
                kbs = [kq(lm) - dl for dl in range(nd + 1) if kq(lm) - dl >= 0]
                for kb in kbs:
                    jobs.append(dict(lm=lm, kb=kb, first=(kb == kbs[0]), last=(kb == kbs[-1]), dl=kq(lm) - kb))

            def a_s0(j, k, g=g):
                zi, ti = k % 3, k % NT
                C.mm(Z[zi], kT[:, j['kb'] * 128:(j['kb'] + 1) * 128], q[:, j['lm'] * 128:(j['lm'] + 1) * 128], True,
                     True, hk, (('z', zi),))
                C.act(tmp['et'][ti][:], Z[zi], AF.Exp, (('z', zi),), (('et', ti),), scale=SCALE)
                C.tt(at[ti][:], tmp['et'][ti][:], AM[:, amask_index(g, j['dl']), :], ALU.mult, (('et', ti), 'AM'),
                     (('at', ti),), eng='gpsimd' if (k % 2) else 'vector')

            def a_s1(j, k, g=g):
                ti, ao = k % NT, j['lm'] % 2
                kb, lm = j['kb'], j['lm']
                C.mm(ACCO[ao], v[:, kb, :], at[ti][:], j['first'], j['last'], (('v', hi), ('at', ti)), (('acco', ao),))
                C.mm(ACCS[ao], PADV[:, kb, :] if kb < 6 else ONESB[:], at[ti][:], j['first'], j['last'],
                     (('at', ti), 'PADV', 'ONESB'), (('ps56', ao),))
                if j['last']:
                    if g == 0:
                        C.cp(AO[:, lm, :], ACCO[ao], (('acco', ao),), (('AO', lm),), eng='scalar')
                        C.cp(AS[:, lm, :], ACCS[ao], (('ps56', ao),), (('AS', lm),), eng='scalar')
                    else:
                        C.tt(AO[:, lm, :], AO[:, lm, :], ACCO[ao], ALU.add, (('AO', lm), ('acco', ao)), (('AO', lm),))
                        C.tt(AS[:, lm, :], AS[:, lm, :], ACCS[ao], ALU.add, (('AS', lm), ('ps56', ao)), (('AS', lm),))

            run_pipeline(jobs, [a_s0, a_s1])
        for lm in range(NB):
            C.recip(AS[:, lm, :], AS[:, lm, :], (('AS', lm),), (('AS', lm),))
            C.tt(ost[oi][:, lm * 128:(lm + 1) * 128], AO[:, lm, :], AS[:, lm, :], ALU.mult, (('AO', lm), ('AS', lm)),
                 (('ost', oi),))
        store_head(slot, oi)

    for hb_ in range(6):
        hi = load_head(6 + hb_)
        oi = C.nxt('ost', 2)
        kT, v, q = kTb[hi], vb[hi], qb[hi]
        hk = (('kT', hi), ('q', hi))
        C.P.op('vector', lambda e, o=kmf[:], s=kT[:].rearrange("p (n k) -> p n k", k=256):
               e.tensor_reduce(out=o, in_=s, axis=AX.X, op=ALU.add), (('kT', hi),), ('kmf',))
        C.ts(kmf[:], kmf[:], 1.0 / 256, None, ALU.mult, None, ('kmf',), ('kmf',))
        C.cp(kmb[:], kmf[:], ('kmf',), ('kmb',))
        C.tt(kml[:], kmf[:], kmb[:], ALU.subtract, ('kmf', 'kmb'), ('kml',))
        for lm in range(NB):
            mp = lm // 2
            qs = q[:, lm * 128:(lm + 1) * 128]
            gi = lm % 2
            G = gsb[gi]
            C.mm(GATE[gi], qs, kmb[:], True, False, (('q', hi), 'kmb'), ('pb7',))
            C.mm(GATE[gi], qs, kml[:], False, True, (('q', hi), 'kml'), ('pb7',))
            C.tt(G[:], GATE[gi], BS[:, 0, mp, :], ALU.add, ('pb7', 'BS'), (('gsb', gi),))
            C.P.op('vector', lambda e, o=top8[gi][:], s=G[:]: e.max(out=o, in_=s), (('gsb', gi),), (('top8', gi),))
            C.ts(G[:], G[:], top8[gi][:, 2:3], None, ALU.is_ge, None, (('gsb', gi), ('top8', gi)), (('gsb', gi),))
            C.tt(G[:], G[:], BS[:, 1, mp, :], ALU.mult, (('gsb', gi), 'BS'), (('gsb', gi),))
            C.tt(G[:], G[:], BS[:, 2, mp, :], ALU.add, (('gsb', gi), 'BS'), (('gsb', gi),))
            C.tr(SBT[gi], G[:], IDF[:], (('gsb', gi), 'IDF'), ('pb7',))
            C.cp(sbTb[lm][:], SBT[gi], ('pb7',), (('sbTb', lm),), eng='scalar')
        jobs = [dict(lm=lm, kb=kb, first=(kb == 0), last=(kb == kq(lm))) for lm in range(NB)
                for kb in range(0, kq(lm) + 1)]

        def b_s0(j, k):
            zi, ti = k % 3, k % NT
            kb, lm = j['kb'], j['lm']
            n2 = kb // 2
            C.mm(Z[zi], kT[:, kb * 128:(kb + 1) * 128], q[:, lm * 128:(lm + 1) * 128], True, False, hk, (('z', zi),))
            C.mm(Z[zi], E19[:, n2 * 128:(n2 + 1) * 128], sbTb[lm][:], False, True, (('sbTb', lm), 'E19'), (('z', zi),))
            if j['last']:
                C.act(tmp['et'][ti][:], Z[zi], AF.Exp, (('z', zi),), (('et', ti),), scale=SCALE)
                C.tt(at[ti][:], tmp['et'][ti][:], TRI[:, 0, :], ALU.mult, (('et', ti), 'TRI'), (('at', ti),))
            else:
                C.act(at[ti][:], Z[zi], AF.Exp, (('z', zi),), (('at', ti),), scale=SCALE)

        def b_s1(j, k, oi=oi):
            ti, ao = k % NT, j['lm'] % 2
            kb, lm = j['kb'], j['lm']
            C.mm(ACCO[ao], v[:, kb, :], at[ti][:], j['first'], j['last'], (('v', hi), ('at', ti)), (('acco', ao),))
            C.mm(ACCS[ao], ONESB[:], at[ti][:], j['first'], j['last'], (('at', ti), 'ONESB'), (('ps56', ao),))
            if j['last']:
                C.recip(rs[ao][:], ACCS[ao], (('ps56', ao),), (('rs', ao),))
                C.tt(ost[oi][:, lm * 128:(lm + 1) * 128], ACCO[ao], rs[ao][:], ALU.mult, (('acco', ao), ('rs', ao)),
                     (('ost', oi),))

        run_pipeline(jobs, [b_s0, b_s1])
        store_head(2 + hb_, oi)


def attn_consts(j):
    s = np.arange(128)[:, None]
    t = np.arange(128)[None, :]
    am = np.zeros((8, 128, 128), np.float32)
    for g, (w, dil) in enumerate(DIL):
        nd = w // 128
        for dl in range(nd + 1):
            dist = t - s + 128 * dl
            am[amask_index(g, dl)] = ((dist >= 0) & (dist <= w) & (dist % dil == 0))
    tri = np.stack([(s <= t), (s < t)]).astype(np.float32)
    neglt = np.where(s < t, 0.0, NEG).astype(np.float32)
    npad = 2 * (3 - j)
    padv = np.zeros((6, 128, 128), np.float32)
    padv[npad:] = 1.0
    bsel = np.zeros((3, 4, NKB2), np.float32)
    for mp in range(4):
        own = 4 * mp + 3
        for n in range(NKB2):
            valid_past = (n >= 3 - j) and (n < own)
            bsel[0, mp, n] = 0.0 if valid_past else -1e30
            bsel[1, mp, n] = -NEG if valid_past else 0.0
            bsel[2, mp, n] = 0.0 if n == own else NEG
    e19 = np.zeros((NKB2, NKB2, 128), np.float32)
    for n in range(NKB2):
        e19[n, n, :] = 1.0
    uo = np.stack([(s > t).astype(np.float32), np.ones((128, 128), np.float32)])
    return dict(amask=np.ascontiguousarray(am.transpose(1, 0, 2)).astype(NPBF),
                tri=np.ascontiguousarray(tri.transpose(1, 0, 2)).astype(NPBF),
                neglt=neglt,
                padv=np.ascontiguousarray(padv.transpose(1, 0, 2)).astype(NPBF),
                bsel=np.ascontiguousarray(np.broadcast_to(bsel[None], (128, 3, 4, NKB2))),
                e19=e19.reshape(NKB2, NKB2 * 128).astype(NPBF),
                uones=np.ascontiguousarray(uo.transpose(1, 0, 2)),
                identf=np.eye(128, dtype=np.float32))


def head_norm(C, accap, acctok, gain_ap, gain_tok, out_bf, out_tok, sq, ss, sstok):
    av = accap.rearrange("p (h d) -> p h d", d=128)
    C.act(sq[:], accap, AF.Square, (acctok,), ('sq',))
    C.P.op('vector', lambda e, o=ss[:], s_=sq[:].rearrange("p (h d) -> p h d", d=128):
           e.tensor_reduce(out=o, in_=s_, axis=AX.X, op=ALU.add), ('sq',), (sstok,))
    rstd_from_ss(C, ss, 4, 1.0 / 128, sstok)
    for h in range(4):
        C.stt(out_bf[:, h, :], av[:, h, :], ss[:, h:h + 1], gain_ap, ALU.mult, ALU.mult,
              (acctok, sstok, gain_tok), (out_tok,))


def phase_post(C, l):
    r3 = lambda ap: ap.rearrange("(kc p) n -> p kc n", p=128)
    wg_d = r3(C.dram_in("w_gate", [DEPTH, D, 3 * D], F32)[l])
    wbr_d = [r3(C.dram_in("w_br_a", [DEPTH, 256, D], F32)[l]), r3(C.dram_in("w_br_b", [DEPTH, 768, D], F32)[l]),
             r3(C.dram_in("w_br_c", [DEPTH, 512, D], F32)[l])]
    wo_d = r3(C.dram_in("w_o", [DEPTH, D, D], F32)[l])
    bg_d = C.dram_in("b_gate_p", [DEPTH, 128, 48], F32)[l]
    ln_d = C.dram_in("ln_mix", [DEPTH, 128, D], F32)[l]
    lnq_d = C.dram_in("ln_mem_q", [DEPTH, 128, D], F32)[l]
    lnkv_d = C.dram_in("ln_mem_kv", [DEPTH, 128, D], F32)[l]
    wmq_d = r3(C.dram_in("wm_q", [DEPTH, D, 512], F32)[l])
    wmkv_d = r3(C.dram_in("wm_kv", [DEPTH, D, 1024], F32)[l])
    wmo_d = r3(C.dram_in("wm_o", [DEPTH, 512, D], F32)[l])
    mg_d = C.dram_in("mem_gain", [DEPTH, 128, 2, 128], F32)[l]
    mem_d = C.dram_in("mem", [256, D], F32)
    oT_d = C.OL

    lnw = C.sb("lnw", [128, D], F32)
    C.dma(lnw[:], ln_d, (), ('lnw',), 'c0')
    BG = C.sb("BG", [128, 48], F32)
    C.dma(BG[:], bg_d, (), ('BG',), 'c0')
    MG = C.sb("MG", [128, 2, 128], F32)
    C.dma(MG[:], mg_d, (), ('MG',), 'c0')
    oT = C.sb("oTs", [128, 12, TOK], BF16)
    C.dma(oT[:], oT_d.rearrange("s d t -> d s t"), tuple(('OL', sl) for sl in range(12)), ('oT',), 'c1')
    ONESB = C.sb("ONESB", [128, 128], BF16)
    C.memset(ONESB[:], 1.0, ('ONESB',))

    make_hT(C, lnw[:], 'lnw')
    hT_all = tuple(('hT', m) for m in range(NB))

    PSB = [C.ps(f"psb{i}", [128, 512], F32) for i in range(6)]
    mT = [C.sb(f"mT{i}", [128, 4, TOK], BF16) for i in range(1)]
    gs = [C.sb(f"gs{i}", [128, 512], F32) for i in range(2)]
    tq = [C.sb(f"tq{i}", [128, 512], F32) for i in range(2)]
    macc = [C.sb(f"macc{i}", [128, 512], F32) for i in range(2)]
    brslot = (0, 2, 8)
    brk = (2, 6, 4)

    for grp in range(4):
        mi = 0
        for ncl in range(4):
            ncg = grp * 4 + ncl
            for br in range(3):
                (wg,), wgt = load_w(C, [wg_d[:, :, br * D + ncg * 128: br * D + (ncg + 1) * 128]])
                (wb,), wbt = load_w(C, [wbr_d[br][:, :, ncg * 128:(ncg + 1) * 128]])
                for tg in range(2):
                    pg = C.nxt('psg', 2)
                    pp = 2 + C.nxt('psp', 2)
                    tsl = slice(tg * 512, (tg + 1) * 512)
                    for kc in range(16):
                        C.mm(PSB[pg][:], wg[:, kc, :], C.hT[:, kc, tsl], kc == 0, kc == 15, hT_all + wgt,
                             (('psb', pg),))
                    for kc in range(brk[br]):
                        C.mm(PSB[pp][:], wb[:, kc, :], oT[:, brslot[br] + kc, tsl], kc == 0, kc == brk[br] - 1,
                             ('oT',) + wbt, (('psb', pp),))
                    gi = C.nxt('gs', 2)
                    C.act(gs[gi][:], PSB[pg][:], AF.Sigmoid, (('psb', pg), 'BG'), (('gs', gi),),
                          bias=BG[:, br * 16 + ncg: br * 16 + ncg + 1], scale=1.0)
                    if br == 0:
                        C.tt(macc[tg][:], gs[gi][:], PSB[pp][:], ALU.mult, (('gs', gi), ('psb', pp)), (('macc', tg),))
                    else:
                        C.tt(tq[gi][:], gs[gi][:], PSB[pp][:], ALU.mult, (('gs', gi), ('psb', pp)), (('tq', gi),))
                        if br == 1:
                            C.tt(macc[tg][:], macc[tg][:], tq[gi][:], ALU.add, (('macc', tg), ('tq', gi)),
                                 (('macc', tg),), eng='gpsimd')
                        else:
                            C.tt(mT[mi][:, ncl, tsl], macc[tg][:], tq[gi][:], ALU.add, (('macc', tg), ('tq', gi)),
                                 (('mT', mi),), eng='gpsimd')
        for n in range(4):
            (wo,), wot = load_w(C, [wo_d[:, grp * 4:(grp + 1) * 4, n * 512:(n + 1) * 512]])
            for m in range(NB):
                pa = 4 + C.nxt('psa', 2)
                for kc in range(4):
                    C.mm(PSB[pa][:], mT[mi][:, kc, m * 128:(m + 1) * 128], wo[:, kc, :], kc == 0, kc == 3,
                         (('mT', mi),) + wot, (('psb', pa),))
                xs = C.X[:, m, n * 512:(n + 1) * 512]
                C.tt(xs, xs, PSB[pa][:], ALU.add, (('X', m), ('psb', pa)), (('X', m),))

    lnq = lnw
    C.dma(lnq[:], lnq_d, (), ('lnw',), 'c0')
    make_hT(C, lnq[:], 'lnw')
    lnkv = lnw
    C.dma(lnkv[:], lnkv_d, (), ('lnw',), 'c0')
    C.P.op('vector', lambda e: e.memset(ONESB[:, 0:1], 1.0), ('ONESB',), ('oT', 'oTfree'))
    oTf = oT[:].rearrange("p s t -> p (s t)")
    memT = oT[:].rearrange("p s t -> p (s t)")[:, 0:4096].rearrange("p (k t) -> p k t", t=256)
    memx = mT[0][:].rearrange("p a t -> p (a t)").bitcast(F32)
    for mb in range(2):
        C.dma(memx, mem_d[mb * 128:(mb + 1) * 128, :], (), (('mT', 0),), 'c1')
        i = C.nxt('hb', C.nhb)
        st = C.nst[i]
        C.act(C.hb[i][:], memx, AF.Square, (('mT', 0),), (('hb', i), ('nst', i)), accum_out=st[:, 0:1])
        rstd_from_ss(C, st, 1, 1.0 / D, ('nst', i))
        C.stt(C.hb[i][:], memx, st[:, 0:1], lnkv[:], ALU.mult, ALU.mult, (('mT', 0), ('nst', i), 'lnw'), (('hb', i),))
        for g in range(4):
            j = C.nxt('tp', 2)
            for a in range(4):
                kc = g * 4 + a
                C.tr(C.tp[j][:, a, :], C.hb[i][:, kc * 128:(kc + 1) * 128], C.ident[:], (('hb', i), 'ident'),
                     (('tp', j),))
            C.cp(memT[:, g * 4:(g + 1) * 4, mb * 128:(mb + 1) * 128], C.tp[j][:], (('tp', j), 'oTfree'), ('memT',),
                 eng='scalar' if g % 2 == 0 else 'vector')
    memT = oTf[:, 0:4096].rearrange("p (k t) -> p k t", t=256)
    sq = C.sb("sq", [128, 512], F32)
    ss = [C.sb(f"ss{i}", [128, 4], F32) for i in range(2)]
    nb16 = [C.sb(f"nb16_{i}", [128, 4, 128], BF16) for i in range(2)]
    kmT = C.sb("kmT", [128, 4, 256], BF16)
    vm = C.sb("vm", [128, 2, 512], BF16)
    qmT = oTf[:, 4096:8192].rearrange("p (h t) -> p h t", t=TOK)
    omT = oTf[:, 8192:12288].rearrange("p (h t) -> p h t", t=TOK)
    for c in range(2):
        ws, wt = [], ()
        for q4 in range(4):
            (w_,), t_ = load_w(C, [wmkv_d[:, q4 * 4:(q4 + 1) * 4, c * 512:(c + 1) * 512]])
            ws.append(w_)
            wt = wt + t_
        for mb in range(2):
            pa = 4 + C.nxt('psa', 2)
            for kc in range(16):
                C.mm(PSB[pa][:], memT[:, kc, mb * 128:(mb + 1) * 128], ws[kc // 4][:, kc % 4, :], kc == 0, kc == 15,
                     ('memT',) + wt, (('psb', pa),))
            if c == 1:
                C.cp(vm[:, mb, :], PSB[pa][:], (('psb', pa),), ('vm',), eng='scalar')
            else:
                i = C.nxt('nb16', 2)
                head_norm(C, PSB[pa][:], ('psb', pa), MG[:, 1, :], 'MG', nb16[i], ('nb16', i), sq, ss[i], ('ss', i))
                j = C.nxt('tp', 2)
                for h in range(4):
                    C.tr(C.tp[j][:, h, :], nb16[i][:, h, :], C.ident[:], (('nb16', i), 'ident'), (('tp', j),))
                C.cp(kmT[:, :, mb * 128:(mb + 1) * 128], C.tp[j][:], (('tp', j),), ('kmT',))
    ws, wt = [], ()
    for q4 in range(4):
        (w_,), t_ = load_w(C, [wmq_d[:, q4 * 4:(q4 + 1) * 4, :]])
        ws.append(w_)
        wt = wt + t_
    for m in range(NB):
        pa = 4 + C.nxt('psa', 2)
        for kc in range(16):
            C.mm(PSB[pa][:], C.hT[:, kc, m * 128:(m + 1) * 128], ws[kc // 4][:, kc % 4, :], kc == 0, kc == 15,
                 (('hT', m),) + wt, (('psb', pa),))
        i = C.nxt('nb16', 2)
        head_norm(C, PSB[pa][:], ('psb', pa), MG[:, 0, :], 'MG', nb16[i], ('nb16', i), sq, ss[i], ('ss', i))
        j = C.nxt('tp', 2)
        for h in range(4):
            C.tr(C.tp[j][:, h, :], nb16[i][:, h, :], C.ident[:], (('nb16', i), 'ident'), (('tp', j),))
        C.cp(qmT[:, :, m * 128:(m + 1) * 128], C.tp[j][:], (('tp', j), 'oTfree'), ('qmT',))
    pb = [C.sb(f"pbm{i}", [128, 512], BF16) for i in range(2)]
    for h in range(4):
        for tg in range(2):
            tsl = slice(tg * 512, (tg + 1) * 512)
            for nb_ in range(2):
                pg = C.nxt('psg', 2)
                C.mm(PSB[pg][:], kmT[:, h, nb_ * 128:(nb_ + 1) * 128], qmT[:, h, tsl], True, True, ('kmT', 'qmT'),
                     (('psb', pg),))
                pi = C.nxt('pbm', 2)
                C.act(pb[pi][:], PSB[pg][:], AF.Exp, (('psb', pg),), (('pbm', pi),), scale=SCALE)
                C.mm(PSB[2][:], vm[:, nb_, h * 128:(h + 1) * 128], pb[pi][:], nb_ == 0, nb_ == 1, ('vm', ('pbm', pi)),
                     (('psb', 2),))
                C.mm(PSB[3][:], ONESB[:], pb[pi][:], nb_ == 0, nb_ == 1, ('ONESB', ('pbm', pi)), (('psb', 3),))
            gi = C.nxt('gs', 2)
            C.recip(gs[gi][:], PSB[3][:], (('psb', 3),), (('gs', gi),))
            C.tt(omT[:, h, tsl], PSB[2][:], gs[gi][:], ALU.mult, (('psb', 2), ('gs', gi), 'oTfree'), ('omT',))
    for n in range(4):
        (wo,), wot = load_w(C, [wmo_d[:, :, n * 512:(n + 1) * 512]])
        for m in range(NB):
            pa = 4 + C.nxt('psa', 2)
            for h in range(4):
                C.mm(PSB[pa][:], omT[:, h, m * 128:(m + 1) * 128], wo[:, h, :], h == 0, h == 3, ('omT',) + wot,
                     (('psb', pa),))
            xs = C.X[:, m, n * 512:(n + 1) * 512]
            C.tt(xs, xs, PSB[pa][:], ALU.add, (('X', m), ('psb', pa)), (('X', m),))


    XG2 = C.XG.rearrange("b r c -> (b r) c")
    for mp in range(4):
        C.dma(C.XHL[mp], C.X[126:128, 2 * mp + 1, :], (('X', 2 * mp + 1),), (('XHL', mp),), 'xh')
        allgather(C, C.XHL[mp], XG2[(1 + 4 * mp) * 2:(1 + 4 * mp + 4) * 2, :], (('XHL', mp),), (('XG', mp),))


NCH = DFF // 128


def phase_ffn(C, l):
    wu_d = C.dram_in("w_up", [DEPTH, D, 2 * DFF], F32)[l].rearrange("(kc p) n -> p kc n", p=128)
    wd_d = C.dram_in("w_down", [DEPTH, DFF, D], F32)[l].rearrange("(kc p) n -> p kc n", p=128)
    ln_d = C.dram_in("ln_ffn", [DEPTH, 128, D], F32)[l]
    cw_d = C.dram_in("conv_wp", [DEPTH, 128, 2 * NCH, 4], F32)[l]

    lnw = C.sb("lnw", [128, D], F32)
    C.dma(lnw[:], ln_d, (), ('lnw',), 'c0')
    CW = C.sb("CW", [128, 2 * NCH, 4], F32)
    C.dma(CW[:], cw_d, (), ('CW',), 'c0')
    xh = C.wst[0]
    C.memset(xh[:], 0.0, (('wst', 0),))
    C.dma(C.XW.rearrange("b r c -> b (r c)"), C.XG[bass.ds(C.jv, 13)].rearrange("b r c -> b (r c)"),
          tuple(('XG', m) for m in range(4)) + ('XGpad',), ('XW',), 'kw')
    for mp in range(4):
        C.dma(xh[2 * mp:2 * mp + 2, :], C.XW[4 * mp], ('XW',), (('wst', 0),), 'ws0')
    hTh = C.sb("hTh", [128, 16, 128], BF16)
    make_hT(C, lnw[:], 'lnw')
    make_hT(C, lnw[:], 'lnw', src=lambda m: (xh[:], ('wst', 0)), nblk=1, dst=hTh, dst_tok='hTh')
    hT_all = tuple(('hT', m) for m in range(NB))

    PSB = [C.ps(f"psb{i}", [128, 512], F32) for i in range(6)]
    UH = C.tp[0][:].rearrange("p a b -> p (a b)").bitcast(F32)
    ub = [[C.sb(f"ub{a}{i}", [128, 4, 258], F32) for i in range(2)] for a in range(2)]
    Y = [[C.sb(f"Y{a}{i}", [128, 4, 256], F32) for i in range(2)] for a in range(2)]
    aT = [C.sb(f"aT{i}", [128, 8, TOK], BF16) for i in range(1)]

    ngrp = (NCH + 7) // 8
    for cg in range(ngrp):
        ai = 0
        chunks = list(range(cg * 8, min(NCH, cg * 8 + 8)))
        for ci, ch in enumerate(chunks):
            bi = C.nxt('ub', 2)
            for a in range(2):
                col = a * DFF + ch * 128
                (w,), wt = load_w(C, [wu_d[:, :, col:col + 128]])
                U = ub[a][bi]
                utok = ('ub', a, bi)
                for tg in range(2):
                    pg = a * 2 + tg
                    for kc in range(16):
                        C.mm(PSB[pg][:], w[:, kc, :], C.hT[:, kc, tg * 512:(tg + 1) * 512], kc == 0, kc == 15,
                             hT_all + wt, (('psb', pg),))
                    C.cp(U[:, 2 * tg:2 * tg + 2, 2:258], PSB[pg][:].rearrange("p (m t) -> p m t", t=256),
                         (('psb', pg),), (utok,), eng='scalar')
                for kc in range(16):
                    C.mm(UH[:, a * 8:a * 8 + 8], w[:, kc, :], hTh[:, kc, 0:8], kc == 0, kc == 15, ('hTh',) + wt,
                         (('uh', a),))
                C.cp(U[:, :, 0:2], UH[:, a * 8:a * 8 + 8].rearrange("p (m r) -> p m r", r=2), (('uh', a),), (utok,))
                cwc = CW[:, a * NCH + ch, :]
                Yt = Y[a][bi]
                ytok = ('Y', a, bi)
                C.act(Yt[:], U[:, :, 2:258], AF.Identity, (utok, 'CW'), (ytok,), scale=cwc[:, 2:3], bias=cwc[:, 3:4])
                C.stt(Yt[:], U[:, :, 1:257], cwc[:, 1:2], Yt[:], ALU.mult, ALU.add, (utok, 'CW', ytok), (ytok,))
                C.stt(Yt[:], U[:, :, 0:256], cwc[:, 0:1], Yt[:], ALU.mult, ALU.add, (utok, 'CW', ytok), (ytok,))
            C.act(Y[0][bi][:], Y[0][bi][:], AF.Silu, (('Y', 0, bi),), (('Y', 0, bi),))
            C.tt(aT[ai][:, ci, :].rearrange("p (m t) -> p m t", t=256), Y[0][bi][:], Y[1][bi][:], ALU.mult,
                 (('Y', 0, bi), ('Y', 1, bi)), (('aT', ai),), eng='gpsimd')
        ng = len(chunks)
        for n in range(4):
            ws, wt = [], ()
            for q4 in range((ng + 3) // 4):
                k0 = cg * 8 + q4 * 4
                k1 = min(cg * 8 + ng, k0 + 4)
                (w_,), t_ = load_w(C, [wd_d[:, k0:k1, n * 512:(n + 1) * 512]])
                ws.append(w_)
                wt = wt + t_
            for m in range(NB):
                pa = 4 + C.nxt('psd', 2)
                for i in range(ng):
                    C.mm(PSB[pa][:], aT[ai][:, i, m * 128:(m + 1) * 128], ws[i // 4][:, i % 4, :], i == 0, i == ng - 1,
                         (('aT', ai),) + wt, (('psb', pa),))
                xs = C.X[:, m, n * 512:(n + 1) * 512]
                C.tt(xs, xs, PSB[pa][:], ALU.add, (('X', m), ('psb', pa)), (('X', m),))


import os as _os
STOP = [_os.environ.get('STOPAT')]


def build_fused():
    C = Ctx()
    load_consts(C)
    load_x(C)
    C.CS = C.sb("CS", [128, 2, NB, 4, 16], F32)
    gathered_bufs(C)
    C.jv = C.nc.partition_id() % 4
    C.phase_begin()
    rope_tables(C)
    zero_pads(C)
    C.phase_end()
    for l in range(DEPTH):
        if STOP[0] == 'init':
            break
        C.phase_begin()
        alloc_norm(C)
        alloc_wstream(C, 3, 8)
        phase_qkv(C, l)
        C.phase_end()
        if STOP[0] == 'qkv':
            break
        C.phase_begin()
        phase_attn(C, l)
        C.phase_end()
        if STOP[0] == 'attn':
            break
        C.phase_begin()
        alloc_norm(C, 1)
        alloc_wstream(C, 2, 4)
        phase_post(C, l)
        C.phase_end()
        if STOP[0] == 'post':
            break
        C.phase_begin()
        alloc_norm(C, 1)
        alloc_wstream(C, 2, 4)
        phase_ffn(C, l)
        C.phase_end()
        if STOP[0] == 'ffn':
            break
    C.P.barrier()
    store_x(C)
    NAMES[:] = [k for k, v in C.dcache.items()]
    return C.finish()


NAMES = []


_PROG = []


def kernel(x, mem, positions, ln_mix, w_qkv, qk_gain, w_br_a, w_br_b, w_br_c, w_gate, b_gate, w_o,
           ln_mem_q, ln_mem_kv, wm_q, wm_kv, wm_o, mem_qk_gain, ln_ffn, w_up, conv_w, conv_b, w_down):
    f32 = lambda a: np.ascontiguousarray(np.asarray(a, dtype=np.float32))
    x = f32(x)
    mem = f32(mem)
    positions = np.asarray(positions).astype(np.int32)
    L = DEPTH
    bcl = lambda a: np.ascontiguousarray(np.broadcast_to(f32(a)[:, None], (L, 128) + tuple(np.asarray(a).shape[1:])))
    cst = consts()
    cwp = np.concatenate([f32(conv_w).reshape(L, 3, 2 * NCH, 128).transpose(0, 3, 2, 1),
                          f32(conv_b).reshape(L, 2 * NCH, 128).transpose(0, 2, 1)[..., None]], axis=3)
    com = dict(ident=cst['ident'], invf=cst['invf'],
               w_qkv=f32(w_qkv), w_gate=f32(w_gate), w_br_a=f32(w_br_a), w_br_b=f32(w_br_b), w_br_c=f32(w_br_c),
               w_o=f32(w_o), wm_q=f32(wm_q), wm_kv=f32(wm_kv), wm_o=f32(wm_o), w_up=f32(w_up), w_down=f32(w_down),
               ln_mix=bcl(ln_mix), ln_mem_q=bcl(ln_mem_q), ln_mem_kv=bcl(ln_mem_kv), ln_ffn=bcl(ln_ffn),
               qk_gain4=bcl(qk_gain), mem_gain=bcl(mem_qk_gain),
               b_gate_p=np.ascontiguousarray(f32(b_gate).reshape(L, 48, 128).transpose(0, 2, 1)),
               conv_wp=np.ascontiguousarray(cwp))
    acst = [attn_consts(j) for j in range(4)]
    cores = list(range(NCORE))
    toks = [own_tokens(c % 4) for c in cores]
    in_maps = []
    for c in cores:
        m = dict(com)
        m.update(acst[c % 4])
        m['x_in'] = np.ascontiguousarray(x[c // 4, toks[c]])
        m['pos'] = np.ascontiguousarray(positions[c // 4, toks[c]].reshape(8, 128).T)
        m['mem'] = mem[c // 4]
        in_maps.append(m)
    if not _PROG:
        _PROG.append(build_fused())
    in_maps = [{k: v for k, v in m.items() if k in NAMES} for m in in_maps]
    res = run(_PROG[0], in_maps)
    out = np.zeros(x.shape, np.float32)
    for c in cores:
        out[c // 4, toks[c]] = np.asarray(res[c]['x_out'])
    return out
```
